# Optimizing a Trainium2 kernel written in Bass

```python
import math
import jax, jax.numpy as jnp
from jax import lax
import numpy as np

D_MODEL = 1024
BATCH = 16
SEQ = 2048
DEPTH = 2
DEC_BATCH = 8
DEC_SEQ = 8192
PAST_LEN = 128

HEAD_DIM = 64
DIL_GROUPS = ((128, 1), (512, 4), (2048, 16))
HEADS_PER_GROUP = 4
N_HEADS_A = HEADS_PER_GROUP * len(DIL_GROUPS)
N_HEADS_B = 8
N_HEADS = N_HEADS_A + N_HEADS_B
WIDTH_A = HEADS_PER_GROUP * HEAD_DIM
WIDTH_B = N_HEADS_B * HEAD_DIM
QKV_WIDTH = 3 * N_HEADS * HEAD_DIM
BLK_A = 64
N_BUCKETS = 32
MAX_DISTANCE = 1024
GRID_W = 64
KH_MAX = 8
KW = 16
QRB = 2
QCB = 16
KCB = 32
D_FF = 4 * D_MODEL
N_MOD = 6
EPS = 1e-6
NEG = -1e30

kernel_name = 'dilated_natten_gated_hybrid_encoder'


def rms_norm(x, g):
    xf = x.astype(jnp.float32)
    y = xf * lax.rsqrt(jnp.mean(xf * xf, axis=-1, keepdims=True) + EPS)
    return (y * g.astype(jnp.float32)).astype(x.dtype)


def t5_bucket(rel):
    nb = N_BUCKETS // 2
    max_exact = nb // 2
    ret = (rel > 0).astype(np.int32) * nb
    n = np.abs(rel)
    large = max_exact + (np.log(np.maximum(n, 1) / max_exact) / np.log(MAX_DISTANCE / max_exact) * (nb - max_exact)).astype(np.int32)
    large = np.minimum(large, nb - 1)
    return (ret + np.where(n < max_exact, n, large)).astype(np.int32)


def dilated_group(q, k, v, table, d, half):
    B_, L, H, E = q.shape
    n = L // d
    nb = -(-n // BLK_A)
    n_pad = nb * BLK_A

    def sub(t):
        return t.reshape(B_, n, d, H, E).transpose(0, 2, 1, 3, 4)

    qs = jnp.pad(sub(q), ((0, 0), (0, 0), (0, n_pad - n), (0, 0), (0, 0))).reshape(B_, d, nb, BLK_A, H, E)

    def windows(t):
        tp = jnp.pad(sub(t), ((0, 0), (0, 0), (BLK_A, n_pad - n + BLK_A), (0, 0), (0, 0)))
        tp = tp.reshape(B_, d, nb + 2, BLK_A, H, E)
        return jnp.concatenate([tp[:, :, :-2], tp[:, :, 1:-1], tp[:, :, 2:]], axis=3)

    kw, vw = windows(k), windows(v)
    j = np.arange(3 * BLK_A)[None, :] - BLK_A - np.arange(BLK_A)[:, None]
    bias = jnp.transpose(table[t5_bucket(d * j)], (2, 0, 1)).astype(jnp.float32)
    key_idx = np.arange(nb)[:, None] * BLK_A + np.arange(3 * BLK_A)[None, :] - BLK_A
    mask = (np.abs(j) <= half)[None] & ((key_idx >= 0) & (key_idx < n))[:, None, :]
    s = jnp.einsum('bdiqhe,bdikhe->bdihqk', qs, kw, preferred_element_type=jnp.float32) * (HEAD_DIM ** -0.5) + bias
    s = jnp.where(mask[:, None], s, NEG)
    lse = jax.nn.logsumexp(s, axis=-1)
    p = jnp.exp(s - lse[..., None]).astype(v.dtype)
    o = jnp.einsum('bdihqk,bdikhe->bdiqhe', p, vw).reshape(B_, d, n_pad, H, E)[:, :, :n]
    o = o.transpose(0, 2, 1, 3, 4).reshape(B_, L, H, E)
    lse = lse.transpose(0, 1, 2, 4, 3).reshape(B_, d, n_pad, H)[:, :, :n]
    lse = lse.transpose(0, 2, 1, 3).reshape(B_, L, H)
    return o, lse


def dilated_mixer(q, k, v, table):
    B_, L, _, E = q.shape
    outs, lses = [], []
    for g, (w, d) in enumerate(DIL_GROUPS):
        sl = slice(g * HEADS_PER_GROUP, (g + 1) * HEADS_PER_GROUP)
        o, l = dilated_group(q[:, :, sl], k[:, :, sl], v[:, :, sl], table[:, sl], d, w // (2 * d))
        outs.append(o)
        lses.append(l)
    wts = jax.nn.softmax(jnp.stack(lses, axis=0), axis=0)
    o = jnp.sum(wts[..., None].astype(q.dtype) * jnp.stack(outs, axis=0), axis=0)
    return o.reshape(B_, L, WIDTH_A)


def neighborhood_mixer(q, k, v, rpb):
    B_, L, H, E = q.shape
    rows = L // GRID_W
    kh = min(KH_MAX, rows)
    krb = min(kh + 2, rows)
    n_rb = rows // QRB
    n_cb = GRID_W // QCB
    q_rows = np.arange(n_rb)[:, None] * QRB + np.arange(QRB)[None, :]
    r_start = np.clip(q_rows - kh // 2, 0, rows - kh)
    band = np.clip(np.arange(n_rb) * QRB - kh // 2, 0, rows - krb)
    k_rows = band[:, None] + np.arange(krb)[None, :]
    row_ok = (k_rows[:, None, :] >= r_start[:, :, None]) & (k_rows[:, None, :] < r_start[:, :, None] + kh)
    dr = np.clip(k_rows[:, None, :] - q_rows[:, :, None], 1 - KH_MAX, KH_MAX - 1) + KH_MAX - 1
    q_cols = np.arange(n_cb)[:, None] * QCB + np.arange(QCB)[None, :]
    c_start = np.clip(q_cols - KW // 2, 0, GRID_W - KW)
    k_cols = np.clip(np.arange(n_cb) * QCB - KW // 2, 0, GRID_W - KCB)[:, None] + np.arange(KCB)[None, :]
    col_ok = (k_cols[:, None, :] >= c_start[:, :, None]) & (k_cols[:, None, :] < c_start[:, :, None] + KW)
    dc = np.clip(k_cols[:, None, :] - q_cols[:, :, None], 1 - KW, KW - 1) + KW - 1
    qg = q.reshape(B_, rows, GRID_W, H, E)
    kg = k.reshape(B_, rows, GRID_W, H, E)
    vg = v.reshape(B_, rows, GRID_W, H, E)
    scale = HEAD_DIM ** -0.5

    def block(args):
        a, b0, rok, dri = args
        qb = lax.dynamic_slice_in_dim(qg, a * QRB, QRB, axis=1).reshape(B_, QRB, n_cb, QCB, H, E)
        kb = lax.dynamic_slice_in_dim(kg, b0, krb, axis=1)[:, :, k_cols]
        vb = lax.dynamic_slice_in_dim(vg, b0, krb, axis=1)[:, :, k_cols]
        s = jnp.einsum('bqnchd,bknjhd->bnhqckj', qb, kb, preferred_element_type=jnp.float32) * scale
        bias = rpb[:, dri[None, :, None, :, None], dc[:, None, :, None, :]]
        ok = rok[None, :, None, :, None] & col_ok[:, None, :, None, :]
        s = jnp.where(ok[:, None], s + jnp.moveaxis(bias, 0, 1).astype(jnp.float32), NEG)
        p = jax.nn.softmax(s.reshape(s.shape[:-2] + (krb * KCB,)), axis=-1).reshape(s.shape).astype(v.dtype)
        o = jnp.einsum('bnhqckj,bknjhd->bqnchd', p, vb)
        return o.reshape(B_, QRB, GRID_W, H, E)

    xs = (jnp.arange(n_rb, dtype=jnp.int32), jnp.asarray(band, dtype=jnp.int32), jnp.asarray(row_ok), jnp.asarray(dr, dtype=jnp.int32))
    out = lax.map(block, xs)
    return out.transpose(1, 0, 2, 3, 4, 5).reshape(B_, L, H * E)


def layer(x, c, norm1_g, norm2_g, w_mod, b_mod, w_in, q_norm_g, k_norm_g, rel_bias, rpb, w_gate, b_gate, w_up_a, w_up_b, w_o, w_ff1, w_ff2):
    B_, L, _ = x.shape
    mod = (jax.nn.silu(c) @ w_mod + b_mod)[:, None, :]
    sh1, sc1, g1, sh2, sc2, g2 = jnp.split(mod, N_MOD, axis=-1)
    h = rms_norm(x, norm1_g) * (1 + sc1) + sh1
    qkv = (h @ w_in).reshape(B_, L, 3, N_HEADS, HEAD_DIM)
    q = rms_norm(qkv[:, :, 0], q_norm_g)
    k = rms_norm(qkv[:, :, 1], k_norm_g)
    v = qkv[:, :, 2]
    o_a = dilated_mixer(q[:, :, :N_HEADS_A], k[:, :, :N_HEADS_A], v[:, :, :N_HEADS_A], rel_bias)
    o_b = neighborhood_mixer(q[:, :, N_HEADS_A:], k[:, :, N_HEADS_A:], v[:, :, N_HEADS_A:], rpb)
    gate_a, gate_b = jnp.split(jax.nn.sigmoid(h @ w_gate + b_gate), 2, axis=-1)
    mixed = gate_a * (o_a @ w_up_a) + gate_b * (o_b @ w_up_b)
    x = x + g1 * (mixed @ w_o)
    h2 = rms_norm(x, norm2_g) * (1 + sc2) + sh2
    f = jnp.square(jax.nn.relu(h2 @ w_ff1)) @ w_ff2
    return x + g2 * f


def setup_inputs(seed: int = 0) -> dict:
    key = jax.random.key(seed)
    ks = jax.random.split(key, 20)

    def nrm(k, shape, s):
        return jax.random.normal(k, shape, jnp.float32) * s

    return {
        'x_prompt': nrm(ks[0], (BATCH, SEQ, D_MODEL), 1.0),
        'x_sample': nrm(ks[1], (DEC_BATCH, DEC_SEQ, D_MODEL), 1.0),
        'c_prompt': nrm(ks[2], (BATCH, D_MODEL), 1.0),
        'c_sample': nrm(ks[3], (DEC_BATCH, D_MODEL), 1.0),
        'norm1_g': 1.0 + nrm(ks[4], (DEPTH, D_MODEL), 0.05),
        'norm2_g': 1.0 + nrm(ks[5], (DEPTH, D_MODEL), 0.05),
        'w_mod': nrm(ks[6], (DEPTH, D_MODEL, N_MOD * D_MODEL), 0.5 * D_MODEL ** -0.5),
        'b_mod': nrm(ks[7], (DEPTH, N_MOD * D_MODEL), 0.02),
        'w_in': nrm(ks[8], (DEPTH, D_MODEL, QKV_WIDTH), D_MODEL ** -0.5),
        'q_norm_g': 1.0 + nrm(ks[9], (DEPTH, N_HEADS, HEAD_DIM), 0.05),
        'k_norm_g': 1.0 + nrm(ks[10], (DEPTH, N_HEADS, HEAD_DIM), 0.05),
        'rel_bias': nrm(ks[11], (N_BUCKETS, N_HEADS_A), 0.5),
        'rpb': nrm(ks[12], (DEPTH, N_HEADS_B, 2 * KH_MAX - 1, 2 * KW - 1), 0.5),
        'w_gate': nrm(ks[13], (DEPTH, D_MODEL, 2 * D_MODEL), D_MODEL ** -0.5),
        'b_gate': nrm(ks[14], (DEPTH, 2 * D_MODEL), 0.02),
        'w_up_a': nrm(ks[15], (DEPTH, WIDTH_A, D_MODEL), WIDTH_A ** -0.5),
        'w_up_b': nrm(ks[16], (DEPTH, WIDTH_B, D_MODEL), WIDTH_B ** -0.5),
        'w_o': nrm(ks[17], (DEPTH, D_MODEL, D_MODEL), D_MODEL ** -0.5),
        'w_ff1': nrm(ks[18], (DEPTH, D_MODEL, D_FF), D_MODEL ** -0.5),
        'w_ff2': nrm(ks[19], (DEPTH, D_FF, D_MODEL), D_FF ** -0.5),
    }


def reference(x_prompt, x_sample, c_prompt, c_sample, norm1_g, norm2_g, w_mod, b_mod, w_in, q_norm_g, k_norm_g, rel_bias, rpb, w_gate, b_gate, w_up_a, w_up_b, w_o, w_ff1, w_ff2):
    y_prompt = x_prompt
    y_sample = x_sample
    for l in range(DEPTH):
        y_prompt = layer(y_prompt, c_prompt, norm1_g[l], norm2_g[l], w_mod[l], b_mod[l], w_in[l], q_norm_g[l], k_norm_g[l], rel_bias, rpb[l], w_gate[l], b_gate[l], w_up_a[l], w_up_b[l], w_o[l], w_ff1[l], w_ff2[l])
        y_sample = layer(y_sample, c_sample, norm1_g[l], norm2_g[l], w_mod[l], b_mod[l], w_in[l], q_norm_g[l], k_norm_g[l], rel_bias, rpb[l], w_gate[l], b_gate[l], w_up_a[l], w_up_b[l], w_o[l], w_ff1[l], w_ff2[l])
    return (y_prompt, y_sample)
```

```python
import numpy as np
from contextlib import ExitStack
import concourse.bass as bass
import concourse.mybir as mybir
from concourse.bass_utils import run_bass_kernel_spmd

F32 = mybir.dt.float32
BF16 = mybir.dt.bfloat16
AF = mybir.ActivationFunctionType
ALU = mybir.AluOpType

D = 1024
DEPTH = 2
NH = 20
HD = 64
DIL = (1, 4, 16)
D_FF = 4096
EPS = 1e-6
NEGV = -30000.0
GRID_W = 64
T = 512
SEG = 2048

SAME_ENG_SYNC = True
DEBUG_SCRATCH = False
USE_PS_BITCAST = True
EPOCH = 30000
DMA_RING = 16


class Buf:
    __slots__ = ("name", "w", "rs", "rd")

    def __init__(self, name=""):
        self.name = name
        self.w = None
        self.rs = {}
        self.rd = []


class Op:
    __slots__ = ("eng", "fn", "waits", "signal", "sem", "val", "is_dma", "idx")

    def __init__(self, eng, fn, is_dma):
        self.eng = eng
        self.fn = fn
        self.waits = []
        self.signal = False
        self.sem = None
        self.val = 0
        self.is_dma = is_dma
        self.idx = 0


class Prog:
    ENGS = ("pe", "act", "dve", "pool", "sp")

    def __init__(self, nc, stack):
        self.nc = nc
        self.stack = stack
        self.ops = {e: [] for e in self.ENGS}
        self.seen = {e: {f: -1 for f in self.ENGS} for e in self.ENGS}
        self.seen_dma = {e: {} for e in self.ENGS}
        self.rings = {}
        self.nsem = 0
        self.pend = {e: [] for e in self.ENGS}

    def new_sem(self, name):
        self.nsem += 1
        return self.stack.enter_context(self.nc.semaphore(f"{name}_{self.nsem}"))

    def _need(self, op, P, raw):
        E = op.eng
        if P is op:
            return
        if P.is_dma:
            sd = self.seen_dma[E]
            key = id(P.sem)
            if sd.get(key, -1) >= P.val:
                return
            sd[key] = P.val
            op.waits.append(P)
            return
        F = P.eng
        if F == E:
            if E == "pe" or E == "sp" or not SAME_ENG_SYNC or not raw:
                return
        if P.idx <= self.seen[E][F]:
            return
        self.seen[E][F] = P.idx
        P.signal = True
        op.waits.append(P)

    def _add(self, op, reads, writes):
        E = op.eng
        lst = self.ops[E]
        op.idx = len(lst)
        if self.pend[E]:
            for Pp in self.pend[E]:
                self._need(op, Pp, True)
            self.pend[E] = []
        for b in reads:
            if b.w is not None:
                self._need(op, b.w, True)
        for b in writes:
            if b.w is not None:
                self._need(op, b.w, True)
            for r in b.rs.values():
                self._need(op, r, False)
            for r in b.rd:
                self._need(op, r, False)
        for b in writes:
            b.w = op
            b.rs = {}
            b.rd = []
        for b in reads:
            if op.is_dma:
                b.rd.append(op)
            else:
                b.rs[E] = op
        lst.append(op)
        return op

    def op(self, eng, fn, reads=(), writes=()):
        return self._add(Op(eng, fn, False), reads, writes)

    def dma(self, q, fn, reads=(), writes=()):
        op = Op(q, fn, True)
        ring = self.rings.get(q)
        if ring is None:
            ring = {"n": 0, "slots": [None] * DMA_RING}
            self.rings[q] = ring
        s = ring["n"] % DMA_RING
        ring["n"] += 1
        slot = ring["slots"][s]
        if slot is None:
            slot = {"sem": self.new_sem(f"d{q}{s}"), "val": 0, "last": None}
            ring["slots"][s] = slot
        if slot["last"] is not None:
            self._need(op, slot["last"], False)
        if slot["val"] + 16 > EPOCH:
            slot["sem"] = self.new_sem(f"d{q}{s}")
            slot["val"] = 0
        slot["val"] += 16
        op.sem = slot["sem"]
        op.val = slot["val"]
        slot["last"] = op
        return self._add(op, reads, writes)

    def barrier(self):
        lasts = []
        for e in self.ENGS:
            for o in reversed(self.ops[e]):
                if not o.is_dma:
                    lasts.append(o)
                    break
        for ring in self.rings.values():
            for slot in ring["slots"]:
                if slot is not None and slot["last"] is not None:
                    lasts.append(slot["last"])
        for e in self.ENGS:
            self.pend[e] = list(lasts)

    def emit(self):
        nc = self.nc
        for e in self.ENGS:
            cnt = 0
            sems = []
            for o in self.ops[e]:
                if o.is_dma or not o.signal:
                    continue
                ep = cnt // EPOCH
                while len(sems) <= ep:
                    sems.append(self.new_sem(f"e{e}"))
                o.sem = sems[ep]
                o.val = cnt % EPOCH + 1
                cnt += 1
        fin = []
        for ring in self.rings.values():
            for slot in ring["slots"]:
                if slot is not None and slot["last"] is not None:
                    fin.append(slot["last"])
        engmap = {"pe": "tensor", "act": "scalar", "dve": "vector", "pool": "gpsimd", "sp": "sync"}
        with nc.Block() as block:
            for e in self.ENGS:
                ops = self.ops[e]

                def body(eng, ops=ops, e=e):
                    for o in ops:
                        for Pw in o.waits:
                            eng.wait_ge(Pw.sem, Pw.val)
                        ins = o.fn(eng)
                        if o.is_dma:
                            ins.then_inc(o.sem, 16)
                        elif o.signal:
                            ins.then_inc(o.sem, 1)
                    if e == "sp":
                        for Pw in fin:
                            eng.wait_ge(Pw.sem, Pw.val)

                getattr(block, engmap[e])(body)


def t5_bucket(rel):
    nb = 16
    max_exact = 8
    ret = (rel > 0).astype(np.int32) * nb
    n = np.abs(rel)
    large = max_exact + (np.log(np.maximum(n, 1) / max_exact) / np.log(1024 / max_exact) * (nb - max_exact)).astype(np.int32)
    large = np.minimum(large, nb - 1)
    return (ret + np.where(n < max_exact, n, large)).astype(np.int32)


def make_consts():
    c = {}
    c["ident"] = np.eye(128, dtype=np.float32)
    c["antiid"] = np.eye(128, dtype=np.float32)[::-1].copy()
    bo = np.zeros((128, 128), np.float32)
    bo[:64, :64] = 1.0 / 64
    bo[64:, 64:] = 1.0 / 64
    c["blockones"] = bo
    c["meanones"] = np.full((128, 128), 1.0 / 1024, np.float32)
    oh = np.zeros((3, 33, 384), np.float32)
    w = np.arange(384)
    j = 191 - w
    for g, d in enumerate(DIL):
        ok = np.abs(j) <= 64
        b = t5_bucket(d * j)
        for ww in range(384):
            if ok[ww]:
                oh[g, b[ww], ww] = 1.0
            else:
                oh[g, 32, ww] = 1.0
    c["ohaug"] = oh
    sel = np.zeros((31, 128), np.float32)
    for dc in range(31):
        sel[dc, 78 - dc] = 1.0
    c["sel"] = sel
    cm = np.zeros((128, 64), np.float32)
    for ci in range(2):
        for cj in range(64):
            kj = 63 - cj
            for qj in range(64):
                cs = min(max(qj - 8, 0), 48)
                if not (cs <= kj < cs + 16):
                    cm[ci * 64 + cj, qj] = NEGV
    c["cmask"] = cm
    return c


WDEF = {
    "w_in": (1024, 3840, 8, 512),
    "w_gate": (1024, 2048, 8, 512),
    "w_up_a": (256, 1024, 2, 1024),
    "w_up_b": (512, 1024, 4, 1024),
    "w_o": (1024, 1024, 8, 512),
    "w_ff1": (1024, 4096, 8, 512),
    "w_ff2": (4096, 1024, 32, 128),
}


def nblk(name):
    K, N, KC, NC = WDEF[name]
    return (N + NC - 1) // NC


class Arena:
    def __init__(self, nc, lo, hi):
        self.nc = nc
        self.lo = lo
        self.hi = hi
        self.p = lo
        self.n = 0

    def alloc(self, shape, dtype):
        esz = 4 if dtype == F32 else 2
        nb = esz
        for s in shape[1:]:
            nb *= s
        off = (self.p + 63) // 64 * 64
        assert off + nb <= self.hi, f"SBUF arena overflow: {off + nb} > {self.hi}"
        self.p = off + nb
        self.n += 1
        return self.nc.alloc_sbuf_tensor_at(f"ar{self.n}", list(shape), dtype, offset=off)

    def mark(self):
        return self.p

    def reset(self, m):
        self.p = m


class Ring:
    def __init__(self, items):
        self.items = items
        self.i = 0

    def next(self):
        it = self.items[self.i % len(self.items)]
        self.i += 1
        return it


class Builder:
    def __init__(self, seqs, depth=DEPTH):
        self.seqs = list(seqs)
        self.NS = len(seqs)
        self.NT = sum(seqs)
        self.depth = depth
        self.s0 = [sum(seqs[:i]) for i in range(self.NS)]

    def build(self):
        nc = bass.Bass("TRN2", target_bir_lowering=False)
        self.nc = nc
        NT, NS, L_ = self.NT, self.NS, self.depth
        di = lambda n, s, dt=F32: nc.dram_tensor(n, list(s), dt, kind="ExternalInput").ap()
        dsc = lambda n, s, dt: nc.dram_tensor(n, list(s), dt, kind=("ExternalOutput" if DEBUG_SCRATCH else "Internal")).ap()
        self.x = di("x", [NT, D])
        self.c = di("c", [NS, D])
        self.norm1_g = di("norm1_g", [DEPTH, D])
        self.norm2_g = di("norm2_g", [DEPTH, D])
        self.w_mod = di("w_mod", [DEPTH, D, 6 * D])
        self.b_mod = di("b_mod", [DEPTH, 6 * D])
        self.q_norm_g = di("q_norm_g", [DEPTH, NH, HD])
        self.k_norm_g = di("k_norm_g", [DEPTH, NH, HD])
        self.rel_bias = di("rel_bias", [32, 12])
        self.rpb = di("rpb", [DEPTH, 8, 15, 31])
        self.b_gate = di("b_gate", [DEPTH, 2 * D])
        self.wsrc = {}
        for n, (K, N, KC, NC) in WDEF.items():
            self.wsrc[n] = di(n, [DEPTH, K, N])
        cs = make_consts()
        self.cin = {k: di("c_" + k, v.shape) for k, v in cs.items()}
        self.y = nc.dram_tensor("y", [NT, D], F32, kind="ExternalOutput").ap()
        self.wq = {n: dsc("wq_" + n, [DEPTH, nblk(n), 128, 4096], BF16) for n in WDEF}
        self.xs = dsc("xs", [8, 128, NT], F32)
        self.qs = dsc("qs", [10, 128, NT], BF16)
        self.ks = dsc("ks", [10, 128, NT], BF16)
        self.vs = dsc("vs", [NT, 1280], BF16)
        self.osr = dsc("osr", [NT, 1300], F32)
        self.rfull = dsc("rfull", [12, 384], BF16)
        self.pnb = dsc("pnb", [DEPTH, 120, 128], BF16)
        with ExitStack() as st:
            self.P = Prog(nc, st)
            self.ar = Arena(nc, 16640, 229376 - 64)
            self.ps = [nc.alloc_psum_tensor(f"psb{i}", [128, 512], F32) for i in range(7)]
            self.psb = [Buf(f"ps{i}") for i in range(8)]
            self.ps16 = nc.alloc_psum_tensor("psb7", [128, 1024], BF16)
            self.prologue()
            for l in range(self.depth):
                self.dense_pass(l)
                self.attention(l)
            self.dense_pass(self.depth)
            self.P.emit()
        return nc

    def prologue(self):
        nc, P, ar = self.nc, self.P, self.ar
        NS = self.NS
        self.ident = ar.alloc([128, 128], F32)
        self.identb = ar.alloc([128, 128], BF16)
        self.J = ar.alloc([128, 128], BF16)
        self.bones = ar.alloc([128, 128], BF16)
        self.mones = ar.alloc([128, 128], BF16)
        self.tab = ar.alloc([128, 128], F32)
        self.modT = ar.alloc([128, DEPTH * 48 * NS], F32)
        self.G = ar.alloc([128, DEPTH * 2 * 8 * NS], F32)
        self.qg8 = ar.alloc([128, DEPTH * 10], F32)
        self.cmask = ar.alloc([128, 64], BF16)
        bC = Buf("consts")
        self.bC = bC
        mk = ar.mark()
        tmpf = ar.alloc([128, 128], F32)
        btmp = Buf()
        P.dma("sp", lambda e: e.dma_start(out=self.ident[:], in_=self.cin["ident"][:, :]), writes=[bC])
        for src, dst in (("ident", self.identb), ("antiid", self.J), ("blockones", self.bones), ("meanones", self.mones)):
            P.dma("sp", lambda e, src=src: e.dma_start(out=tmpf[:], in_=self.cin[src][:, :]), writes=[btmp])
            P.op("dve", lambda e, dst=dst: e.tensor_copy(out=dst[:], in_=tmpf[:]), reads=[btmp], writes=[bC])
        ptab = ar.alloc([128, 128], F32)
        bpt = Buf()
        P.op("dve", lambda e: e.memset(ptab[:], 0.0), writes=[bpt])
        rows = []
        r = 0
        self.col = {}

        def addrows(name, ap2d, n):
            nonlocal r
            self.col[name] = r
            P.dma("sp", lambda e, r=r: e.dma_start(out=ptab[r:r + n, :], in_=ap2d), writes=[bpt])
            r += n

        addrows("c", self.c.rearrange("s (k p) -> (s k) p", p=128), NS * 8)
        addrows("n1", self.norm1_g.rearrange("l (k p) -> (l k) p", p=128), DEPTH * 8)
        addrows("n2", self.norm2_g.rearrange("l (k p) -> (l k) p", p=128), DEPTH * 8)
        addrows("qg", self.q_norm_g.rearrange("l (j a) e -> (l j) (a e)", a=2), DEPTH * 10)
        addrows("kg", self.k_norm_g.rearrange("l (j a) e -> (l j) (a e)", a=2), DEPTH * 10)
        addrows("bg", self.b_gate.rearrange("l (k p) -> (l k) p", p=128), DEPTH * 16)
        assert r <= 128
        pt = self.ps[0]
        P.op("pe", lambda e: e.transpose(pt[:, 0:128], ptab[:], self.ident[:]), reads=[bpt, bC], writes=[self.psb[0]])
        P.op("dve", lambda e: e.tensor_copy(out=self.tab[:], in_=pt[:, 0:128]), reads=[self.psb[0]], writes=[bC])
        cq = self.col["qg"]
        P.op("dve", lambda e: e.tensor_scalar(out=self.qg8[:], in0=self.tab[:, cq:cq + DEPTH * 10], scalar1=0.125, scalar2=None, op0=ALU.mult), reads=[bC], writes=[bC])
        bmT = ar.alloc([128, 96], F32)
        ptab2 = ar.alloc([128, 128], F32)
        bpt2 = Buf()
        P.dma("sp", lambda e: e.dma_start(out=ptab2[0:96, :], in_=self.b_mod.rearrange("l (k p) -> (l k) p", p=128)), writes=[bpt2])
        P.op("pe", lambda e: e.transpose(pt[:, 128:224], ptab2[0:96, :], self.ident[0:96, 0:96]), reads=[bpt2, bC], writes=[self.psb[0]])
        bbm = Buf()
        P.op("dve", lambda e: e.tensor_copy(out=bmT[:], in_=pt[:, 128:224]), reads=[self.psb[0]], writes=[bbm])
        siluT = ar.alloc([128, NS * 8], F32)
        bsl = Buf()
        cc = self.col["c"]
        P.op("act", lambda e: e.activation(out=siluT[:], in_=self.tab[:, cc:cc + NS * 8], func=AF.Silu), reads=[bC], writes=[bsl])
        wm = [ar.alloc([128, 8, 512], F32) for _ in range(2)]
        bwm = [Buf(), Buf()]
        bmod = Buf("mod")
        self.bmod = bmod
        it = 0
        for l in range(self.depth):
            for blk in range(12):
                wt, bw = wm[it % 2], bwm[it % 2]
                it += 1
                P.dma("sp", lambda e, wt=wt, l=l, blk=blk: e.dma_start(out=wt[:], in_=self.w_mod[l].rearrange("(k p) n -> p k n", p=128)[:, :, blk * 512:(blk + 1) * 512]), writes=[bw])
                for c4 in range(4):
                    ccol = blk * 4 + c4
                    pb = 1 + (ccol % 2)
                    pst = self.ps[pb]
                    for kc in range(8):
                        P.op("pe", lambda e, wt=wt, kc=kc, c4=c4, pst=pst: e.matmul(pst[:, 0:NS], lhsT=wt[:, kc, c4 * 128:(c4 + 1) * 128], rhs=siluT[:].rearrange("p (s k) -> p k s", k=8)[:, kc, :], start=(kc == 0), stop=(kc == 7)), reads=[bw, bsl], writes=[self.psb[pb]])
                    o0 = (l * 48 + ccol) * NS
                    P.op("dve", lambda e, pst=pst, o0=o0, l=l, ccol=ccol: e.tensor_scalar(out=self.modT[:, o0:o0 + NS], in0=pst[:, 0:NS], scalar1=bmT[:, l * 48 + ccol:l * 48 + ccol + 1], scalar2=None, op0=ALU.add), reads=[self.psb[pb], bbm], writes=[bmod])
        for l in range(self.depth):
            for wh in range(2):
                for kc in range(8):
                    mi = (l * 48 + (1 if wh == 0 else 4) * 8 + kc) * NS
                    gi = ((l * 2 + wh) * 8 + kc) * NS
                    ncol = self.col["n1" if wh == 0 else "n2"] + l * 8 + kc
                    P.op("dve", lambda e, mi=mi, gi=gi, ncol=ncol: e.tensor_scalar(out=self.G[:, gi:gi + NS], in0=self.modT[:, mi:mi + NS], scalar1=1.0, scalar2=self.tab[:, ncol:ncol + 1], op0=ALU.add, op1=ALU.mult), reads=[bmod, bC], writes=[bmod])
        self.bwq = {}
        for l in range(self.depth):
            for n in ("w_in", "w_gate", "w_up_a", "w_up_b", "w_o", "w_ff1", "w_ff2"):
                K, N, KC, NC = WDEF[n]
                for b in range(nblk(n)):
                    ncol = min(NC, N - b * NC)
                    bb = Buf()
                    self.bwq[(n, l, b)] = bb
                    src = self.wsrc[n][l].rearrange("(k p) n -> p k n", p=128)[:, :, b * NC:b * NC + ncol]
                    dst = self.wq[n][l, b][:, 0:KC * ncol].rearrange("p (k n) -> p k n", k=KC)
                    P.dma("pool", lambda e, src=src, dst=dst: e.dma_start(out=dst, in_=src), writes=[bb])
        taug = ar.alloc([33, 12], F32)
        btg = Buf()
        P.op("dve", lambda e: e.memset(taug[32:33, :], NEGV), writes=[btg])
        P.dma("sp", lambda e: e.dma_start(out=taug[0:32, :], in_=self.rel_bias[:, :]), writes=[btg])
        oh = ar.alloc([33, 3, 384], F32)
        boh = Buf()
        P.dma("sp", lambda e: e.dma_start(out=oh[:], in_=self.cin["ohaug"].rearrange("g b w -> b g w")), writes=[boh])
        rst = ar.alloc([4, 3, 384], BF16)
        brst = Buf()
        for g in range(3):
            pst = self.ps[3]
            P.op("pe", lambda e, g=g, pst=pst: e.matmul(pst[0:4, 0:384], lhsT=taug[:, 4 * g:4 * g + 4], rhs=oh[:, g, :], start=True, stop=True), reads=[btg, boh], writes=[self.psb[3]])
            P.op("dve", lambda e, g=g, pst=pst: e.tensor_copy(out=rst[:, g, :], in_=pst[0:4, 0:384]), reads=[self.psb[3]], writes=[brst])
        self.brf = Buf("rfull")
        P.dma("pool", lambda e: e.dma_start(out=self.rfull.rearrange("(g h) w -> h g w", h=4), in_=rst[:]), reads=[brst], writes=[self.brf])
        self.bpnb = Buf("pnb")
        sel = ar.alloc([31, 128], F32)
        bsel = Buf()
        P.dma("sp", lambda e: e.dma_start(out=sel[:], in_=self.cin["sel"][:, :]), writes=[bsel])
        for l in range(self.depth):
            rc = ar.alloc([120, 31], F32)
            brc = Buf()
            P.dma("sp", lambda e, l=l, rc=rc: e.dma_start(out=rc[:], in_=self.rpb[l].rearrange("h r c -> (h r) c")), writes=[brc])
            pst = self.ps[4]
            P.op("pe", lambda e, rc=rc, pst=pst: e.transpose(pst[0:31, 0:120], rc[:], self.ident[0:120, 0:120]), reads=[brc, bC], writes=[self.psb[4]])
            xr = ar.alloc([31, 120], F32)
            bxr = Buf()
            P.op("dve", lambda e, xr=xr, pst=pst: e.tensor_copy(out=xr[:], in_=pst[0:31, 0:120]), reads=[self.psb[4]], writes=[bxr])
            P.op("pe", lambda e, xr=xr, pst=pst: e.matmul(pst[0:120, 128:256], lhsT=xr[:], rhs=sel[:], start=True, stop=True), reads=[bxr, bsel], writes=[self.psb[4]])
            pn = ar.alloc([120, 128], BF16)
            bpn = Buf()
            P.op("dve", lambda e, pn=pn, pst=pst: e.tensor_copy(out=pn[:], in_=pst[0:120, 128:256]), reads=[self.psb[4]], writes=[bpn])
            P.dma("pool", lambda e, l=l, pn=pn: e.dma_start(out=self.pnb[l], in_=pn[:]), reads=[bpn], writes=[self.bpnb])
        P.dma("pool", lambda e: e.dma_start(out=self.cmask[:], in_=self.cin["cmask"][:, :]), writes=[bC])
        P.barrier()
        self.base_mark = mk
        self.bxs = [[Buf() for _ in range(self.NT // T)] for _ in range(1)][0]
        self.bqk = Buf("qk")
        self.bv = Buf("v")
        self.bos = Buf("os")

    def dense_pass(self, lp):
        nc, P, ar = self.nc, self.P, self.ar
        NS, NT = self.NS, self.NT
        ar.reset(self.base_mark)
        doC = lp > 0
        doA = lp < self.depth
        lc = lp - 1
        la = lp
        NSUB = T // 128
        xTs = [ar.alloc([128, 8, T], F32) for _ in range(2)]
        hTs = [ar.alloc([128, 8, T], BF16) for _ in range(2)]
        rstd = ar.alloc([128, T], F32)
        big = ar.alloc([128, 32, T], BF16)
        iost = nc.alloc_sbuf_tensor_at("iost%d" % lp, [128, 4, 1024], F32, offset=_off(big) + 8 * T * 4)
        tmps = [ar.alloc([128, T], F32) for _ in range(3)]
        gT = ar.alloc([128, 16, T], BF16)
        sq = ar.alloc([128, 8, T], BF16)
        oraw = [ar.alloc([128, 1300], F32) for _ in range(2)]
        otoks = [ar.alloc([128, 768], BF16) for _ in range(NSUB)]
        osum = ar.alloc([128, 260], F32)
        rden = ar.alloc([128, 12], F32)
        oT = ar.alloc([128, 6, T], BF16)
        t12 = [ar.alloc([128, T], BF16) for _ in range(4)]
        mixT = ar.alloc([128, 8, T], BF16)
        rl = [ar.alloc([128, T], BF16) for _ in range(2)]
        sq2 = [ar.alloc([128, T], BF16) for _ in range(2)]
        rs2 = [ar.alloc([128, T], F32) for _ in range(2)]
        qst = [ar.alloc([128, T], BF16) for _ in range(3)]
        vst = [ar.alloc([128, 512], BF16) for _ in range(3)]
        NW = 5
        wsl = [ar.alloc([128, 4096], BF16) for _ in range(NW)]
        bxTs = [[Buf() for _ in range(8)] for _ in range(2)]
        bhTs = [[Buf() for _ in range(8)] for _ in range(2)]
        brstd = Buf()
        bbig = [Buf() for _ in range(32)]
        btmps = [Buf() for _ in range(3)]
        bgT = [Buf() for _ in range(16)]
        bsq = [Buf() for _ in range(8)]
        boraw = [Buf(), Buf()]
        botoks = [Buf() for _ in range(NSUB)]
        bosum, brden = Buf(), Buf()
        boT = Buf()
        bt12 = [Buf() for _ in range(4)]
        bmix = [Buf() for _ in range(8)]
        brl = [Buf(), Buf()]
        bsq2 = [Buf(), Buf()]
        brs2 = [Buf(), Buf()]
        bqst = [Buf() for _ in range(3)]
        bvst = [Buf(), Buf(), Buf()]
        bws = [Buf() for _ in range(NW)]
        bC, bmod = self.bC, self.bmod
        tab, modT, G = self.tab, self.modT, self.G
        ntiles = NT // T
        R = {}

        def mkrings():
            R["acc"] = Ring([(self.ps[i], self.psb[i]) for i in range(5)])
            R["stt"] = Ring([(self.ps[i], self.psb[i]) for i in (5, 6)])
            R["t12"] = Ring(list(zip(t12, bt12)))
            R["rl"] = Ring(list(zip(rl, brl)))
            R["sq2"] = Ring(list(zip(sq2, bsq2)))
            R["rs2"] = Ring(list(zip(rs2, brs2)))
            R["qst"] = Ring(list(zip(qst, bqst)))
            R["oraw"] = Ring(list(zip(oraw, boraw)))
            R["tmp"] = Ring(list(zip(tmps, btmps)))
            R["vst"] = Ring(list(zip(vst, bvst)))

        order = []
        wstate = {"emit": 0, "use": 0, "dry": True}

        def wget(n, l, b):
            K_, N_, KC, NC = WDEF[n]
            ncol = min(NC, N_ - b * NC)
            if wstate["dry"]:
                order.append((n, l, b))
                return None, None, ncol
            while wstate["emit"] < len(order) and wstate["emit"] < wstate["use"] + NW - 2:
                i = wstate["emit"]
                n2, l2, b2 = order[i]
                sl, bs = wsl[i % NW], bws[i % NW]
                P.dma("sp", lambda e, sl=sl, n2=n2, l2=l2, b2=b2: e.dma_start(out=sl[:], in_=self.wq[n2][l2, b2]), reads=[self.bwq[(n2, l2, b2)]], writes=[bs])
                wstate["emit"] += 1
            i = wstate["use"]
            wstate["use"] += 1
            assert order[i] == (n, l, b), (order[i], (n, l, b))
            return wsl[i % NW][:, 0:KC * ncol].rearrange("p (k n) -> p k n", k=KC), bws[i % NW], ncol

        def op(*a, **k):
            if not wstate["dry"]:
                P.op(*a, **k)

        def dma(*a, **k):
            if not wstate["dry"]:
                P.dma(*a, **k)

        def mcol(l, j, kc, s):
            o = (l * 48 + j * 8 + kc) * NS + s
            return modT[:, o:o + 1]

        def gcol(l, wh, kc, s):
            o = ((l * 2 + wh) * 8 + kc) * NS + s
            return G[:, o:o + 1]

        def tinfo(ti):
            t0 = ti * T
            s = max(i for i in range(NS) if self.s0[i] <= t0)
            return t0, s, self.s0[s], self.seqs[s]

        def norm_sq(xi, kc):
            xT, bxT = xTs[xi], bxTs[xi]
            op("act", lambda e, kc=kc: e.activation(out=sq[:, kc, :], in_=xT[:, kc, :], func=AF.Square), reads=[bxT[kc]], writes=[bsq[kc]])

        def norm_fin(l, wh, s, xi, hi):
            xT, bxT, hT, bhT = xTs[xi], bxTs[xi], hTs[hi], bhTs[hi]
            pst, bp = R["stt"].next()
            for kc in range(8):
                op("pe", lambda e, kc=kc, pst=pst: e.matmul(pst[:, 0:T], lhsT=self.mones[:], rhs=sq[:, kc, :], start=(kc == 0), stop=(kc == 7)), reads=[bsq[kc], bC], writes=[bp])
            op("act", lambda e, pst=pst: e.activation(out=rstd[:], in_=pst[:, 0:T], func=AF.Ln, bias=EPS, scale=1.0), reads=[bp], writes=[brstd])
            op("act", lambda e: e.activation(out=rstd[:], in_=rstd[:], func=AF.Exp, scale=-0.5), reads=[brstd], writes=[brstd])
            shj = 0 if wh == 0 else 3
            for kc in range(8):
                tm, btm = R["tmp"].next()
                op("dve", lambda e, kc=kc, tm=tm: e.scalar_tensor_tensor(out=tm[:], in0=xT[:, kc, :], scalar=gcol(l, wh, kc, s), in1=rstd[:], op0=ALU.mult, op1=ALU.mult), reads=[bxT[kc], brstd, bmod], writes=[btm])
                op("act", lambda e, kc=kc, tm=tm: e.activation(out=hT[:, kc, :], in_=tm[:], func=AF.Identity, bias=mcol(l, shj, kc, s), scale=1.0), reads=[btm, bmod], writes=[bhT[kc]])

        def norm(l, wh, s, xi, hi):
            for kc in range(8):
                norm_sq(xi, kc)
            norm_fin(l, wh, s, xi, hi)

        def load(ti):
            t0, s, s0, Ls = tinfo(ti)
            xi = ti % 2
            xT, bxT = xTs[xi], bxTs[xi]
            if lp == 0:
                for sub in range(NSUB):
                    dma("sp", lambda e, sub=sub: e.dma_start(out=iost[:, sub, :], in_=self.x[t0 + sub * 128:t0 + (sub + 1) * 128, :]), writes=bbig[16:32])
                for kc in range(8):
                    pst, bp = R["acc"].next()
                    for sub in range(NSUB):
                        op("pe", lambda e, kc=kc, sub=sub, pst=pst: e.transpose(pst[:, sub * 128:(sub + 1) * 128], iost[:, sub, kc * 128:(kc + 1) * 128], self.ident[:]), reads=bbig[16:32] + [bC], writes=[bp])
                    op("dve", lambda e, kc=kc, pst=pst: e.tensor_copy(out=xT[:, kc, :], in_=pst[:, 0:T]), reads=[bp], writes=[bxT[kc]])
            else:
                dma("sp", lambda e: e.dma_start(out=xT[:], in_=self.xs[:, :, t0:t0 + T].rearrange("k p t -> p k t")), reads=[self.bxs[ti]], writes=bxT)

        def omerge(ti):
            t0, s, s0, Ls = tinfo(ti)
            for sub in range(NSUB):
                otok, botok = otoks[sub], botoks[sub]
                orw, bor = R["oraw"].next()
                dma("sp", lambda e, orw=orw, sub=sub: e.dma_start(out=orw[:], in_=self.osr[t0 + sub * 128:t0 + (sub + 1) * 128, :]), reads=[self.bos], writes=[bor])
                op("pool", lambda e, orw=orw: e.tensor_tensor(out=osum[:], in0=orw[:, 0:260], in1=orw[:, 260:520], op=ALU.add), reads=[bor], writes=[bosum])
                op("pool", lambda e, orw=orw: e.tensor_tensor(out=osum[:], in0=osum[:], in1=orw[:, 520:780], op=ALU.add), reads=[bor, bosum], writes=[bosum])
                op("dve", lambda e: e.reciprocal(out=rden[:, 0:4], in_=osum[:].rearrange("p (h e) -> p h e", e=65)[:, :, 64]), reads=[bosum], writes=[brden])
                op("dve", lambda e, orw=orw: e.reciprocal(out=rden[:, 4:12], in_=orw[:, 780:1300].rearrange("p (h e) -> p h e", e=65)[:, :, 64]), reads=[bor], writes=[brden])
                op("dve", lambda e, otok=otok: e.tensor_tensor(out=otok[:, 0:256].rearrange("p (h e) -> p h e", e=64), in0=osum[:].rearrange("p (h e) -> p h e", e=65)[:, :, 0:64], in1=rden[:, 0:4].unsqueeze(2).to_broadcast([128, 4, 64]), op=ALU.mult), reads=[bosum, brden], writes=[botok])
                op("dve", lambda e, orw=orw, otok=otok: e.tensor_tensor(out=otok[:, 256:768].rearrange("p (h e) -> p h e", e=64), in0=orw[:, 780:1300].rearrange("p (h e) -> p h e", e=65)[:, :, 0:64], in1=rden[:, 4:12].unsqueeze(2).to_broadcast([128, 8, 64]), op=ALU.mult), reads=[bor, brden], writes=[botok])

        def gate(ti):
            t0, s, s0, Ls = tinfo(ti)
            l = lc
            hT, bhT = hTs[0], bhTs[0]
            for b in range(4):
                wt, bw, _ = wget("w_gate", l, b)
                for c4 in range(4):
                    cc = b * 4 + c4
                    pst, bp = R["acc"].next()
                    for kc in range(8):
                        op("pe", lambda e, wt=wt, kc=kc, c4=c4, pst=pst: e.matmul(pst[:, 0:T], lhsT=wt[:, kc, c4 * 128:(c4 + 1) * 128], rhs=hT[:, kc, :], start=(kc == 0), stop=(kc == 7)), reads=[bw, bhT[kc]], writes=[bp])
                    bcol = self.col["bg"] + l * 16 + cc
                    op("act", lambda e, cc=cc, pst=pst, bcol=bcol: e.activation(out=gT[:, cc, :], in_=pst[:, 0:T], func=AF.Sigmoid, bias=tab[:, bcol:bcol + 1], scale=1.0), reads=[bp, bC], writes=[bgT[cc]])

        def c_rest(ti, after_T=None):
            t0, s, s0, Ls = tinfo(ti)
            l = lc
            xi = ti % 2
            xT, bxT = xTs[xi], bxTs[xi]
            hT, bhT = hTs[0], bhTs[0]
            for sub in range(NSUB):
                otok, botok = otoks[sub], botoks[sub]
                if USE_PS_BITCAST:
                    pstf, bpT = R["acc"].next()
                    pT = pstf[:, 0:384].bitcast(BF16)
                else:
                    pT, bpT = self.ps16[:, 0:768], self.psb[7]
                for kc in range(6):
                    op("pe", lambda e, kc=kc, otok=otok, pT=pT: e.transpose(pT[:, kc * 128:(kc + 1) * 128], otok[:, kc * 128:(kc + 1) * 128], self.identb[:]), reads=[botok, bC], writes=[bpT])
                op("act", lambda e, sub=sub, pT=pT: e.copy(out=oT[:, :, sub * 128:(sub + 1) * 128], in_=pT.rearrange("p (k t) -> p k t", k=6)), reads=[bpT], writes=[boT])
            wa, bwa, _ = wget("w_up_a", l, 0)
            wb, bwb, _ = wget("w_up_b", l, 0)
            for cc in range(8):
                pa, bpa = R["acc"].next()
                for kc in range(2):
                    op("pe", lambda e, kc=kc, cc=cc, pa=pa: e.matmul(pa[:, 0:T], lhsT=wa[:, kc, cc * 128:(cc + 1) * 128], rhs=oT[:, kc, :], start=(kc == 0), stop=(kc == 1)), reads=[bwa, boT], writes=[bpa])
                pb_, bpb = R["acc"].next()
                for kc in range(4):
                    op("pe", lambda e, kc=kc, cc=cc, pb_=pb_: e.matmul(pb_[:, 0:T], lhsT=wb[:, kc, cc * 128:(cc + 1) * 128], rhs=oT[:, 2 + kc, :], start=(kc == 0), stop=(kc == 3)), reads=[bwb, boT], writes=[bpb])
                t1, bt1 = R["t12"].next()
                t2, bt2 = R["t12"].next()
                op("dve", lambda e, cc=cc, pa=pa, t1=t1: e.tensor_tensor(out=t1[:], in0=pa[:, 0:T], in1=gT[:, cc, :], op=ALU.mult), reads=[bpa, bgT[cc]], writes=[bt1])
                op("dve", lambda e, cc=cc, pb_=pb_, t2=t2: e.tensor_tensor(out=t2[:], in0=pb_[:, 0:T], in1=gT[:, 8 + cc, :], op=ALU.mult), reads=[bpb, bgT[8 + cc]], writes=[bt2])
                op("pool", lambda e, cc=cc, t1=t1, t2=t2: e.tensor_tensor(out=mixT[:, cc, :], in0=t1[:], in1=t2[:], op=ALU.add), reads=[bt1, bt2], writes=[bmix[cc]])
            for b in range(2):
                wt, bw, _ = wget("w_o", l, b)
                for c4 in range(4):
                    cc = b * 4 + c4
                    pst, bp = R["acc"].next()
                    for kc in range(8):
                        op("pe", lambda e, wt=wt, kc=kc, c4=c4, pst=pst: e.matmul(pst[:, 0:T], lhsT=wt[:, kc, c4 * 128:(c4 + 1) * 128], rhs=mixT[:, kc, :], start=(kc == 0), stop=(kc == 7)), reads=[bw, bmix[kc]], writes=[bp])
                    op("dve", lambda e, cc=cc, pst=pst: e.scalar_tensor_tensor(out=xT[:, cc, :], in0=pst[:, 0:T], scalar=mcol(l, 2, cc, s), in1=xT[:, cc, :], op0=ALU.mult, op1=ALU.add), reads=[bp, bxT[cc], bmod], writes=[bxT[cc]])
                    norm_sq(xi, cc)
            norm_fin(l, 1, s, xi, 0)
            if after_T is not None:
                after_T()
            for b in range(8):
                wt, bw, _ = wget("w_ff1", l, b)
                for c4 in range(4):
                    cc = b * 4 + c4
                    pst, bp = R["acc"].next()
                    for kc in range(8):
                        op("pe", lambda e, wt=wt, kc=kc, c4=c4, pst=pst: e.matmul(pst[:, 0:T], lhsT=wt[:, kc, c4 * 128:(c4 + 1) * 128], rhs=hT[:, kc, :], start=(kc == 0), stop=(kc == 7)), reads=[bw, bhT[kc]], writes=[bp])
                    r_, br_ = R["rl"].next()
                    op("act", lambda e, pst=pst, r_=r_: e.activation(out=r_[:], in_=pst[:, 0:T], func=AF.Relu), reads=[bp], writes=[br_])
                    op("pool", lambda e, cc=cc, r_=r_: e.tensor_tensor(out=big[:, cc, :], in0=r_[:], in1=r_[:], op=ALU.mult), reads=[br_], writes=[bbig[cc]])
            if ti + 1 < ntiles:
                t0n, sn, _, _ = tinfo(ti + 1)
                norm(lc, 0, sn, (ti + 1) % 2, 0)
            for cc in range(8):
                wt, bw, _ = wget("w_ff2", l, cc)
                pst, bp = R["acc"].next()
                for kc in range(32):
                    op("pe", lambda e, wt=wt, kc=kc, pst=pst: e.matmul(pst[:, 0:T], lhsT=wt[:, kc, :], rhs=big[:, kc, :], start=(kc == 0), stop=(kc == 31)), reads=[bw, bbig[kc]], writes=[bp])
                op("dve", lambda e, cc=cc, pst=pst: e.scalar_tensor_tensor(out=xT[:, cc, :], in0=pst[:, 0:T], scalar=mcol(l, 5, cc, s), in1=xT[:, cc, :], op0=ALU.mult, op1=ALU.add), reads=[bp, bxT[cc], bmod], writes=[bxT[cc]])
                if doA:
                    norm_sq(xi, cc)
            if not doA:
                for sub in range(NSUB):
                    for hf in range(2):
                        pst, bp = R["acc"].next()
                        for k4 in range(4):
                            kc = hf * 4 + k4
                            op("pe", lambda e, kc=kc, k4=k4, sub=sub, pst=pst: e.transpose(pst[:, k4 * 128:(k4 + 1) * 128], xT[:, kc, sub * 128:(sub + 1) * 128], self.ident[:]), reads=[bxT[kc], bC], writes=[bp])
                        wr = [bbig[16 + sub * 4 + hf * 2], bbig[16 + sub * 4 + hf * 2 + 1]]
                        if hf:
                            op("dve", lambda e, hf=hf, sub=sub, pst=pst: e.tensor_copy(out=iost[:, sub, hf * 512:(hf + 1) * 512], in_=pst[:, 0:512]), reads=[bp], writes=wr)
                        else:
                            op("act", lambda e, hf=hf, sub=sub, pst=pst: e.copy(out=iost[:, sub, hf * 512:(hf + 1) * 512], in_=pst[:, 0:512]), reads=[bp], writes=wr)
                    dma("pool", lambda e, sub=sub: e.dma_start(out=self.y[t0 + sub * 128:t0 + (sub + 1) * 128, :], in_=iost[:, sub, :]), reads=bbig[16 + sub * 4:16 + sub * 4 + 4], writes=[])

        def store_x(ti):
            t0, s, s0, Ls = tinfo(ti)
            xi = ti % 2
            dma("pool", lambda e: e.dma_start(out=self.xs[:, :, t0:t0 + T].rearrange("k p t -> p k t"), in_=xTs[xi][:]), reads=bxTs[xi], writes=[self.bxs[ti]])

        def a_qk(ti, hi, j0, j1, held):
            t0, s, s0, Ls = tinfo(ti)
            l = la
            hT, bhT = hTs[hi], bhTs[hi]

            def finish(ctx):
                j, pst, bp, s2, bs2 = ctx
                jj = j % 10
                d = DIL[jj // 2] if jj < 6 else 1
                pm, bpm = R["stt"].next()
                op("pe", lambda e, pm=pm, s2=s2: e.matmul(pm[:, 0:T], lhsT=self.bones[:], rhs=s2[:], start=True, stop=True), reads=[bs2, bC], writes=[bpm])
                r2, br2 = R["rs2"].next()
                op("act", lambda e, pm=pm, r2=r2: e.activation(out=r2[:], in_=pm[:, 0:T], func=AF.Ln, bias=EPS, scale=1.0), reads=[bpm], writes=[br2])
                op("act", lambda e, r2=r2: e.activation(out=r2[:], in_=r2[:], func=AF.Exp, scale=-0.5), reads=[br2], writes=[br2])
                qo, bqo = R["qst"].next()
                if j < 10:
                    gc = self.qg8[:, l * 10 + jj:l * 10 + jj + 1]
                else:
                    kcol = self.col["kg"] + l * 10 + jj
                    gc = tab[:, kcol:kcol + 1]
                op("dve", lambda e, pst=pst, r2=r2, qo=qo, gc=gc, d=d: e.scalar_tensor_tensor(out=qo[:].rearrange("p (r m) -> p r m", r=d), in0=pst[:, 0:T].rearrange("p (m r) -> p r m", r=d), scalar=gc, in1=r2[:].rearrange("p (m r) -> p r m", r=d), op0=ALU.mult, op1=ALU.mult), reads=[bp, br2, bC], writes=[bqo])
                dst = (self.qs if j < 10 else self.ks)
                n = Ls // d
                col0 = s0 + (t0 - s0) // d
                dap = bass.AP(dst.tensor, jj * 128 * NT + col0, [[NT, 128], [n, d], [1, T // d]])
                dma("pool", lambda e, dap=dap, qo=qo, d=d: e.dma_start(out=dap, in_=qo[:].rearrange("p (r m) -> p r m", r=d)), reads=[bqo], writes=[self.bqk])

            for j in range(j0, j1):
                if j % 4 == 0:
                    held[0] = wget("w_in", l, j // 4)
                wt, bw, _ = held[0]
                c4 = j % 4
                jj = j % 10
                d = DIL[jj // 2] if jj < 6 else 1
                pst, bp = R["acc"].next()
                for kc in range(8):
                    op("pe", lambda e, wt=wt, kc=kc, c4=c4, pst=pst: e.matmul(pst[:, 0:T], lhsT=wt[:, kc, c4 * 128:(c4 + 1) * 128], rhs=hT[:, kc, :], start=(kc == 0), stop=(kc == 7)), reads=[bw, bhT[kc]], writes=[bp])
                s2, bs2 = R["sq2"].next()
                op("act", lambda e, pst=pst, s2=s2: e.activation(out=s2[:], in_=pst[:, 0:T], func=AF.Square), reads=[bp], writes=[bs2])
                if held[1] is not None:
                    finish(held[1])
                held[1] = (j, pst, bp, s2, bs2)
                while held[2] < -(-(j + 1) * 3 * NSUB // 20):
                    v_group(ti, hi, held)
            if j1 == 20:
                finish(held[1])
                held[1] = None

        def v_group(ti, hi, held):
            t0, s, s0, Ls = tinfo(ti)
            l = la
            hT, bhT = hTs[hi], bhTs[hi]
            gi = held[2]
            held[2] += 1
            vi, sub = gi // NSUB, gi % NSUB
            if sub == 0:
                held[3] = wget("w_in", l, 5 + vi)
            wt, bw, ncol = held[3]
            pst, bp = R["acc"].next()
            for kc in range(8):
                op("pe", lambda e, wt=wt, kc=kc, sub=sub, pst=pst, ncol=ncol: e.matmul(pst[:, 0:ncol], lhsT=hT[:, kc, sub * 128:(sub + 1) * 128], rhs=wt[:, kc, :], start=(kc == 0), stop=(kc == 7)), reads=[bw, bhT[kc]], writes=[bp])
            vs_, bvs_ = R["vst"].next()
            op("dve", lambda e, pst=pst, ncol=ncol, vs_=vs_: e.tensor_copy(out=vs_[:, 0:ncol], in_=pst[:, 0:ncol]), reads=[bp], writes=[bvs_])
            dma("pool", lambda e, sub=sub, vi=vi, ncol=ncol, vs_=vs_: e.dma_start(out=self.vs[t0 + sub * 128:t0 + (sub + 1) * 128, vi * 512:vi * 512 + ncol], in_=vs_[:, 0:ncol]), reads=[bvs_], writes=[self.bv])

        def emit_all():
            mkrings()
            if lp == 0:
                load(0)
                _, s_, _, _ = tinfo(0)
                norm(la, 0, s_, 0, 0)
                for ti in range(ntiles):
                    if ti + 1 < ntiles:
                        load(ti + 1)
                    store_x(ti)
                    held = [None, None, 0, None]
                    a_qk(ti, ti % 2, 0, 10, held)
                    if ti + 1 < ntiles:
                        _, sn, _, _ = tinfo(ti + 1)
                        norm(la, 0, sn, (ti + 1) % 2, (ti + 1) % 2)
                    a_qk(ti, ti % 2, 10, 20, held)
            else:
                load(0)
                omerge(0)
                _, s_, _, _ = tinfo(0)
                norm(lc, 0, s_, 0, 0)
                gate(0)
                for ti in range(ntiles):
                    if ti + 1 < ntiles:
                        load(ti + 1)
                    c_rest(ti, (lambda ti=ti: omerge(ti + 1)) if ti + 1 < ntiles else None)
                    if doA:
                        store_x(ti)
                        _, s_, _, _ = tinfo(ti)
                        norm_fin(la, 0, s_, ti % 2, 1)
                    if ti + 1 < ntiles:
                        gate(ti + 1)
                    if doA:
                        held = [None, None, 0, None]
                        a_qk(ti, 1, 0, 20, held)

        emit_all()
        wstate["dry"] = False
        emit_all()
        P.barrier()

    def attention(self, l):
        nc, P, ar = self.nc, self.P, self.ar
        NS, NT = self.NS, self.NT
        ar.reset(self.base_mark)
        bC = self.bC
        Tdil = ar.alloc([128, 12, 256], BF16)
        bTd = Buf()
        for half, off in ((0, 128), (1, 0)):
            src = bass.AP(self.rfull.tensor, off, [[1, 128], [384, 12], [1, 128]])
            P.dma("sp", lambda e, src=src, half=half: e.dma_start(out=Tdil[:, :, half * 128:(half + 1) * 128], in_=src), reads=[self.brf], writes=[bTd])
        Tnb = ar.alloc([128, 2, 8, 448], BF16)
        bTn = Buf()
        for par in range(2):
            for slot in range(7):
                e_ = (-6 if par == 0 else -7) + 2 * slot
                for ci in range(2):
                    src = bass.AP(self.pnb.tensor, l * 120 * 128 + (e_ + 8 - ci) * 128, [[1, 64], [15 * 128, 8], [1, 64]])
                    P.dma("sp", lambda e, src=src, par=par, slot=slot, ci=ci: e.dma_start(out=Tnb[ci * 64:(ci + 1) * 64, par, :, slot * 64:(slot + 1) * 64], in_=src), reads=[self.bpnb], writes=[bTn])
        P.op("dve", lambda e: e.tensor_tensor(out=Tnb[:].rearrange("p a h (s q) -> p (a h s) q", q=64), in0=Tnb[:].rearrange("p a h (s q) -> p (a h s) q", q=64), in1=self.cmask[:].unsqueeze(1).to_broadcast([128, 112, 64]), op=ALU.add), reads=[bTn, bC], writes=[bTn])
        EBd = ar.alloc([128, 12, 256], BF16)
        EBn = ar.alloc([128, 2, 8, 448], BF16)
        bEd, bEn = Buf(), Buf()
        for hh in range(12):
            pb = hh % 3
            P.op("pe", lambda e, hh=hh, pb=pb: e.matmul(self.ps[pb][:, 0:256], lhsT=self.J[:], rhs=Tdil[:, hh, :], start=True, stop=True), reads=[bC, bTd], writes=[self.psb[pb]])
            P.op("act", lambda e, hh=hh, pb=pb: e.activation(out=EBd[:, hh, :], in_=self.ps[pb][:, 0:256], func=AF.Exp), reads=[self.psb[pb]], writes=[bEd])
        for par in range(2):
            for hh in range(8):
                pb = hh % 3
                P.op("pe", lambda e, hh=hh, pb=pb, par=par: e.matmul(self.ps[pb][:, 0:448], lhsT=self.J[:], rhs=Tnb[:, par, hh, :], start=True, stop=True), reads=[bC, bTn], writes=[self.psb[pb]])
                P.op("act", lambda e, hh=hh, pb=pb, par=par: e.activation(out=EBn[:, par, hh, :], in_=self.ps[pb][:, 0:448], func=AF.Exp), reads=[self.psb[pb]], writes=[bEn])
        NCH = SEG // 128
        QT = [ar.alloc([128, 2, SEG], BF16) for _ in range(2)]
        KTW = 4096
        KT = [ar.alloc([128, 2, KTW], BF16) for _ in range(2)]
        VV = [ar.alloc([128, 40, 4, 65], BF16) for _ in range(2)]
        OST = [ar.alloc([128, NCH, 260], F32) for _ in range(2)]
        PT = [ar.alloc([128, 256], BF16) for _ in range(8)]
        bQT, bKT, bVV, bOST = [Buf(), Buf()], [Buf(), Buf()], [Buf(), Buf()], [Buf(), Buf()]
        bPT = [Buf() for _ in range(8)]
        for i in range(2):
            P.op("pool", lambda e, i=i: e.memset(VV[i][:], 1.0), writes=[bVV[i]])
            P.op("pool", lambda e, i=i: e.memset(KT[i][:], 0.0), writes=[bKT[i]])
        stR = Ring([(self.ps[i], self.psb[i]) for i in (0, 1, 2, 5, 6)])
        poR = Ring([(self.ps[i], self.psb[i]) for i in (3, 4)])
        ptR = Ring(list(zip(PT, bPT)))
        item = [0]
        pending = []

        def run_item(kind, s, g, rs, P0, S):
            ib = item[0] % 2
            item[0] += 1
            qt, kt, vv, ost = QT[ib], KT[ib], VV[ib], OST[ib]
            bq, bk, bv_, bo = bQT[ib], bKT[ib], bVV[ib], bOST[ib]
            s0, Ls = self.s0[s], self.seqs[s]
            nsub = len(rs)
            nC = S // 128
            if kind == "d":
                d = DIL[g]
                n = Ls // d
                jq = 2 * g
                vcol = 256 * g
                ocol = 260 * g
                cb = s0 + rs[0] * n
                klo, khi = max(P0 - 64, 0), min(P0 + S + 64, n)
                koff = klo - (P0 - 64)
                KW_ = S + 128
                VT_ = nC + 1
                assert nsub == 1 or (P0 == 0 and S == n)
                assert nsub * S <= SEG and nsub * KW_ <= KTW and nsub * VT_ <= 40 and nsub * nC <= SEG // 128
            else:
                d = 1
                n = Ls
                jq = 6 + 2 * g
                vcol = 768 + 256 * g
                ocol = 780 + 260 * g
                cb = s0
                klo, khi = max(P0 - 256, 0), min(P0 + S + 256, n)
                koff = klo - (P0 - 256)
            P.dma("sp", lambda e: e.dma_start(out=qt[:, :, 0:nsub * S], in_=self.qs[jq:jq + 2, :, cb + P0:cb + P0 + nsub * S].rearrange("j p t -> p j t")), reads=[self.bqk], writes=[bq])
            if kind == "d":
                if nsub > 1 or koff > 0 or khi < P0 + S + 64:
                    P.op("pool", lambda e: e.memset(kt[:, :, 0:nsub * KW_], 0.0), writes=[bk])
                for pr in range(2):
                    dstv = kt[:, pr, 0:nsub * KW_].rearrange("p (i c) -> p i c", c=KW_)[:, :, koff:koff + khi - klo]
                    srcv = bass.AP(self.ks.tensor, (jq + pr) * 128 * NT + cb + klo, [[NT, 128], [n, nsub], [1, khi - klo]])
                    P.dma("sp", lambda e, dstv=dstv, srcv=srcv: e.dma_start(out=dstv, in_=srcv), reads=[self.bqk], writes=[bk])
                for i, r in enumerate(rs):
                    for t_ in range(VT_):
                        p_lo = P0 + 128 * t_ - 64
                        a, b_ = 0, 128
                        if p_lo < 0:
                            a = 64
                        if p_lo + 128 > n:
                            b_ = 64
                        if a > 0 or b_ < 128:
                            za, zb = (0, 64) if a > 0 else (64, 128)
                            P.op("pool", lambda e, i=i, t_=t_, za=za, zb=zb: e.memset(vv[za:zb, i * VT_ + t_, :, :], 0.0), writes=[bv_])
                        src = bass.AP(self.vs.tensor, (s0 + r + d * (p_lo + a)) * 1280 + vcol, [[d * 1280, b_ - a], [64, 4], [1, 64]])
                        P.dma("sp", lambda e, i=i, t_=t_, a=a, b_=b_, src=src: e.dma_start(out=vv[a:b_, i * VT_ + t_, :, 0:64], in_=src), reads=[self.bv], writes=[bv_])
            else:
                P.dma("sp", lambda e: e.dma_start(out=kt[:, :, koff:koff + khi - klo], in_=self.ks[jq:jq + 2, :, cb + klo:cb + khi].rearrange("j p t -> p j t")), reads=[self.bqk], writes=[bk])
                R0 = P0 // 64
                rows = Ls // 64
                rbase = max(R0 - 4, 0)
                rhi = min(R0 + S // 64 + 4, rows) - 2
                for rho in range(rbase, rhi + 1):
                    src = bass.AP(self.vs.tensor, (s0 + rho * 64) * 1280 + vcol, [[1280, 128], [64, 4], [1, 64]])
                    P.dma("sp", lambda e, rho=rho, src=src: e.dma_start(out=vv[:, rho - rbase, :, 0:64], in_=src), reads=[self.bv], writes=[bv_])
            units = []
            if kind == "d":
                for i in range(nsub):
                    for c in range(nC):
                        for h in range(4):
                            units.append((i, c, h, 0))
            else:
                for c in range(nC):
                    for hpair in range(2):
                        for half in range(2):
                            units.append((0, c, 2 * hpair, half))
                            units.append((0, c, 2 * hpair + 1, half))
            pobox = {}

            def emitS2(ua, ub):
                res = []
                ctx = []
                for u in (ua, ub):
                    i, c, h, half = u
                    pair, hp = h // 2, (h % 2) * 64
                    stp, bst = stR.next()
                    pt_, bpt = ptR.next()
                    ctx.append((i, c, h, half, pair, hp, stp, bst, pt_, bpt))
                if kind == "d":
                    for ab in range(2):
                        for (i, c, h, half, pair, hp, stp, bst, pt_, bpt) in ctx:
                            qc = i * S + 128 * c
                            kc_ = i * KW_ + 128 * c + 128 * ab
                            P.op("pe", lambda e, stp=stp, pair=pair, hp=hp, qc=qc, kc_=kc_, ab=ab: e.matmul(stp[:, 128 * ab:128 * ab + 128], lhsT=kt[hp:hp + 64, pair, kc_:kc_ + 128], rhs=qt[hp:hp + 64, pair, qc:qc + 128], start=(ab == 0), stop=(ab == 1)), reads=[bk, bq], writes=[bst])
                    infos = [None, None]
                    ebs = [(EBd[:, 4 * g + cx[2], :], bEd) for cx in ctx]
                else:
                    infos = []
                    ebs = []
                    geo = []
                    for (i, c, h, half, pair, hp, stp, bst, pt_, bpt) in ctx:
                        i_ = R0 + 2 * c + half
                        rstart = min(max(i_ - 4, 0), rows - 8)
                        dr0 = rstart - i_
                        par = dr0 % 2
                        emin = -6 if par == 0 else -7
                        toff = 64 * ((dr0 - emin) // 2)
                        ebs.append((EBn[:, par, 4 * g + h, toff:toff + 256], bEn))
                        infos.append(rstart)
                        geo.append((rstart, (2 * c + half) * 64))
                    for p4 in range(4):
                        for cx, (rstart, qcol) in zip(ctx, geo):
                            (i, c, h, half, pair, hp, stp, bst, pt_, bpt) = cx
                            rho = rstart + 2 * p4
                            kcol = 64 * rho - (P0 - 256)
                            P.op("pe", lambda e, stp=stp, pair=pair, hp=hp, p4=p4, kcol=kcol, qcol=qcol: e.matmul(stp[:, 64 * p4:64 * p4 + 64], lhsT=kt[hp:hp + 64, pair, kcol:kcol + 128], rhs=qt[hp:hp + 64, pair, qcol:qcol + 64], start=(p4 == 0), stop=(p4 == 3)), reads=[bk, bq], writes=[bst])
                for cx, info, (eb, beb) in zip(ctx, infos, ebs):
                    (i, c, h, half, pair, hp, stp, bst, pt_, bpt) = cx
                    P.op("act", lambda e, stp=stp, pt_=pt_: e.activation(out=pt_[:], in_=stp[:, 0:256], func=AF.Exp), reads=[bst], writes=[bpt])
                    P.op("dve", lambda e, pt_=pt_, eb=eb: e.tensor_tensor(out=pt_[:], in0=pt_[:], in1=eb, op=ALU.mult), reads=[bpt, beb], writes=[bpt])
                    res.append((pt_, bpt, info))
                return res

            def emitPV(u, sres):
                i, c, h, half = u
                pt_, bpt, info = sres
                oc = i * nC + c
                if h == 0 and half == 0:
                    pobox[oc] = poR.next()
                po, bpo = pobox[oc]
                if kind == "d":
                    vt = i * VT_ + c
                    P.op("pe", lambda e: e.matmul(po[:, 65 * h:65 * h + 65], lhsT=pt_[:, 0:128], rhs=vv[:, vt, h, :], start=True, stop=False), reads=[bpt, bv_], writes=[bpo])
                    P.op("pe", lambda e: e.matmul(po[:, 65 * h:65 * h + 65], lhsT=pt_[:, 128:256], rhs=vv[:, vt + 1, h, :], start=False, stop=True), reads=[bpt, bv_], writes=[bpo])
                    last = (h == 3)
                else:
                    rstart = info
                    for p4 in range(4):
                        rho = rstart + 2 * p4
                        P.op("pe", lambda e, p4=p4, rho=rho: e.matmul(po[64 * half:64 * half + 64, 65 * h:65 * h + 65], lhsT=pt_[:, 64 * p4:64 * p4 + 64], rhs=vv[:, rho - rbase, h, :], start=(p4 == 0), stop=(p4 == 3)), reads=[bpt, bv_], writes=[bpo])
                    last = (h == 3 and half == 1)
                if last:
                    P.op("dve", lambda e: e.tensor_copy(out=ost[:, oc, :], in_=po[:, 0:260]), reads=[bpo], writes=[bo])

            def finish():
                for i, r in enumerate(rs):
                    tok0 = (s0 + r + d * P0) if kind == "d" else (s0 + P0)
                    dap = bass.AP(self.osr.tensor, tok0 * 1300 + ocol, [[d * 1300, 128], [128 * d * 1300, nC], [1, 260]])
                    P.dma("pool", lambda e, dap=dap, i=i: e.dma_start(out=dap, in_=ost[:, i * nC:(i + 1) * nC, :]), reads=[bo], writes=[self.bos])
                    if kind == "d":
                        for t_ in range(VT_):
                            p_lo = P0 + 128 * t_ - 64
                            if p_lo < 0:
                                P.op("pool", lambda e, i=i, t_=t_: e.memset(vv[0:64, i * VT_ + t_, :, 64:65], 1.0), reads=[bv_], writes=[bv_])
                            if p_lo + 128 > n:
                                P.op("pool", lambda e, i=i, t_=t_: e.memset(vv[64:128, i * VT_ + t_, :, 64:65], 1.0), reads=[bv_], writes=[bv_])

            for ui in range(0, len(units), 2):
                ua, ub = units[ui], units[ui + 1]
                ra, rb = emitS2(ua, ub)
                while len(pending) > 2:
                    pending.pop(0)()
                pending.append(lambda u=ua, sres=ra: emitPV(u, sres))
                pending.append(lambda u=ub, sres=rb: emitPV(u, sres))
            pending.append(finish)

        for s in range(NS):
            Ls = self.seqs[s]
            for g in range(3):
                d = DIL[g]
                n = Ls // d
                if n >= SEG:
                    for r in range(d):
                        for P0 in range(0, n, SEG):
                            run_item("d", s, g, [r], P0, SEG)
                else:
                    per = min(d, SEG // n, KTW // (n + 128), 40 // (n // 128 + 1))
                    for r0 in range(0, d, per):
                        run_item("d", s, g, list(range(r0, min(d, r0 + per))), 0, n)
            for g in range(2):
                for P0 in range(0, Ls, SEG):
                    run_item("n", s, g, [0], P0, min(SEG, Ls))
        while pending:
            pending.pop(0)()
        P.barrier()


def _off(t):
    return t.manual_sbuf_range[0]


_CACHE = {}


def _get_nc(seqs):
    key = tuple(seqs)
    if key not in _CACHE:
        _CACHE[key] = Builder(seqs).build()
    return _CACHE[key]


def kernel(x_prompt, x_sample, c_prompt, c_sample, norm1_g, norm2_g, w_mod, b_mod, w_in, q_norm_g, k_norm_g,
           rel_bias, rpb, w_gate, b_gate, w_up_a, w_up_b, w_o, w_ff1, w_ff2):
    f = lambda a: np.ascontiguousarray(np.asarray(a, dtype=np.float32))
    x_prompt, x_sample, c_prompt, c_sample = f(x_prompt), f(x_sample), f(c_prompt), f(c_sample)
    ncores = 8
    Bp, Lp, _ = x_prompt.shape
    Bs, Ls, _ = x_sample.shape
    pp, sp_ = Bp // ncores, Bs // ncores
    seqs = [Lp] * pp + [Ls] * sp_
    nc = _get_nc(seqs)
    shared = {"norm1_g": f(norm1_g), "norm2_g": f(norm2_g), "w_mod": f(w_mod), "b_mod": f(b_mod), "w_in": f(w_in),
              "q_norm_g": f(q_norm_g), "k_norm_g": f(k_norm_g), "rel_bias": f(rel_bias), "rpb": f(rpb),
              "w_gate": f(w_gate), "b_gate": f(b_gate), "w_up_a": f(w_up_a), "w_up_b": f(w_up_b), "w_o": f(w_o),
              "w_ff1": f(w_ff1), "w_ff2": f(w_ff2)}
    for k, v in make_consts().items():
        shared["c_" + k] = v
    in_maps = []
    for c in range(ncores):
        xs = [x_prompt[c * pp + i] for i in range(pp)] + [x_sample[c * sp_ + i] for i in range(sp_)]
        cs = [c_prompt[c * pp + i] for i in range(pp)] + [c_sample[c * sp_ + i] for i in range(sp_)]
        m = dict(shared)
        m["x"] = np.ascontiguousarray(np.concatenate(xs, axis=0))
        m["c"] = np.ascontiguousarray(np.stack(cs, axis=0))
        in_maps.append(m)
    res = run_bass_kernel_spmd(nc, in_maps, core_ids=list(range(ncores)))
    yp = np.empty_like(x_prompt)
    ys = np.empty_like(x_sample)
    for c in range(ncores):
        y = res.results[c]["y"]
        o = 0
        for i in range(pp):
            yp[c * pp + i] = y[o:o + Lp]
            o += Lp
        for i in range(sp_):
            ys[c * sp_ + i] = y[o:o + Ls]
            o += Ls
    return (yp, ys)
```

```python
import numpy as np
from contextlib import ExitStack
import concourse.bass as bass
import concourse.mybir as mybir
from concourse.bass_utils import run_bass_kernel_spmd

F32 = mybir.dt.float32
BF16 = mybir.dt.bfloat16
AF = mybir.ActivationFunctionType
ALU = mybir.AluOpType

D = 1024
DEPTH = 2
NH = 20
HD = 64
DIL = (1, 4, 16)
D_FF = 4096
EPS = 1e-6
NEGV = -30000.0
GRID_W = 64
T = 512
SEG = 2048

SAME_ENG_SYNC = True
DEBUG_SCRATCH = False
USE_PS_BITCAST = True
EPOCH = 30000
DMA_RING = 16


class Buf:
    __slots__ = ("name", "w", "rs", "rd")

    def __init__(self, name=""):
        self.name = name
        self.w = None
        self.rs = {}
        self.rd = []


class Op:
    __slots__ = ("eng", "fn", "waits", "signal", "sem", "val", "is_dma", "idx")

    def __init__(self, eng, fn, is_dma):
        self.eng = eng
        self.fn = fn
        self.waits = []
        self.signal = False
        self.sem = None
        self.val = 0
        self.is_dma = is_dma
        self.idx = 0


class Prog:
    ENGS = ("pe", "act", "dve", "pool", "sp")

    def __init__(self, nc, stack):
        self.nc = nc
        self.stack = stack
        self.ops = {e: [] for e in self.ENGS}
        self.seen = {e: {f: -1 for f in self.ENGS} for e in self.ENGS}
        self.seen_dma = {e: {} for e in self.ENGS}
        self.rings = {}
        self.nsem = 0
        self.pend = {e: [] for e in self.ENGS}

    def new_sem(self, name):
        self.nsem += 1
        return self.stack.enter_context(self.nc.semaphore(f"{name}_{self.nsem}"))

    def _need(self, op, P, raw):
        E = op.eng
        if P is op:
            return
        if P.is_dma:
            sd = self.seen_dma[E]
            key = id(P.sem)
            if sd.get(key, -1) >= P.val:
                return
            sd[key] = P.val
            op.waits.append(P)
            return
        F = P.eng
        if F == E:
            if E == "pe" or E == "sp" or not SAME_ENG_SYNC or not raw:
                return
        if P.idx <= self.seen[E][F]:
            return
        self.seen[E][F] = P.idx
        P.signal = True
        op.waits.append(P)

    def _add(self, op, reads, writes):
        E = op.eng
        lst = self.ops[E]
        op.idx = len(lst)
        if self.pend[E]:
            for Pp in self.pend[E]:
                self._need(op, Pp, True)
            self.pend[E] = []
        for b in reads:
            if b.w is not None:
                self._need(op, b.w, True)
        for b in writes:
            if b.w is not None:
                self._need(op, b.w, True)
            for r in b.rs.values():
                self._need(op, r, False)
            for r in b.rd:
                self._need(op, r, False)
        for b in writes:
            b.w = op
            b.rs = {}
            b.rd = []
        for b in reads:
            if op.is_dma:
                b.rd.append(op)
            else:
                b.rs[E] = op
        lst.append(op)
        return op

    def op(self, eng, fn, reads=(), writes=()):
        return self._add(Op(eng, fn, False), reads, writes)

    def dma(self, q, fn, reads=(), writes=()):
        op = Op(q, fn, True)
        ring = self.rings.get(q)
        if ring is None:
            ring = {"n": 0, "slots": [None] * DMA_RING}
            self.rings[q] = ring
        s = ring["n"] % DMA_RING
        ring["n"] += 1
        slot = ring["slots"][s]
        if slot is None:
            slot = {"sem": self.new_sem(f"d{q}{s}"), "val": 0, "last": None}
            ring["slots"][s] = slot
        if slot["last"] is not None:
            self._need(op, slot["last"], False)
        if slot["val"] + 16 > EPOCH:
            slot["sem"] = self.new_sem(f"d{q}{s}")
            slot["val"] = 0
        slot["val"] += 16
        op.sem = slot["sem"]
        op.val = slot["val"]
        slot["last"] = op
        return self._add(op, reads, writes)

    def barrier(self):
        lasts = []
        for e in self.ENGS:
            for o in reversed(self.ops[e]):
                if not o.is_dma:
                    lasts.append(o)
                    break
        for ring in self.rings.values():
            for slot in ring["slots"]:
                if slot is not None and slot["last"] is not None:
                    lasts.append(slot["last"])
        for e in self.ENGS:
            self.pend[e] = list(lasts)

    def emit(self):
        nc = self.nc
        for e in self.ENGS:
            cnt = 0
            sems = []
            for o in self.ops[e]:
                if o.is_dma or not o.signal:
                    continue
                ep = cnt // EPOCH
                while len(sems) <= ep:
                    sems.append(self.new_sem(f"e{e}"))
                o.sem = sems[ep]
                o.val = cnt % EPOCH + 1
                cnt += 1
        fin = []
        for ring in self.rings.values():
            for slot in ring["slots"]:
                if slot is not None and slot["last"] is not None:
                    fin.append(slot["last"])
        engmap = {"pe": "tensor", "act": "scalar", "dve": "vector", "pool": "gpsimd", "sp": "sync"}
        with nc.Block() as block:
            for e in self.ENGS:
                ops = self.ops[e]

                def body(eng, ops=ops, e=e):
                    for o in ops:
                        for Pw in o.waits:
                            eng.wait_ge(Pw.sem, Pw.val)
                        ins = o.fn(eng)
                        if o.is_dma:
                            ins.then_inc(o.sem, 16)
                        elif o.signal:
                            ins.then_inc(o.sem, 1)
                    if e == "sp":
                        for Pw in fin:
                            eng.wait_ge(Pw.sem, Pw.val)

                getattr(block, engmap[e])(body)


def t5_bucket(rel):
    nb = 16
    max_exact = 8
    ret = (rel > 0).astype(np.int32) * nb
    n = np.abs(rel)
    large = max_exact + (np.log(np.maximum(n, 1) / max_exact) / np.log(1024 / max_exact) * (nb - max_exact)).astype(np.int32)
    large = np.minimum(large, nb - 1)
    return (ret + np.where(n < max_exact, n, large)).astype(np.int32)


def make_consts():
    c = {}
    c["ident"] = np.eye(128, dtype=np.float32)
    c["antiid"] = np.eye(128, dtype=np.float32)[::-1].copy()
    bo = np.zeros((128, 128), np.float32)
    bo[:64, :64] = 1.0 / 64
    bo[64:, 64:] = 1.0 / 64
    c["blockones"] = bo
    c["meanones"] = np.full((128, 128), 1.0 / 1024, np.float32)
    oh = np.zeros((3, 33, 384), np.float32)
    w = np.arange(384)
    j = 191 - w
    for g, d in enumerate(DIL):
        ok = np.abs(j) <= 64
        b = t5_bucket(d * j)
        for ww in range(384):
            if ok[ww]:
                oh[g, b[ww], ww] = 1.0
            else:
                oh[g, 32, ww] = 1.0
    c["ohaug"] = oh
    sel = np.zeros((31, 128), np.float32)
    for dc in range(31):
        sel[dc, 78 - dc] = 1.0
    c["sel"] = sel
    cm = np.zeros((128, 64), np.float32)
    for ci in range(2):
        for cj in range(64):
            kj = 63 - cj
            for qj in range(64):
                cs = min(max(qj - 8, 0), 48)
                if not (cs <= kj < cs + 16):
                    cm[ci * 64 + cj, qj] = NEGV
    c["cmask"] = cm
    return c


WDEF = {
    "w_in": (1024, 3840, 8, 512),
    "w_gate": (1024, 2048, 8, 512),
    "w_up_a": (256, 1024, 2, 1024),
    "w_up_b": (512, 1024, 4, 1024),
    "w_o": (1024, 1024, 8, 512),
    "w_ff1": (1024, 4096, 8, 512),
    "w_ff2": (4096, 1024, 32, 128),
}


def nblk(name):
    K, N, KC, NC = WDEF[name]
    return (N + NC - 1) // NC


class Arena:
    def __init__(self, nc, lo, hi):
        self.nc = nc
        self.lo = lo
        self.hi = hi
        self.p = lo
        self.n = 0

    def alloc(self, shape, dtype):
        esz = 4 if dtype == F32 else 2
        nb = esz
        for s in shape[1:]:
            nb *= s
        off = (self.p + 63) // 64 * 64
        assert off + nb <= self.hi, f"SBUF arena overflow: {off + nb} > {self.hi}"
        self.p = off + nb
        self.n += 1
        return self.nc.alloc_sbuf_tensor_at(f"ar{self.n}", list(shape), dtype, offset=off)

    def mark(self):
        return self.p

    def reset(self, m):
        self.p = m


class Ring:
    def __init__(self, items):
        self.items = items
        self.i = 0

    def next(self):
        it = self.items[self.i % len(self.items)]
        self.i += 1
        return it


class Builder:
    def __init__(self, seqs, depth=DEPTH):
        self.seqs = list(seqs)
        self.NS = len(seqs)
        self.NT = sum(seqs)
        self.depth = depth
        self.s0 = [sum(seqs[:i]) for i in range(self.NS)]

    def build(self):
        nc = bass.Bass("TRN2", target_bir_lowering=False)
        self.nc = nc
        NT, NS, L_ = self.NT, self.NS, self.depth
        di = lambda n, s, dt=F32: nc.dram_tensor(n, list(s), dt, kind="ExternalInput").ap()
        dsc = lambda n, s, dt: nc.dram_tensor(n, list(s), dt, kind=("ExternalOutput" if DEBUG_SCRATCH else "Internal")).ap()
        self.x = di("x", [NT, D])
        self.c = di("c", [NS, D])
        self.norm1_g = di("norm1_g", [DEPTH, D])
        self.norm2_g = di("norm2_g", [DEPTH, D])
        self.w_mod = di("w_mod", [DEPTH, D, 6 * D])
        self.b_mod = di("b_mod", [DEPTH, 6 * D])
        self.q_norm_g = di("q_norm_g", [DEPTH, NH, HD])
        self.k_norm_g = di("k_norm_g", [DEPTH, NH, HD])
        self.rel_bias = di("rel_bias", [32, 12])
        self.rpb = di("rpb", [DEPTH, 8, 15, 31])
        self.b_gate = di("b_gate", [DEPTH, 2 * D])
        self.wsrc = {}
        for n, (K, N, KC, NC) in WDEF.items():
            self.wsrc[n] = di(n, [DEPTH, K, N])
        cs = make_consts()
        self.cin = {k: di("c_" + k, v.shape) for k, v in cs.items()}
        self.y = nc.dram_tensor("y", [NT, D], F32, kind="ExternalOutput").ap()
        self.wq = {n: dsc("wq_" + n, [DEPTH, nblk(n), 128, 4096], BF16) for n in WDEF}
        self.xs = dsc("xs", [8, 128, NT], F32)
        self.qs = dsc("qs", [10, 128, NT], BF16)
        self.ks = dsc("ks", [10, 128, NT], BF16)
        self.vs = dsc("vs", [NT, 1280], BF16)
        self.osr = dsc("osr", [NT, 1300], F32)
        self.rfull = dsc("rfull", [12, 384], BF16)
        self.pnb = dsc("pnb", [DEPTH, 120, 128], BF16)
        with ExitStack() as st:
            self.P = Prog(nc, st)
            self.ar = Arena(nc, 16640, 229376 - 64)
            self.ps = [nc.alloc_psum_tensor(f"psb{i}", [128, 512], F32) for i in range(7)]
            self.psb = [Buf(f"ps{i}") for i in range(8)]
            self.ps16 = nc.alloc_psum_tensor("psb7", [128, 1024], BF16)
            self.prologue()
            for l in range(self.depth):
                self.dense_pass(l)
                self.attention(l)
            self.dense_pass(self.depth)
            self.P.emit()
        return nc

    def prologue(self):
        nc, P, ar = self.nc, self.P, self.ar
        NS = self.NS
        self.ident = ar.alloc([128, 128], F32)
        self.identb = ar.alloc([128, 128], BF16)
        self.J = ar.alloc([128, 128], BF16)
        self.bones = ar.alloc([128, 128], BF16)
        self.mones = ar.alloc([128, 128], BF16)
        self.tab = ar.alloc([128, 128], F32)
        self.modT = ar.alloc([128, DEPTH * 48 * NS], F32)
        self.G = ar.alloc([128, DEPTH * 2 * 8 * NS], F32)
        self.qg8 = ar.alloc([128, DEPTH * 10], F32)
        self.cmask = ar.alloc([128, 64], BF16)
        bC = Buf("consts")
        self.bC = bC
        mk = ar.mark()
        tmpf = ar.alloc([128, 128], F32)
        btmp = Buf()
        P.dma("sp", lambda e: e.dma_start(out=self.ident[:], in_=self.cin["ident"][:, :]), writes=[bC])
        for src, dst in (("ident", self.identb), ("antiid", self.J), ("blockones", self.bones), ("meanones", self.mones)):
            P.dma("sp", lambda e, src=src: e.dma_start(out=tmpf[:], in_=self.cin[src][:, :]), writes=[btmp])
            P.op("dve", lambda e, dst=dst: e.tensor_copy(out=dst[:], in_=tmpf[:]), reads=[btmp], writes=[bC])
        ptab = ar.alloc([128, 128], F32)
        bpt = Buf()
        P.op("dve", lambda e: e.memset(ptab[:], 0.0), writes=[bpt])
        rows = []
        r = 0
        self.col = {}

        def addrows(name, ap2d, n):
            nonlocal r
            self.col[name] = r
            P.dma("sp", lambda e, r=r: e.dma_start(out=ptab[r:r + n, :], in_=ap2d), writes=[bpt])
            r += n

        addrows("c", self.c.rearrange("s (k p) -> (s k) p", p=128), NS * 8)
        addrows("n1", self.norm1_g.rearrange("l (k p) -> (l k) p", p=128), DEPTH * 8)
        addrows("n2", self.norm2_g.rearrange("l (k p) -> (l k) p", p=128), DEPTH * 8)
        addrows("qg", self.q_norm_g.rearrange("l (j a) e -> (l j) (a e)", a=2), DEPTH * 10)
        addrows("kg", self.k_norm_g.rearrange("l (j a) e -> (l j) (a e)", a=2), DEPTH * 10)
        addrows("bg", self.b_gate.rearrange("l (k p) -> (l k) p", p=128), DEPTH * 16)
        assert r <= 128
        pt = self.ps[0]
        P.op("pe", lambda e: e.transpose(pt[:, 0:128], ptab[:], self.ident[:]), reads=[bpt, bC], writes=[self.psb[0]])
        P.op("dve", lambda e: e.tensor_copy(out=self.tab[:], in_=pt[:, 0:128]), reads=[self.psb[0]], writes=[bC])
        cq = self.col["qg"]
        P.op("dve", lambda e: e.tensor_scalar(out=self.qg8[:], in0=self.tab[:, cq:cq + DEPTH * 10], scalar1=0.125, scalar2=None, op0=ALU.mult), reads=[bC], writes=[bC])
        bmT = ar.alloc([128, 96], F32)
        ptab2 = ar.alloc([128, 128], F32)
        bpt2 = Buf()
        P.dma("sp", lambda e: e.dma_start(out=ptab2[0:96, :], in_=self.b_mod.rearrange("l (k p) -> (l k) p", p=128)), writes=[bpt2])
        P.op("pe", lambda e: e.transpose(pt[:, 128:224], ptab2[0:96, :], self.ident[0:96, 0:96]), reads=[bpt2, bC], writes=[self.psb[0]])
        bbm = Buf()
        P.op("dve", lambda e: e.tensor_copy(out=bmT[:], in_=pt[:, 128:224]), reads=[self.psb[0]], writes=[bbm])
        siluT = ar.alloc([128, NS * 8], F32)
        bsl = Buf()
        cc = self.col["c"]
        P.op("act", lambda e: e.activation(out=siluT[:], in_=self.tab[:, cc:cc + NS * 8], func=AF.Silu), reads=[bC], writes=[bsl])
        wm = [ar.alloc([128, 8, 512], F32) for _ in range(2)]
        bwm = [Buf(), Buf()]
        bmod = Buf("mod")
        self.bmod = bmod
        it = 0
        for l in range(self.depth):
            for blk in range(12):
                wt, bw = wm[it % 2], bwm[it % 2]
                it += 1
                P.dma("sp", lambda e, wt=wt, l=l, blk=blk: e.dma_start(out=wt[:], in_=self.w_mod[l].rearrange("(k p) n -> p k n", p=128)[:, :, blk * 512:(blk + 1) * 512]), writes=[bw])
                for c4 in range(4):
                    ccol = blk * 4 + c4
                    pb = 1 + (ccol % 2)
                    pst = self.ps[pb]
                    for kc in range(8):
                        P.op("pe", lambda e, wt=wt, kc=kc, c4=c4, pst=pst: e.matmul(pst[:, 0:NS], lhsT=wt[:, kc, c4 * 128:(c4 + 1) * 128], rhs=siluT[:].rearrange("p (s k) -> p k s", k=8)[:, kc, :], start=(kc == 0), stop=(kc == 7)), reads=[bw, bsl], writes=[self.psb[pb]])
                    o0 = (l * 48 + ccol) * NS
                    P.op("dve", lambda e, pst=pst, o0=o0, l=l, ccol=ccol: e.tensor_scalar(out=self.modT[:, o0:o0 + NS], in0=pst[:, 0:NS], scalar1=bmT[:, l * 48 + ccol:l * 48 + ccol + 1], scalar2=None, op0=ALU.add), reads=[self.psb[pb], bbm], writes=[bmod])
        for l in range(self.depth):
            for wh in range(2):
                for kc in range(8):
                    mi = (l * 48 + (1 if wh == 0 else 4) * 8 + kc) * NS
                    gi = ((l * 2 + wh) * 8 + kc) * NS
                    ncol = self.col["n1" if wh == 0 else "n2"] + l * 8 + kc
                    P.op("dve", lambda e, mi=mi, gi=gi, ncol=ncol: e.tensor_scalar(out=self.G[:, gi:gi + NS], in0=self.modT[:, mi:mi + NS], scalar1=1.0, scalar2=self.tab[:, ncol:ncol + 1], op0=ALU.add, op1=ALU.mult), reads=[bmod, bC], writes=[bmod])
        self.bwq = {}
        for l in range(self.depth):
            for n in ("w_in", "w_gate", "w_up_a", "w_up_b", "w_o", "w_ff1", "w_ff2"):
                K, N, KC, NC = WDEF[n]
                for b in range(nblk(n)):
                    ncol = min(NC, N - b * NC)
                    bb = Buf()
                    self.bwq[(n, l, b)] = bb
                    src = self.wsrc[n][l].rearrange("(k p) n -> p k n", p=128)[:, :, b * NC:b * NC + ncol]
                    dst = self.wq[n][l, b][:, 0:KC * ncol].rearrange("p (k n) -> p k n", k=KC)
                    P.dma("pool", lambda e, src=src, dst=dst: e.dma_start(out=dst, in_=src), writes=[bb])
        taug = ar.alloc([33, 12], F32)
        btg = Buf()
        P.op("dve", lambda e: e.memset(taug[32:33, :], NEGV), writes=[btg])
        P.dma("sp", lambda e: e.dma_start(out=taug[0:32, :], in_=self.rel_bias[:, :]), writes=[btg])
        oh = ar.alloc([33, 3, 384], F32)
        boh = Buf()
        P.dma("sp", lambda e: e.dma_start(out=oh[:], in_=self.cin["ohaug"].rearrange("g b w -> b g w")), writes=[boh])
        rst = ar.alloc([4, 3, 384], BF16)
        brst = Buf()
        for g in range(3):
            pst = self.ps[3]
            P.op("pe", lambda e, g=g, pst=pst: e.matmul(pst[0:4, 0:384], lhsT=taug[:, 4 * g:4 * g + 4], rhs=oh[:, g, :], start=True, stop=True), reads=[btg, boh], writes=[self.psb[3]])
            P.op("dve", lambda e, g=g, pst=pst: e.tensor_copy(out=rst[:, g, :], in_=pst[0:4, 0:384]), reads=[self.psb[3]], writes=[brst])
        self.brf = Buf("rfull")
        P.dma("pool", lambda e: e.dma_start(out=self.rfull.rearrange("(g h) w -> h g w", h=4), in_=rst[:]), reads=[brst], writes=[self.brf])
        self.bpnb = Buf("pnb")
        sel = ar.alloc([31, 128], F32)
        bsel = Buf()
        P.dma("sp", lambda e: e.dma_start(out=sel[:], in_=self.cin["sel"][:, :]), writes=[bsel])
        for l in range(self.depth):
            rc = ar.alloc([120, 31], F32)
            brc = Buf()
            P.dma("sp", lambda e, l=l, rc=rc: e.dma_start(out=rc[:], in_=self.rpb[l].rearrange("h r c -> (h r) c")), writes=[brc])
            pst = self.ps[4]
            P.op("pe", lambda e, rc=rc, pst=pst: e.transpose(pst[0:31, 0:120], rc[:], self.ident[0:120, 0:120]), reads=[brc, bC], writes=[self.psb[4]])
            xr = ar.alloc([31, 120], F32)
            bxr = Buf()
            P.op("dve", lambda e, xr=xr, pst=pst: e.tensor_copy(out=xr[:], in_=pst[0:31, 0:120]), reads=[self.psb[4]], writes=[bxr])
            P.op("pe", lambda e, xr=xr, pst=pst: e.matmul(pst[0:120, 128:256], lhsT=xr[:], rhs=sel[:], start=True, stop=True), reads=[bxr, bsel], writes=[self.psb[4]])
            pn = ar.alloc([120, 128], BF16)
            bpn = Buf()
            P.op("dve", lambda e, pn=pn, pst=pst: e.tensor_copy(out=pn[:], in_=pst[0:120, 128:256]), reads=[self.psb[4]], writes=[bpn])
            P.dma("pool", lambda e, l=l, pn=pn: e.dma_start(out=self.pnb[l], in_=pn[:]), reads=[bpn], writes=[self.bpnb])
        P.dma("pool", lambda e: e.dma_start(out=self.cmask[:], in_=self.cin["cmask"][:, :]), writes=[bC])
        P.barrier()
        self.base_mark = mk
        self.bxs = [[Buf() for _ in range(self.NT // T)] for _ in range(1)][0]
        self.bqk = Buf("qk")
        self.bv = Buf("v")
        self.bos = Buf("os")

    def dense_pass(self, lp):
        nc, P, ar = self.nc, self.P, self.ar
        NS, NT = self.NS, self.NT
        ar.reset(self.base_mark)
        doC = lp > 0
        doA = lp < self.depth
        lc = lp - 1
        la = lp
        NSUB = T // 128
        xTs = [ar.alloc([128, 8, T], F32) for _ in range(2)]
        hTs = [ar.alloc([128, 8, T], BF16) for _ in range(2)]
        rstd = ar.alloc([128, T], F32)
        big = ar.alloc([128, 32, T], BF16)
        iost = nc.alloc_sbuf_tensor_at("iost%d" % lp, [128, 4, 1024], F32, offset=_off(big) + 8 * T * 4)
        tmps = [ar.alloc([128, T], F32) for _ in range(3)]
        gT = ar.alloc([128, 16, T], BF16)
        sq = ar.alloc([128, 8, T], BF16)
        oraw = [ar.alloc([128, 1300], F32) for _ in range(2)]
        otoks = [ar.alloc([128, 768], BF16) for _ in range(NSUB)]
        osum = ar.alloc([128, 260], F32)
        rden = ar.alloc([128, 12], F32)
        oT = ar.alloc([128, 6, T], BF16)
        t12 = [ar.alloc([128, T], BF16) for _ in range(4)]
        mixT = ar.alloc([128, 8, T], BF16)
        rl = [ar.alloc([128, T], BF16) for _ in range(2)]
        sq2 = [ar.alloc([128, T], BF16) for _ in range(2)]
        rs2 = [ar.alloc([128, T], F32) for _ in range(2)]
        qst = [ar.alloc([128, T], BF16) for _ in range(3)]
        vst = [ar.alloc([128, 512], BF16) for _ in range(3)]
        NW = 5
        wsl = [ar.alloc([128, 4096], BF16) for _ in range(NW)]
        bxTs = [[Buf() for _ in range(8)] for _ in range(2)]
        bhTs = [[Buf() for _ in range(8)] for _ in range(2)]
        brstd = Buf()
        bbig = [Buf() for _ in range(32)]
        btmps = [Buf() for _ in range(3)]
        bgT = [Buf() for _ in range(16)]
        bsq = [Buf() for _ in range(8)]
        boraw = [Buf(), Buf()]
        botoks = [Buf() for _ in range(NSUB)]
        bosum, brden = Buf(), Buf()
        boT = Buf()
        bt12 = [Buf() for _ in range(4)]
        bmix = [Buf() for _ in range(8)]
        brl = [Buf(), Buf()]
        bsq2 = [Buf(), Buf()]
        brs2 = [Buf(), Buf()]
        bqst = [Buf() for _ in range(3)]
        bvst = [Buf(), Buf(), Buf()]
        bws = [Buf() for _ in range(NW)]
        bC, bmod = self.bC, self.bmod
        tab, modT, G = self.tab, self.modT, self.G
        ntiles = NT // T
        R = {}

        def mkrings():
            R["acc"] = Ring([(self.ps[i], self.psb[i]) for i in range(5)])
            R["stt"] = Ring([(self.ps[i], self.psb[i]) for i in (5, 6)])
            R["t12"] = Ring(list(zip(t12, bt12)))
            R["rl"] = Ring(list(zip(rl, brl)))
            R["sq2"] = Ring(list(zip(sq2, bsq2)))
            R["rs2"] = Ring(list(zip(rs2, brs2)))
            R["qst"] = Ring(list(zip(qst, bqst)))
            R["oraw"] = Ring(list(zip(oraw, boraw)))
            R["tmp"] = Ring(list(zip(tmps, btmps)))
            R["vst"] = Ring(list(zip(vst, bvst)))

        order = []
        wstate = {"emit": 0, "use": 0, "dry": True}

        def wget(n, l, b):
            K_, N_, KC, NC = WDEF[n]
            ncol = min(NC, N_ - b * NC)
            if wstate["dry"]:
                order.append((n, l, b))
                return None, None, ncol
            while wstate["emit"] < len(order) and wstate["emit"] < wstate["use"] + NW - 2:
                i = wstate["emit"]
                n2, l2, b2 = order[i]
                sl, bs = wsl[i % NW], bws[i % NW]
                P.dma("sp", lambda e, sl=sl, n2=n2, l2=l2, b2=b2: e.dma_start(out=sl[:], in_=self.wq[n2][l2, b2]), reads=[self.bwq[(n2, l2, b2)]], writes=[bs])
                wstate["emit"] += 1
            i = wstate["use"]
            wstate["use"] += 1
            assert order[i] == (n, l, b), (order[i], (n, l, b))
            return wsl[i % NW][:, 0:KC * ncol].rearrange("p (k n) -> p k n", k=KC), bws[i % NW], ncol

        def op(*a, **k):
            if not wstate["dry"]:
                P.op(*a, **k)

        def dma(*a, **k):
            if not wstate["dry"]:
                P.dma(*a, **k)

        def mcol(l, j, kc, s):
            o = (l * 48 + j * 8 + kc) * NS + s
            return modT[:, o:o + 1]

        def gcol(l, wh, kc, s):
            o = ((l * 2 + wh) * 8 + kc) * NS + s
            return G[:, o:o + 1]

        def tinfo(ti):
            t0 = ti * T
            s = max(i for i in range(NS) if self.s0[i] <= t0)
            return t0, s, self.s0[s], self.seqs[s]

        def norm_sq(xi, kc):
            xT, bxT = xTs[xi], bxTs[xi]
            op("act", lambda e, kc=kc: e.activation(out=sq[:, kc, :], in_=xT[:, kc, :], func=AF.Square), reads=[bxT[kc]], writes=[bsq[kc]])

        def norm_fin(l, wh, s, xi, hi):
            xT, bxT, hT, bhT = xTs[xi], bxTs[xi], hTs[hi], bhTs[hi]
            pst, bp = R["stt"].next()
            for kc in range(8):
                op("pe", lambda e, kc=kc, pst=pst: e.matmul(pst[:, 0:T], lhsT=self.mones[:], rhs=sq[:, kc, :], start=(kc == 0), stop=(kc == 7)), reads=[bsq[kc], bC], writes=[bp])
            op("act", lambda e, pst=pst: e.activation(out=rstd[:], in_=pst[:, 0:T], func=AF.Ln, bias=EPS, scale=1.0), reads=[bp], writes=[brstd])
            op("act", lambda e: e.activation(out=rstd[:], in_=rstd[:], func=AF.Exp, scale=-0.5), reads=[brstd], writes=[brstd])
            shj = 0 if wh == 0 else 3
            for kc in range(8):
                tm, btm = R["tmp"].next()
                op("dve", lambda e, kc=kc, tm=tm: e.scalar_tensor_tensor(out=tm[:], in0=xT[:, kc, :], scalar=gcol(l, wh, kc, s), in1=rstd[:], op0=ALU.mult, op1=ALU.mult), reads=[bxT[kc], brstd, bmod], writes=[btm])
                op("act", lambda e, kc=kc, tm=tm: e.activation(out=hT[:, kc, :], in_=tm[:], func=AF.Identity, bias=mcol(l, shj, kc, s), scale=1.0), reads=[btm, bmod], writes=[bhT[kc]])

        def norm(l, wh, s, xi, hi):
            for kc in range(8):
                norm_sq(xi, kc)
            norm_fin(l, wh, s, xi, hi)

        def load(ti):
            t0, s, s0, Ls = tinfo(ti)
            xi = ti % 2
            xT, bxT = xTs[xi], bxTs[xi]
            if lp == 0:
                for sub in range(NSUB):
                    dma("sp", lambda e, sub=sub: e.dma_start(out=iost[:, sub, :], in_=self.x[t0 + sub * 128:t0 + (sub + 1) * 128, :]), writes=bbig[16 + 4 * sub:20 + 4 * sub])
                for kc in range(8):
                    pst, bp = R["acc"].next()
                    for sub in range(NSUB):
                        op("pe", lambda e, kc=kc, sub=sub, pst=pst: e.transpose(pst[:, sub * 128:(sub + 1) * 128], iost[:, sub, kc * 128:(kc + 1) * 128], self.ident[:]), reads=bbig[16 + 4 * sub:20 + 4 * sub] + [bC], writes=[bp])
                    op("dve", lambda e, kc=kc, pst=pst: e.tensor_copy(out=xT[:, kc, :], in_=pst[:, 0:T]), reads=[bp], writes=[bxT[kc]])
            else:
                dma("sp", lambda e: e.dma_start(out=xT[:], in_=self.xs[:, :, t0:t0 + T].rearrange("k p t -> p k t")), reads=[self.bxs[ti]], writes=bxT)

        def omerge(ti):
            t0, s, s0, Ls = tinfo(ti)
            for sub in range(NSUB):
                otok, botok = otoks[sub], botoks[sub]
                orw, bor = R["oraw"].next()
                dma("sp", lambda e, orw=orw, sub=sub: e.dma_start(out=orw[:], in_=self.osr[t0 + sub * 128:t0 + (sub + 1) * 128, :]), reads=[self.bos], writes=[bor])
                op("pool", lambda e, orw=orw: e.tensor_tensor(out=osum[:], in0=orw[:, 0:260], in1=orw[:, 260:520], op=ALU.add), reads=[bor], writes=[bosum])
                op("pool", lambda e, orw=orw: e.tensor_tensor(out=osum[:], in0=osum[:], in1=orw[:, 520:780], op=ALU.add), reads=[bor, bosum], writes=[bosum])
                op("dve", lambda e: e.reciprocal(out=rden[:, 0:4], in_=osum[:].rearrange("p (h e) -> p h e", e=65)[:, :, 64]), reads=[bosum], writes=[brden])
                op("dve", lambda e, orw=orw: e.reciprocal(out=rden[:, 4:12], in_=orw[:, 780:1300].rearrange("p (h e) -> p h e", e=65)[:, :, 64]), reads=[bor], writes=[brden])
                op("dve", lambda e, otok=otok: e.tensor_tensor(out=otok[:, 0:256].rearrange("p (h e) -> p h e", e=64), in0=osum[:].rearrange("p (h e) -> p h e", e=65)[:, :, 0:64], in1=rden[:, 0:4].unsqueeze(2).to_broadcast([128, 4, 64]), op=ALU.mult), reads=[bosum, brden], writes=[botok])
                op("dve", lambda e, orw=orw, otok=otok: e.tensor_tensor(out=otok[:, 256:768].rearrange("p (h e) -> p h e", e=64), in0=orw[:, 780:1300].rearrange("p (h e) -> p h e", e=65)[:, :, 0:64], in1=rden[:, 4:12].unsqueeze(2).to_broadcast([128, 8, 64]), op=ALU.mult), reads=[bor, brden], writes=[botok])

        def gate(ti):
            t0, s, s0, Ls = tinfo(ti)
            l = lc
            hT, bhT = hTs[0], bhTs[0]
            for b in range(4):
                wt, bw, _ = wget("w_gate", l, b)
                for c4 in range(4):
                    cc = b * 4 + c4
                    pst, bp = R["acc"].next()
                    for kc in range(8):
                        op("pe", lambda e, wt=wt, kc=kc, c4=c4, pst=pst: e.matmul(pst[:, 0:T], lhsT=wt[:, kc, c4 * 128:(c4 + 1) * 128], rhs=hT[:, kc, :], start=(kc == 0), stop=(kc == 7)), reads=[bw, bhT[kc]], writes=[bp])
                    bcol = self.col["bg"] + l * 16 + cc
                    op("act", lambda e, cc=cc, pst=pst, bcol=bcol: e.activation(out=gT[:, cc, :], in_=pst[:, 0:T], func=AF.Sigmoid, bias=tab[:, bcol:bcol + 1], scale=1.0), reads=[bp, bC], writes=[bgT[cc]])

        def c_rest(ti, after_T=None):
            t0, s, s0, Ls = tinfo(ti)
            l = lc
            xi = ti % 2
            xT, bxT = xTs[xi], bxTs[xi]
            hT, bhT = hTs[0], bhTs[0]
            for sub in range(NSUB):
                otok, botok = otoks[sub], botoks[sub]
                if USE_PS_BITCAST:
                    pstf, bpT = R["acc"].next()
                    pT = pstf[:, 0:384].bitcast(BF16)
                else:
                    pT, bpT = self.ps16[:, 0:768], self.psb[7]
                for kc in range(6):
                    op("pe", lambda e, kc=kc, otok=otok, pT=pT: e.transpose(pT[:, kc * 128:(kc + 1) * 128], otok[:, kc * 128:(kc + 1) * 128], self.identb[:]), reads=[botok, bC], writes=[bpT])
                op("act", lambda e, sub=sub, pT=pT: e.copy(out=oT[:, :, sub * 128:(sub + 1) * 128], in_=pT.rearrange("p (k t) -> p k t", k=6)), reads=[bpT], writes=[boT])
            wa, bwa, _ = wget("w_up_a", l, 0)
            wb, bwb, _ = wget("w_up_b", l, 0)
            for cc in range(8):
                pa, bpa = R["acc"].next()
                for kc in range(2):
                    op("pe", lambda e, kc=kc, cc=cc, pa=pa: e.matmul(pa[:, 0:T], lhsT=wa[:, kc, cc * 128:(cc + 1) * 128], rhs=oT[:, kc, :], start=(kc == 0), stop=(kc == 1)), reads=[bwa, boT], writes=[bpa])
                pb_, bpb = R["acc"].next()
                for kc in range(4):
                    op("pe", lambda e, kc=kc, cc=cc, pb_=pb_: e.matmul(pb_[:, 0:T], lhsT=wb[:, kc, cc * 128:(cc + 1) * 128], rhs=oT[:, 2 + kc, :], start=(kc == 0), stop=(kc == 3)), reads=[bwb, boT], writes=[bpb])
                t1, bt1 = R["t12"].next()
                t2, bt2 = R["t12"].next()
                op("dve", lambda e, cc=cc, pa=pa, t1=t1: e.tensor_tensor(out=t1[:], in0=pa[:, 0:T], in1=gT[:, cc, :], op=ALU.mult), reads=[bpa, bgT[cc]], writes=[bt1])
                op("dve", lambda e, cc=cc, pb_=pb_, t2=t2: e.tensor_tensor(out=t2[:], in0=pb_[:, 0:T], in1=gT[:, 8 + cc, :], op=ALU.mult), reads=[bpb, bgT[8 + cc]], writes=[bt2])
                op("pool", lambda e, cc=cc, t1=t1, t2=t2: e.tensor_tensor(out=mixT[:, cc, :], in0=t1[:], in1=t2[:], op=ALU.add), reads=[bt1, bt2], writes=[bmix[cc]])
            for b in range(2):
                wt, bw, _ = wget("w_o", l, b)
                for c4 in range(4):
                    cc = b * 4 + c4
                    pst, bp = R["acc"].next()
                    for kc in range(8):
                        op("pe", lambda e, wt=wt, kc=kc, c4=c4, pst=pst: e.matmul(pst[:, 0:T], lhsT=wt[:, kc, c4 * 128:(c4 + 1) * 128], rhs=mixT[:, kc, :], start=(kc == 0), stop=(kc == 7)), reads=[bw, bmix[kc]], writes=[bp])
                    op("dve", lambda e, cc=cc, pst=pst: e.scalar_tensor_tensor(out=xT[:, cc, :], in0=pst[:, 0:T], scalar=mcol(l, 2, cc, s), in1=xT[:, cc, :], op0=ALU.mult, op1=ALU.add), reads=[bp, bxT[cc], bmod], writes=[bxT[cc]])
                    norm_sq(xi, cc)
            norm_fin(l, 1, s, xi, 0)
            if after_T is not None:
                after_T()
            for b in range(8):
                wt, bw, _ = wget("w_ff1", l, b)
                for c4 in range(4):
                    cc = b * 4 + c4
                    pst, bp = R["acc"].next()
                    for kc in range(8):
                        op("pe", lambda e, wt=wt, kc=kc, c4=c4, pst=pst: e.matmul(pst[:, 0:T], lhsT=wt[:, kc, c4 * 128:(c4 + 1) * 128], rhs=hT[:, kc, :], start=(kc == 0), stop=(kc == 7)), reads=[bw, bhT[kc]], writes=[bp])
                    r_, br_ = R["rl"].next()
                    op("act", lambda e, pst=pst, r_=r_: e.activation(out=r_[:], in_=pst[:, 0:T], func=AF.Relu), reads=[bp], writes=[br_])
                    op("pool", lambda e, cc=cc, r_=r_: e.tensor_tensor(out=big[:, cc, :], in0=r_[:], in1=r_[:], op=ALU.mult), reads=[br_], writes=[bbig[cc]])
            if ti + 1 < ntiles:
                t0n, sn, _, _ = tinfo(ti + 1)
                norm(lc, 0, sn, (ti + 1) % 2, 0)
            for cc in range(8):
                wt, bw, _ = wget("w_ff2", l, cc)
                pst, bp = R["acc"].next()
                for kc in range(32):
                    op("pe", lambda e, wt=wt, kc=kc, pst=pst: e.matmul(pst[:, 0:T], lhsT=wt[:, kc, :], rhs=big[:, kc, :], start=(kc == 0), stop=(kc == 31)), reads=[bw, bbig[kc]], writes=[bp])
                op("dve", lambda e, cc=cc, pst=pst: e.scalar_tensor_tensor(out=xT[:, cc, :], in0=pst[:, 0:T], scalar=mcol(l, 5, cc, s), in1=xT[:, cc, :], op0=ALU.mult, op1=ALU.add), reads=[bp, bxT[cc], bmod], writes=[bxT[cc]])
                if doA:
                    norm_sq(xi, cc)
            if not doA:
                for sub in range(NSUB):
                    for hf in range(2):
                        pst, bp = R["acc"].next()
                        for k4 in range(4):
                            kc = hf * 4 + k4
                            op("pe", lambda e, kc=kc, k4=k4, sub=sub, pst=pst: e.transpose(pst[:, k4 * 128:(k4 + 1) * 128], xT[:, kc, sub * 128:(sub + 1) * 128], self.ident[:]), reads=[bxT[kc], bC], writes=[bp])
                        wr = [bbig[16 + sub * 4 + hf * 2], bbig[16 + sub * 4 + hf * 2 + 1]]
                        if hf:
                            op("dve", lambda e, hf=hf, sub=sub, pst=pst: e.tensor_copy(out=iost[:, sub, hf * 512:(hf + 1) * 512], in_=pst[:, 0:512]), reads=[bp], writes=wr)
                        else:
                            op("act", lambda e, hf=hf, sub=sub, pst=pst: e.copy(out=iost[:, sub, hf * 512:(hf + 1) * 512], in_=pst[:, 0:512]), reads=[bp], writes=wr)
                    dma("pool", lambda e, sub=sub: e.dma_start(out=self.y[t0 + sub * 128:t0 + (sub + 1) * 128, :], in_=iost[:, sub, :]), reads=bbig[16 + sub * 4:16 + sub * 4 + 4], writes=[])

        def store_x(ti):
            t0, s, s0, Ls = tinfo(ti)
            xi = ti % 2
            dma("pool", lambda e: e.dma_start(out=self.xs[:, :, t0:t0 + T].rearrange("k p t -> p k t"), in_=xTs[xi][:]), reads=bxTs[xi], writes=[self.bxs[ti]])

        def a_qk(ti, hi, j0, j1, held):
            t0, s, s0, Ls = tinfo(ti)
            l = la
            hT, bhT = hTs[hi], bhTs[hi]

            def finish(ctx):
                j, pst, bp, s2, bs2 = ctx
                jj = j % 10
                d = DIL[jj // 2] if jj < 6 else 1
                pm, bpm = R["stt"].next()
                op("pe", lambda e, pm=pm, s2=s2: e.matmul(pm[:, 0:T], lhsT=self.bones[:], rhs=s2[:], start=True, stop=True), reads=[bs2, bC], writes=[bpm])
                r2, br2 = R["rs2"].next()
                op("act", lambda e, pm=pm, r2=r2: e.activation(out=r2[:], in_=pm[:, 0:T], func=AF.Ln, bias=EPS, scale=1.0), reads=[bpm], writes=[br2])
                op("act", lambda e, r2=r2: e.activation(out=r2[:], in_=r2[:], func=AF.Exp, scale=-0.5), reads=[br2], writes=[br2])
                qo, bqo = R["qst"].next()
                if j < 10:
                    gc = self.qg8[:, l * 10 + jj:l * 10 + jj + 1]
                else:
                    kcol = self.col["kg"] + l * 10 + jj
                    gc = tab[:, kcol:kcol + 1]
                op("dve", lambda e, pst=pst, r2=r2, qo=qo, gc=gc, d=d: e.scalar_tensor_tensor(out=qo[:].rearrange("p (r m) -> p r m", r=d), in0=pst[:, 0:T].rearrange("p (m r) -> p r m", r=d), scalar=gc, in1=r2[:].rearrange("p (m r) -> p r m", r=d), op0=ALU.mult, op1=ALU.mult), reads=[bp, br2, bC], writes=[bqo])
                dst = (self.qs if j < 10 else self.ks)
                n = Ls // d
                col0 = s0 + (t0 - s0) // d
                dap = bass.AP(dst.tensor, jj * 128 * NT + col0, [[NT, 128], [n, d], [1, T // d]])
                dma("pool", lambda e, dap=dap, qo=qo, d=d: e.dma_start(out=dap, in_=qo[:].rearrange("p (r m) -> p r m", r=d)), reads=[bqo], writes=[self.bqk])

            for j in range(j0, j1):
                if j % 4 == 0:
                    held[0] = wget("w_in", l, j // 4)
                wt, bw, _ = held[0]
                c4 = j % 4
                jj = j % 10
                d = DIL[jj // 2] if jj < 6 else 1
                pst, bp = R["acc"].next()
                for kc in range(8):
                    op("pe", lambda e, wt=wt, kc=kc, c4=c4, pst=pst: e.matmul(pst[:, 0:T], lhsT=wt[:, kc, c4 * 128:(c4 + 1) * 128], rhs=hT[:, kc, :], start=(kc == 0), stop=(kc == 7)), reads=[bw, bhT[kc]], writes=[bp])
                s2, bs2 = R["sq2"].next()
                op("act", lambda e, pst=pst, s2=s2: e.activation(out=s2[:], in_=pst[:, 0:T], func=AF.Square), reads=[bp], writes=[bs2])
                if held[1] is not None:
                    finish(held[1])
                held[1] = (j, pst, bp, s2, bs2)
                while held[2] < -(-(j + 1) * 3 * NSUB // 20):
                    v_group(ti, hi, held)
            if j1 == 20:
                finish(held[1])
                held[1] = None

        def v_group(ti, hi, held):
            t0, s, s0, Ls = tinfo(ti)
            l = la
            hT, bhT = hTs[hi], bhTs[hi]
            gi = held[2]
            held[2] += 1
            vi, sub = gi // NSUB, gi % NSUB
            if sub == 0:
                held[3] = wget("w_in", l, 5 + vi)
            wt, bw, ncol = held[3]
            pst, bp = R["acc"].next()
            for kc in range(8):
                op("pe", lambda e, wt=wt, kc=kc, sub=sub, pst=pst, ncol=ncol: e.matmul(pst[:, 0:ncol], lhsT=hT[:, kc, sub * 128:(sub + 1) * 128], rhs=wt[:, kc, :], start=(kc == 0), stop=(kc == 7)), reads=[bw, bhT[kc]], writes=[bp])
            vs_, bvs_ = R["vst"].next()
            op("dve", lambda e, pst=pst, ncol=ncol, vs_=vs_: e.tensor_copy(out=vs_[:, 0:ncol], in_=pst[:, 0:ncol]), reads=[bp], writes=[bvs_])
            dma("pool", lambda e, sub=sub, vi=vi, ncol=ncol, vs_=vs_: e.dma_start(out=self.vs[t0 + sub * 128:t0 + (sub + 1) * 128, vi * 512:vi * 512 + ncol], in_=vs_[:, 0:ncol]), reads=[bvs_], writes=[self.bv])

        def emit_all():
            mkrings()
            if lp == 0:
                load(0)
                _, s_, _, _ = tinfo(0)
                norm(la, 0, s_, 0, 0)
                for ti in range(ntiles):
                    if ti + 1 < ntiles:
                        load(ti + 1)
                    store_x(ti)
                    held = [None, None, 0, None]
                    a_qk(ti, ti % 2, 0, 10, held)
                    if ti + 1 < ntiles:
                        _, sn, _, _ = tinfo(ti + 1)
                        norm(la, 0, sn, (ti + 1) % 2, (ti + 1) % 2)
                    a_qk(ti, ti % 2, 10, 20, held)
            else:
                load(0)
                omerge(0)
                _, s_, _, _ = tinfo(0)
                norm(lc, 0, s_, 0, 0)
                gate(0)
                for ti in range(ntiles):
                    if ti + 1 < ntiles:
                        load(ti + 1)
                    c_rest(ti, (lambda ti=ti: omerge(ti + 1)) if ti + 1 < ntiles else None)
                    if doA:
                        store_x(ti)
                        _, s_, _, _ = tinfo(ti)
                        norm_fin(la, 0, s_, ti % 2, 1)
                    if ti + 1 < ntiles:
                        gate(ti + 1)
                    if doA:
                        held = [None, None, 0, None]
                        a_qk(ti, 1, 0, 20, held)

        emit_all()
        wstate["dry"] = False
        emit_all()
        P.barrier()

    def attention(self, l):
        nc, P, ar = self.nc, self.P, self.ar
        NS, NT = self.NS, self.NT
        ar.reset(self.base_mark)
        bC = self.bC
        Tdil = ar.alloc([128, 12, 256], BF16)
        bTd = Buf()
        for half, off in ((0, 128), (1, 0)):
            src = bass.AP(self.rfull.tensor, off, [[1, 128], [384, 12], [1, 128]])
            P.dma("sp", lambda e, src=src, half=half: e.dma_start(out=Tdil[:, :, half * 128:(half + 1) * 128], in_=src), reads=[self.brf], writes=[bTd])
        Tnb = ar.alloc([128, 2, 8, 448], BF16)
        bTn = Buf()
        for par in range(2):
            for slot in range(7):
                e_ = (-6 if par == 0 else -7) + 2 * slot
                for ci in range(2):
                    src = bass.AP(self.pnb.tensor, l * 120 * 128 + (e_ + 8 - ci) * 128, [[1, 64], [15 * 128, 8], [1, 64]])
                    P.dma("sp", lambda e, src=src, par=par, slot=slot, ci=ci: e.dma_start(out=Tnb[ci * 64:(ci + 1) * 64, par, :, slot * 64:(slot + 1) * 64], in_=src), reads=[self.bpnb], writes=[bTn])
        P.op("dve", lambda e: e.tensor_tensor(out=Tnb[:].rearrange("p a h (s q) -> p (a h s) q", q=64), in0=Tnb[:].rearrange("p a h (s q) -> p (a h s) q", q=64), in1=self.cmask[:].unsqueeze(1).to_broadcast([128, 112, 64]), op=ALU.add), reads=[bTn, bC], writes=[bTn])
        EBd = ar.alloc([128, 12, 256], BF16)
        EBn = ar.alloc([128, 2, 8, 448], BF16)
        bEd, bEn = Buf(), Buf()
        for hh in range(12):
            pb = hh % 3
            P.op("pe", lambda e, hh=hh, pb=pb: e.matmul(self.ps[pb][:, 0:256], lhsT=self.J[:], rhs=Tdil[:, hh, :], start=True, stop=True), reads=[bC, bTd], writes=[self.psb[pb]])
            P.op("act", lambda e, hh=hh, pb=pb: e.activation(out=EBd[:, hh, :], in_=self.ps[pb][:, 0:256], func=AF.Exp), reads=[self.psb[pb]], writes=[bEd])
        for par in range(2):
            for hh in range(8):
                pb = hh % 3
                P.op("pe", lambda e, hh=hh, pb=pb, par=par: e.matmul(self.ps[pb][:, 0:448], lhsT=self.J[:], rhs=Tnb[:, par, hh, :], start=True, stop=True), reads=[bC, bTn], writes=[self.psb[pb]])
                P.op("act", lambda e, hh=hh, pb=pb, par=par: e.activation(out=EBn[:, par, hh, :], in_=self.ps[pb][:, 0:448], func=AF.Exp), reads=[self.psb[pb]], writes=[bEn])
        NCH = SEG // 128
        QT = [ar.alloc([128, 2, SEG], BF16) for _ in range(2)]
        KTW = 4096
        KT = [ar.alloc([128, 2, KTW], BF16) for _ in range(2)]
        VV = [ar.alloc([128, 40, 4, 65], BF16) for _ in range(2)]
        OST = [ar.alloc([128, NCH, 260], F32) for _ in range(2)]
        PT = [ar.alloc([128, 256], BF16) for _ in range(8)]
        bQT, bOST = [Buf(), Buf()], [Buf(), Buf()]
        bKT = [[Buf(), Buf()] for _ in range(2)]
        bVV = [[Buf() for _ in range(40)] for _ in range(2)]
        bPT = [Buf() for _ in range(8)]
        for i in range(2):
            P.op("pool", lambda e, i=i: e.memset(VV[i][:], 1.0), writes=bVV[i])
            P.op("pool", lambda e, i=i: e.memset(KT[i][:], 0.0), writes=bKT[i])
        stR = Ring([(self.ps[i], self.psb[i]) for i in (0, 1, 2, 5, 6)])
        poR = Ring([(self.ps[i], self.psb[i]) for i in (3, 4)])
        ptR = Ring(list(zip(PT, bPT)))
        item = [0]
        pending = []

        def run_item(kind, s, g, rs, P0, S):
            ib = item[0] % 2
            item[0] += 1
            qt, kt, vv, ost = QT[ib], KT[ib], VV[ib], OST[ib]
            bq, bk, bv_, bo = bQT[ib], bKT[ib], bVV[ib], bOST[ib]
            s0, Ls = self.s0[s], self.seqs[s]
            nsub = len(rs)
            nC = S // 128
            if kind == "d":
                d = DIL[g]
                n = Ls // d
                jq = 2 * g
                vcol = 256 * g
                ocol = 260 * g
                cb = s0 + rs[0] * n
                klo, khi = max(P0 - 64, 0), min(P0 + S + 64, n)
                koff = klo - (P0 - 64)
                KW_ = S + 128
                VT_ = nC + 1
                assert nsub == 1 or (P0 == 0 and S == n)
                assert nsub * S <= SEG and nsub * KW_ <= KTW and nsub * VT_ <= 40 and nsub * nC <= SEG // 128
            else:
                d = 1
                n = Ls
                jq = 6 + 2 * g
                vcol = 768 + 256 * g
                ocol = 780 + 260 * g
                cb = s0
                klo, khi = max(P0 - 256, 0), min(P0 + S + 256, n)
                koff = klo - (P0 - 256)
            P.dma("sp", lambda e: e.dma_start(out=qt[:, :, 0:nsub * S], in_=self.qs[jq:jq + 2, :, cb + P0:cb + P0 + nsub * S].rearrange("j p t -> p j t")), reads=[self.bqk], writes=[bq])
            if kind == "d":
                if nsub > 1 or koff > 0 or khi < P0 + S + 64:
                    P.op("pool", lambda e: e.memset(kt[:, :, 0:nsub * KW_], 0.0), writes=bk)
                for pr in range(2):
                    dstv = kt[:, pr, 0:nsub * KW_].rearrange("p (i c) -> p i c", c=KW_)[:, :, koff:koff + khi - klo]
                    srcv = bass.AP(self.ks.tensor, (jq + pr) * 128 * NT + cb + klo, [[NT, 128], [n, nsub], [1, khi - klo]])
                    P.dma("sp", lambda e, dstv=dstv, srcv=srcv: e.dma_start(out=dstv, in_=srcv), reads=[self.bqk], writes=[bk[pr]])
                for i, r in enumerate(rs):
                    for t_ in range(VT_):
                        p_lo = P0 + 128 * t_ - 64
                        a, b_ = 0, 128
                        if p_lo < 0:
                            a = 64
                        if p_lo + 128 > n:
                            b_ = 64
                        if a > 0 or b_ < 128:
                            za, zb = (0, 64) if a > 0 else (64, 128)
                            P.op("pool", lambda e, i=i, t_=t_, za=za, zb=zb: e.memset(vv[za:zb, i * VT_ + t_, :, :], 0.0), writes=[bv_[i * VT_ + t_]])
                        src = bass.AP(self.vs.tensor, (s0 + r + d * (p_lo + a)) * 1280 + vcol, [[d * 1280, b_ - a], [64, 4], [1, 64]])
                        P.dma("sp", lambda e, i=i, t_=t_, a=a, b_=b_, src=src: e.dma_start(out=vv[a:b_, i * VT_ + t_, :, 0:64], in_=src), reads=[self.bv], writes=[bv_[i * VT_ + t_]])
            else:
                P.dma("sp", lambda e: e.dma_start(out=kt[:, :, koff:koff + khi - klo], in_=self.ks[jq:jq + 2, :, cb + klo:cb + khi].rearrange("j p t -> p j t")), reads=[self.bqk], writes=bk)
                R0 = P0 // 64
                rows = Ls // 64
                rbase = max(R0 - 4, 0)
                rhi = min(R0 + S // 64 + 4, rows) - 2
                for rho in range(rbase, rhi + 1):
                    src = bass.AP(self.vs.tensor, (s0 + rho * 64) * 1280 + vcol, [[1280, 128], [64, 4], [1, 64]])
                    P.dma("sp", lambda e, rho=rho, src=src: e.dma_start(out=vv[:, rho - rbase, :, 0:64], in_=src), reads=[self.bv], writes=[bv_[rho - rbase]])
            units = []
            if kind == "d":
                for i in range(nsub):
                    for c in range(nC):
                        for h in range(4):
                            units.append((i, c, h, 0))
            else:
                for c in range(nC):
                    for hpair in range(2):
                        for half in range(2):
                            units.append((0, c, 2 * hpair, half))
                            units.append((0, c, 2 * hpair + 1, half))
            pobox = {}

            def emitS2(ua, ub):
                res = []
                ctx = []
                for u in (ua, ub):
                    i, c, h, half = u
                    pair, hp = h // 2, (h % 2) * 64
                    stp, bst = stR.next()
                    pt_, bpt = ptR.next()
                    ctx.append((i, c, h, half, pair, hp, stp, bst, pt_, bpt))
                if kind == "d":
                    for ab in range(2):
                        for (i, c, h, half, pair, hp, stp, bst, pt_, bpt) in ctx:
                            qc = i * S + 128 * c
                            kc_ = i * KW_ + 128 * c + 128 * ab
                            P.op("pe", lambda e, stp=stp, pair=pair, hp=hp, qc=qc, kc_=kc_, ab=ab: e.matmul(stp[:, 128 * ab:128 * ab + 128], lhsT=kt[hp:hp + 64, pair, kc_:kc_ + 128], rhs=qt[hp:hp + 64, pair, qc:qc + 128], start=(ab == 0), stop=(ab == 1)), reads=[bk[pair], bq], writes=[bst])
                    infos = [None, None]
                    ebs = [(EBd[:, 4 * g + cx[2], :], bEd) for cx in ctx]
                else:
                    infos = []
                    ebs = []
                    geo = []
                    for (i, c, h, half, pair, hp, stp, bst, pt_, bpt) in ctx:
                        i_ = R0 + 2 * c + half
                        rstart = min(max(i_ - 4, 0), rows - 8)
                        dr0 = rstart - i_
                        par = dr0 % 2
                        emin = -6 if par == 0 else -7
                        toff = 64 * ((dr0 - emin) // 2)
                        ebs.append((EBn[:, par, 4 * g + h, toff:toff + 256], bEn))
                        infos.append(rstart)
                        geo.append((rstart, (2 * c + half) * 64))
                    for p4 in range(4):
                        for cx, (rstart, qcol) in zip(ctx, geo):
                            (i, c, h, half, pair, hp, stp, bst, pt_, bpt) = cx
                            rho = rstart + 2 * p4
                            kcol = 64 * rho - (P0 - 256)
                            P.op("pe", lambda e, stp=stp, pair=pair, hp=hp, p4=p4, kcol=kcol, qcol=qcol: e.matmul(stp[:, 64 * p4:64 * p4 + 64], lhsT=kt[hp:hp + 64, pair, kcol:kcol + 128], rhs=qt[hp:hp + 64, pair, qcol:qcol + 64], start=(p4 == 0), stop=(p4 == 3)), reads=[bk[pair], bq], writes=[bst])
                for cx, info, (eb, beb) in zip(ctx, infos, ebs):
                    (i, c, h, half, pair, hp, stp, bst, pt_, bpt) = cx
                    P.op("act", lambda e, stp=stp, pt_=pt_: e.activation(out=pt_[:], in_=stp[:, 0:256], func=AF.Exp), reads=[bst], writes=[bpt])
                    P.op("dve", lambda e, pt_=pt_, eb=eb: e.tensor_tensor(out=pt_[:], in0=pt_[:], in1=eb, op=ALU.mult), reads=[bpt, beb], writes=[bpt])
                    res.append((pt_, bpt, info))
                return res

            def emitPV(u, sres):
                i, c, h, half = u
                pt_, bpt, info = sres
                oc = i * nC + c
                if h == 0 and half == 0:
                    pobox[oc] = poR.next()
                po, bpo = pobox[oc]
                if kind == "d":
                    vt = i * VT_ + c
                    P.op("pe", lambda e: e.matmul(po[:, 65 * h:65 * h + 65], lhsT=pt_[:, 0:128], rhs=vv[:, vt, h, :], start=True, stop=False), reads=[bpt, bv_[vt]], writes=[bpo])
                    P.op("pe", lambda e: e.matmul(po[:, 65 * h:65 * h + 65], lhsT=pt_[:, 128:256], rhs=vv[:, vt + 1, h, :], start=False, stop=True), reads=[bpt, bv_[vt + 1]], writes=[bpo])
                    last = (h == 3)
                else:
                    rstart = info
                    for p4 in range(4):
                        rho = rstart + 2 * p4
                        P.op("pe", lambda e, p4=p4, rho=rho: e.matmul(po[64 * half:64 * half + 64, 65 * h:65 * h + 65], lhsT=pt_[:, 64 * p4:64 * p4 + 64], rhs=vv[:, rho - rbase, h, :], start=(p4 == 0), stop=(p4 == 3)), reads=[bpt, bv_[rho - rbase]], writes=[bpo])
                    last = (h == 3 and half == 1)
                if last:
                    P.op("dve", lambda e: e.tensor_copy(out=ost[:, oc, :], in_=po[:, 0:260]), reads=[bpo], writes=[bo])

            def finish():
                for i, r in enumerate(rs):
                    tok0 = (s0 + r + d * P0) if kind == "d" else (s0 + P0)
                    dap = bass.AP(self.osr.tensor, tok0 * 1300 + ocol, [[d * 1300, 128], [128 * d * 1300, nC], [1, 260]])
                    P.dma("pool", lambda e, dap=dap, i=i: e.dma_start(out=dap, in_=ost[:, i * nC:(i + 1) * nC, :]), reads=[bo], writes=[self.bos])
                    if kind == "d":
                        for t_ in range(VT_):
                            p_lo = P0 + 128 * t_ - 64
                            if p_lo < 0:
                                P.op("pool", lambda e, i=i, t_=t_: e.memset(vv[0:64, i * VT_ + t_, :, 64:65], 1.0), writes=[bv_[i * VT_ + t_]])
                            if p_lo + 128 > n:
                                P.op("pool", lambda e, i=i, t_=t_: e.memset(vv[64:128, i * VT_ + t_, :, 64:65], 1.0), writes=[bv_[i * VT_ + t_]])

            for ui in range(0, len(units), 2):
                ua, ub = units[ui], units[ui + 1]
                ra, rb = emitS2(ua, ub)
                while len(pending) > 2:
                    pending.pop(0)()
                pending.append(lambda u=ua, sres=ra: emitPV(u, sres))
                pending.append(lambda u=ub, sres=rb: emitPV(u, sres))
            pending.append(finish)

        for s in range(NS):
            Ls = self.seqs[s]
            for g in range(3):
                d = DIL[g]
                n = Ls // d
                if n >= SEG:
                    for r in range(d):
                        for P0 in range(0, n, SEG):
                            run_item("d", s, g, [r], P0, SEG)
                else:
                    per = min(d, SEG // n, KTW // (n + 128), 40 // (n // 128 + 1))
                    for r0 in range(0, d, per):
                        run_item("d", s, g, list(range(r0, min(d, r0 + per))), 0, n)
            for g in range(2):
                for P0 in range(0, Ls, SEG):
                    run_item("n", s, g, [0], P0, min(SEG, Ls))
        while pending:
            pending.pop(0)()
        P.barrier()


def _off(t):
    return t.manual_sbuf_range[0]


_CACHE = {}


def _get_nc(seqs):
    key = tuple(seqs)
    if key not in _CACHE:
        _CACHE[key] = Builder(seqs).build()
    return _CACHE[key]


def kernel(x_prompt, x_sample, c_prompt, c_sample, norm1_g, norm2_g, w_mod, b_mod, w_in, q_norm_g, k_norm_g,
           rel_bias, rpb, w_gate, b_gate, w_up_a, w_up_b, w_o, w_ff1, w_ff2):
    f = lambda a: np.ascontiguousarray(np.asarray(a, dtype=np.float32))
    x_prompt, x_sample, c_prompt, c_sample = f(x_prompt), f(x_sample), f(c_prompt), f(c_sample)
    ncores = 8
    Bp, Lp, _ = x_prompt.shape
    Bs, Ls, _ = x_sample.shape
    pp, sp_ = Bp // ncores, Bs // ncores
    seqs = [Lp] * pp + [Ls] * sp_
    nc = _get_nc(seqs)
    shared = {"norm1_g": f(norm1_g), "norm2_g": f(norm2_g), "w_mod": f(w_mod), "b_mod": f(b_mod), "w_in": f(w_in),
              "q_norm_g": f(q_norm_g), "k_norm_g": f(k_norm_g), "rel_bias": f(rel_bias), "rpb": f(rpb),
              "w_gate": f(w_gate), "b_gate": f(b_gate), "w_up_a": f(w_up_a), "w_up_b": f(w_up_b), "w_o": f(w_o),
              "w_ff1": f(w_ff1), "w_ff2": f(w_ff2)}
    for k, v in make_consts().items():
        shared["c_" + k] = v
    in_maps = []
    for c in range(ncores):
        xs = [x_prompt[c * pp + i] for i in range(pp)] + [x_sample[c * sp_ + i] for i in range(sp_)]
        cs = [c_prompt[c * pp + i] for i in range(pp)] + [c_sample[c * sp_ + i] for i in range(sp_)]
        m = dict(shared)
        m["x"] = np.ascontiguousarray(np.concatenate(xs, axis=0))
        m["c"] = np.ascontiguousarray(np.stack(cs, axis=0))
        in_maps.append(m)
    res = run_bass_kernel_spmd(nc, in_maps, core_ids=list(range(ncores)))
    yp = np.empty_like(x_prompt)
    ys = np.empty_like(x_sample)
    for c in range(ncores):
        y = res.results[c]["y"]
        o = 0
        for i in range(pp):
            yp[c * pp + i] = y[o:o + Lp]
            o += Lp
        for i in range(sp_):
            ys[c * sp_ + i] = y[o:o + Ls]
            o += Ls
    return (yp, ys)
```

```python
import numpy as np
from contextlib import ExitStack
import concourse.bass as bass
import concourse.mybir as mybir
from concourse.bass_utils import run_bass_kernel_spmd

F32 = mybir.dt.float32
BF16 = mybir.dt.bfloat16
AF = mybir.ActivationFunctionType
ALU = mybir.AluOpType

D = 1024
DEPTH = 2
NH = 20
HD = 64
DIL = (1, 4, 16)
D_FF = 4096
EPS = 1e-6
NEGV = -30000.0
GRID_W = 64
T = 512
SEG = 2048

SAME_ENG_SYNC = True
DEBUG_SCRATCH = False
USE_PS_BITCAST = True
EPOCH = 30000
DMA_RING = 16


class Buf:
    __slots__ = ("name", "w", "rs", "rd")

    def __init__(self, name=""):
        self.name = name
        self.w = None
        self.rs = {}
        self.rd = []


class Op:
    __slots__ = ("eng", "fn", "waits", "signal", "sem", "val", "is_dma", "idx")

    def __init__(self, eng, fn, is_dma):
        self.eng = eng
        self.fn = fn
        self.waits = []
        self.signal = False
        self.sem = None
        self.val = 0
        self.is_dma = is_dma
        self.idx = 0


class Prog:
    ENGS = ("pe", "act", "dve", "pool", "sp")

    def __init__(self, nc, stack):
        self.nc = nc
        self.stack = stack
        self.ops = {e: [] for e in self.ENGS}
        self.seen = {e: {f: -1 for f in self.ENGS} for e in self.ENGS}
        self.seen_dma = {e: {} for e in self.ENGS}
        self.rings = {}
        self.nsem = 0
        self.pend = {e: [] for e in self.ENGS}

    def new_sem(self, name):
        self.nsem += 1
        return self.stack.enter_context(self.nc.semaphore(f"{name}_{self.nsem}"))

    def _need(self, op, P, raw):
        E = op.eng
        if P is op:
            return
        if P.is_dma:
            sd = self.seen_dma[E]
            key = id(P.sem)
            if sd.get(key, -1) >= P.val:
                return
            sd[key] = P.val
            op.waits.append(P)
            return
        F = P.eng
        if F == E:
            if E == "pe" or E == "sp" or not SAME_ENG_SYNC or not raw:
                return
        if P.idx <= self.seen[E][F]:
            return
        self.seen[E][F] = P.idx
        P.signal = True
        op.waits.append(P)

    def _add(self, op, reads, writes):
        E = op.eng
        lst = self.ops[E]
        op.idx = len(lst)
        if self.pend[E]:
            for Pp in self.pend[E]:
                self._need(op, Pp, True)
            self.pend[E] = []
        for b in reads:
            if b.w is not None:
                self._need(op, b.w, True)
        for b in writes:
            if b.w is not None:
                self._need(op, b.w, True)
            for r in b.rs.values():
                self._need(op, r, False)
            for r in b.rd:
                self._need(op, r, False)
        for b in writes:
            b.w = op
            b.rs = {}
            b.rd = []
        for b in reads:
            if op.is_dma:
                b.rd.append(op)
            else:
                b.rs[E] = op
        lst.append(op)
        return op

    def op(self, eng, fn, reads=(), writes=()):
        return self._add(Op(eng, fn, False), reads, writes)

    def dma(self, q, fn, reads=(), writes=()):
        op = Op(q, fn, True)
        ring = self.rings.get(q)
        if ring is None:
            ring = {"n": 0, "slots": [None] * DMA_RING}
            self.rings[q] = ring
        s = ring["n"] % DMA_RING
        ring["n"] += 1
        slot = ring["slots"][s]
        if slot is None:
            slot = {"sem": self.new_sem(f"d{q}{s}"), "val": 0, "last": None}
            ring["slots"][s] = slot
        if slot["last"] is not None:
            self._need(op, slot["last"], False)
        if slot["val"] + 16 > EPOCH:
            slot["sem"] = self.new_sem(f"d{q}{s}")
            slot["val"] = 0
        slot["val"] += 16
        op.sem = slot["sem"]
        op.val = slot["val"]
        slot["last"] = op
        return self._add(op, reads, writes)

    def barrier(self):
        lasts = []
        for e in self.ENGS:
            for o in reversed(self.ops[e]):
                if not o.is_dma:
                    lasts.append(o)
                    break
        for ring in self.rings.values():
            for slot in ring["slots"]:
                if slot is not None and slot["last"] is not None:
                    lasts.append(slot["last"])
        for e in self.ENGS:
            self.pend[e] = list(lasts)

    def emit(self):
        nc = self.nc
        for e in self.ENGS:
            cnt = 0
            sems = []
            for o in self.ops[e]:
                if o.is_dma or not o.signal:
                    continue
                ep = cnt // EPOCH
                while len(sems) <= ep:
                    sems.append(self.new_sem(f"e{e}"))
                o.sem = sems[ep]
                o.val = cnt % EPOCH + 1
                cnt += 1
        fin = []
        for ring in self.rings.values():
            for slot in ring["slots"]:
                if slot is not None and slot["last"] is not None:
                    fin.append(slot["last"])
        engmap = {"pe": "tensor", "act": "scalar", "dve": "vector", "pool": "gpsimd", "sp": "sync"}
        with nc.Block() as block:
            for e in self.ENGS:
                ops = self.ops[e]

                def body(eng, ops=ops, e=e):
                    for o in ops:
                        for Pw in o.waits:
                            eng.wait_ge(Pw.sem, Pw.val)
                        ins = o.fn(eng)
                        if o.is_dma:
                            ins.then_inc(o.sem, 16)
                        elif o.signal:
                            ins.then_inc(o.sem, 1)
                    if e == "sp":
                        for Pw in fin:
                            eng.wait_ge(Pw.sem, Pw.val)

                getattr(block, engmap[e])(body)


def t5_bucket(rel):
    nb = 16
    max_exact = 8
    ret = (rel > 0).astype(np.int32) * nb
    n = np.abs(rel)
    large = max_exact + (np.log(np.maximum(n, 1) / max_exact) / np.log(1024 / max_exact) * (nb - max_exact)).astype(np.int32)
    large = np.minimum(large, nb - 1)
    return (ret + np.where(n < max_exact, n, large)).astype(np.int32)


def make_consts():
    c = {}
    c["ident"] = np.eye(128, dtype=np.float32)
    c["antiid"] = np.eye(128, dtype=np.float32)[::-1].copy()
    bo = np.zeros((128, 128), np.float32)
    bo[:64, :64] = 1.0 / 64
    bo[64:, 64:] = 1.0 / 64
    c["blockones"] = bo
    c["meanones"] = np.full((128, 128), 1.0 / 1024, np.float32)
    oh = np.zeros((3, 33, 384), np.float32)
    w = np.arange(384)
    j = 191 - w
    for g, d in enumerate(DIL):
        ok = np.abs(j) <= 64
        b = t5_bucket(d * j)
        for ww in range(384):
            if ok[ww]:
                oh[g, b[ww], ww] = 1.0
            else:
                oh[g, 32, ww] = 1.0
    c["ohaug"] = oh
    sel = np.zeros((31, 128), np.float32)
    for dc in range(31):
        sel[dc, 78 - dc] = 1.0
    c["sel"] = sel
    cm = np.zeros((128, 64), np.float32)
    for ci in range(2):
        for cj in range(64):
            kj = 63 - cj
            for qj in range(64):
                cs = min(max(qj - 8, 0), 48)
                if not (cs <= kj < cs + 16):
                    cm[ci * 64 + cj, qj] = NEGV
    c["cmask"] = cm
    return c


WDEF = {
    "w_in": (1024, 3840, 8, 512),
    "w_gate": (1024, 2048, 8, 512),
    "w_up_a": (256, 1024, 2, 1024),
    "w_up_b": (512, 1024, 4, 1024),
    "w_o": (1024, 1024, 8, 512),
    "w_ff1": (1024, 4096, 8, 512),
    "w_ff2": (4096, 1024, 32, 128),
}


def nblk(name):
    K, N, KC, NC = WDEF[name]
    return (N + NC - 1) // NC


class Arena:
    def __init__(self, nc, lo, hi):
        self.nc = nc
        self.lo = lo
        self.hi = hi
        self.p = lo
        self.n = 0

    def alloc(self, shape, dtype):
        esz = 4 if dtype == F32 else 2
        nb = esz
        for s in shape[1:]:
            nb *= s
        off = (self.p + 63) // 64 * 64
        assert off + nb <= self.hi, f"SBUF arena overflow: {off + nb} > {self.hi}"
        self.p = off + nb
        self.n += 1
        return self.nc.alloc_sbuf_tensor_at(f"ar{self.n}", list(shape), dtype, offset=off)

    def mark(self):
        return self.p

    def reset(self, m):
        self.p = m


class Ring:
    def __init__(self, items):
        self.items = items
        self.i = 0

    def next(self):
        it = self.items[self.i % len(self.items)]
        self.i += 1
        return it


class Builder:
    def __init__(self, seqs, depth=DEPTH):
        self.seqs = list(seqs)
        self.NS = len(seqs)
        self.NT = sum(seqs)
        self.depth = depth
        self.s0 = [sum(seqs[:i]) for i in range(self.NS)]

    def build(self):
        nc = bass.Bass("TRN2", target_bir_lowering=False)
        self.nc = nc
        NT, NS, L_ = self.NT, self.NS, self.depth
        di = lambda n, s, dt=F32: nc.dram_tensor(n, list(s), dt, kind="ExternalInput").ap()
        dsc = lambda n, s, dt: nc.dram_tensor(n, list(s), dt, kind=("ExternalOutput" if DEBUG_SCRATCH else "Internal")).ap()
        self.x = di("x", [NT, D])
        self.c = di("c", [NS, D])
        self.norm1_g = di("norm1_g", [DEPTH, D])
        self.norm2_g = di("norm2_g", [DEPTH, D])
        self.w_mod = di("w_mod", [DEPTH, D, 6 * D])
        self.b_mod = di("b_mod", [DEPTH, 6 * D])
        self.q_norm_g = di("q_norm_g", [DEPTH, NH, HD])
        self.k_norm_g = di("k_norm_g", [DEPTH, NH, HD])
        self.rel_bias = di("rel_bias", [32, 12])
        self.rpb = di("rpb", [DEPTH, 8, 15, 31])
        self.b_gate = di("b_gate", [DEPTH, 2 * D])
        self.wsrc = {}
        for n, (K, N, KC, NC) in WDEF.items():
            self.wsrc[n] = di(n, [DEPTH, K, N])
        cs = make_consts()
        self.cin = {k: di("c_" + k, v.shape) for k, v in cs.items()}
        self.y = nc.dram_tensor("y", [NT, D], F32, kind="ExternalOutput").ap()
        self.wq = {n: dsc("wq_" + n, [DEPTH, nblk(n), 128, 4096], BF16) for n in WDEF}
        self.xs = dsc("xs", [8, 128, NT], F32)
        self.qs = dsc("qs", [10, 128, NT], BF16)
        self.ks = dsc("ks", [10, 128, NT], BF16)
        self.vs = dsc("vs", [NT, 1280], BF16)
        self.osr = dsc("osr", [NT, 1300], F32)
        self.rfull = dsc("rfull", [12, 384], BF16)
        self.pnb = dsc("pnb", [DEPTH, 120, 128], BF16)
        with ExitStack() as st:
            self.P = Prog(nc, st)
            self.ar = Arena(nc, 16640, 229376 - 64)
            self.ps = [nc.alloc_psum_tensor(f"psb{i}", [128, 512], F32) for i in range(7)]
            self.psb = [Buf(f"ps{i}") for i in range(8)]
            self.ps16 = nc.alloc_psum_tensor("psb7", [128, 1024], BF16)
            self.prologue()
            for l in range(self.depth):
                self.dense_pass(l)
                self.attention(l)
            self.dense_pass(self.depth)
            self.P.emit()
        return nc

    def prologue(self):
        nc, P, ar = self.nc, self.P, self.ar
        NS = self.NS
        self.ident = ar.alloc([128, 128], F32)
        self.identb = ar.alloc([128, 128], BF16)
        self.J = ar.alloc([128, 128], BF16)
        self.bones = ar.alloc([128, 128], BF16)
        self.mones = ar.alloc([128, 128], BF16)
        self.tab = ar.alloc([128, 128], F32)
        self.modT = ar.alloc([128, DEPTH * 48 * NS], F32)
        self.G = ar.alloc([128, DEPTH * 2 * 8 * NS], F32)
        self.qg8 = ar.alloc([128, DEPTH * 10], F32)
        self.cmask = ar.alloc([128, 64], BF16)
        bC = Buf("consts")
        self.bC = bC
        mk = ar.mark()
        tmpf = ar.alloc([128, 128], F32)
        btmp = Buf()
        P.dma("sp", lambda e: e.dma_start(out=self.ident[:], in_=self.cin["ident"][:, :]), writes=[bC])
        for src, dst in (("ident", self.identb), ("antiid", self.J), ("blockones", self.bones), ("meanones", self.mones)):
            P.dma("sp", lambda e, src=src: e.dma_start(out=tmpf[:], in_=self.cin[src][:, :]), writes=[btmp])
            P.op("dve", lambda e, dst=dst: e.tensor_copy(out=dst[:], in_=tmpf[:]), reads=[btmp], writes=[bC])
        ptab = ar.alloc([128, 128], F32)
        bpt = Buf()
        P.op("dve", lambda e: e.memset(ptab[:], 0.0), writes=[bpt])
        rows = []
        r = 0
        self.col = {}

        def addrows(name, ap2d, n):
            nonlocal r
            self.col[name] = r
            P.dma("sp", lambda e, r=r: e.dma_start(out=ptab[r:r + n, :], in_=ap2d), writes=[bpt])
            r += n

        addrows("c", self.c.rearrange("s (k p) -> (s k) p", p=128), NS * 8)
        addrows("n1", self.norm1_g.rearrange("l (k p) -> (l k) p", p=128), DEPTH * 8)
        addrows("n2", self.norm2_g.rearrange("l (k p) -> (l k) p", p=128), DEPTH * 8)
        addrows("qg", self.q_norm_g.rearrange("l (j a) e -> (l j) (a e)", a=2), DEPTH * 10)
        addrows("kg", self.k_norm_g.rearrange("l (j a) e -> (l j) (a e)", a=2), DEPTH * 10)
        addrows("bg", self.b_gate.rearrange("l (k p) -> (l k) p", p=128), DEPTH * 16)
        assert r <= 128
        pt = self.ps[0]
        P.op("pe", lambda e: e.transpose(pt[:, 0:128], ptab[:], self.ident[:]), reads=[bpt, bC], writes=[self.psb[0]])
        P.op("dve", lambda e: e.tensor_copy(out=self.tab[:], in_=pt[:, 0:128]), reads=[self.psb[0]], writes=[bC])
        cq = self.col["qg"]
        P.op("dve", lambda e: e.tensor_scalar(out=self.qg8[:], in0=self.tab[:, cq:cq + DEPTH * 10], scalar1=0.125, scalar2=None, op0=ALU.mult), reads=[bC], writes=[bC])
        bmT = ar.alloc([128, 96], F32)
        ptab2 = ar.alloc([128, 128], F32)
        bpt2 = Buf()
        P.dma("sp", lambda e: e.dma_start(out=ptab2[0:96, :], in_=self.b_mod.rearrange("l (k p) -> (l k) p", p=128)), writes=[bpt2])
        P.op("pe", lambda e: e.transpose(pt[:, 128:224], ptab2[0:96, :], self.ident[0:96, 0:96]), reads=[bpt2, bC], writes=[self.psb[0]])
        bbm = Buf()
        P.op("dve", lambda e: e.tensor_copy(out=bmT[:], in_=pt[:, 128:224]), reads=[self.psb[0]], writes=[bbm])
        siluT = ar.alloc([128, NS * 8], F32)
        bsl = Buf()
        cc = self.col["c"]
        P.op("act", lambda e: e.activation(out=siluT[:], in_=self.tab[:, cc:cc + NS * 8], func=AF.Silu), reads=[bC], writes=[bsl])
        wm = [ar.alloc([128, 8, 512], F32) for _ in range(2)]
        bwm = [Buf(), Buf()]
        bmod = Buf("mod")
        self.bmod = bmod
        it = 0
        for l in range(self.depth):
            for blk in range(12):
                wt, bw = wm[it % 2], bwm[it % 2]
                it += 1
                P.dma("sp", lambda e, wt=wt, l=l, blk=blk: e.dma_start(out=wt[:], in_=self.w_mod[l].rearrange("(k p) n -> p k n", p=128)[:, :, blk * 512:(blk + 1) * 512]), writes=[bw])
                for c4 in range(4):
                    ccol = blk * 4 + c4
                    pb = 1 + (ccol % 2)
                    pst = self.ps[pb]
                    for kc in range(8):
                        P.op("pe", lambda e, wt=wt, kc=kc, c4=c4, pst=pst: e.matmul(pst[:, 0:NS], lhsT=wt[:, kc, c4 * 128:(c4 + 1) * 128], rhs=siluT[:].rearrange("p (s k) -> p k s", k=8)[:, kc, :], start=(kc == 0), stop=(kc == 7)), reads=[bw, bsl], writes=[self.psb[pb]])
                    o0 = (l * 48 + ccol) * NS
                    P.op("dve", lambda e, pst=pst, o0=o0, l=l, ccol=ccol: e.tensor_scalar(out=self.modT[:, o0:o0 + NS], in0=pst[:, 0:NS], scalar1=bmT[:, l * 48 + ccol:l * 48 + ccol + 1], scalar2=None, op0=ALU.add), reads=[self.psb[pb], bbm], writes=[bmod])
        for l in range(self.depth):
            for wh in range(2):
                for kc in range(8):
                    mi = (l * 48 + (1 if wh == 0 else 4) * 8 + kc) * NS
                    gi = ((l * 2 + wh) * 8 + kc) * NS
                    ncol = self.col["n1" if wh == 0 else "n2"] + l * 8 + kc
                    P.op("dve", lambda e, mi=mi, gi=gi, ncol=ncol: e.tensor_scalar(out=self.G[:, gi:gi + NS], in0=self.modT[:, mi:mi + NS], scalar1=1.0, scalar2=self.tab[:, ncol:ncol + 1], op0=ALU.add, op1=ALU.mult), reads=[bmod, bC], writes=[bmod])
        self.bwq = {}
        for l in range(self.depth):
            for n in ("w_in", "w_gate", "w_up_a", "w_up_b", "w_o", "w_ff1", "w_ff2"):
                K, N, KC, NC = WDEF[n]
                for b in range(nblk(n)):
                    ncol = min(NC, N - b * NC)
                    bb = Buf()
                    self.bwq[(n, l, b)] = bb
                    src = self.wsrc[n][l].rearrange("(k p) n -> p k n", p=128)[:, :, b * NC:b * NC + ncol]
                    dst = self.wq[n][l, b][:, 0:KC * ncol].rearrange("p (k n) -> p k n", k=KC)
                    P.dma("pool", lambda e, src=src, dst=dst: e.dma_start(out=dst, in_=src), writes=[bb])
        taug = ar.alloc([33, 12], F32)
        btg = Buf()
        P.op("dve", lambda e: e.memset(taug[32:33, :], NEGV), writes=[btg])
        P.dma("sp", lambda e: e.dma_start(out=taug[0:32, :], in_=self.rel_bias[:, :]), writes=[btg])
        oh = ar.alloc([33, 3, 384], F32)
        boh = Buf()
        P.dma("sp", lambda e: e.dma_start(out=oh[:], in_=self.cin["ohaug"].rearrange("g b w -> b g w")), writes=[boh])
        rst = ar.alloc([4, 3, 384], BF16)
        brst = Buf()
        for g in range(3):
            pst = self.ps[3]
            P.op("pe", lambda e, g=g, pst=pst: e.matmul(pst[0:4, 0:384], lhsT=taug[:, 4 * g:4 * g + 4], rhs=oh[:, g, :], start=True, stop=True), reads=[btg, boh], writes=[self.psb[3]])
            P.op("dve", lambda e, g=g, pst=pst: e.tensor_copy(out=rst[:, g, :], in_=pst[0:4, 0:384]), reads=[self.psb[3]], writes=[brst])
        self.brf = Buf("rfull")
        P.dma("pool", lambda e: e.dma_start(out=self.rfull.rearrange("(g h) w -> h g w", h=4), in_=rst[:]), reads=[brst], writes=[self.brf])
        self.bpnb = Buf("pnb")
        sel = ar.alloc([31, 128], F32)
        bsel = Buf()
        P.dma("sp", lambda e: e.dma_start(out=sel[:], in_=self.cin["sel"][:, :]), writes=[bsel])
        for l in range(self.depth):
            rc = ar.alloc([120, 31], F32)
            brc = Buf()
            P.dma("sp", lambda e, l=l, rc=rc: e.dma_start(out=rc[:], in_=self.rpb[l].rearrange("h r c -> (h r) c")), writes=[brc])
            pst = self.ps[4]
            P.op("pe", lambda e, rc=rc, pst=pst: e.transpose(pst[0:31, 0:120], rc[:], self.ident[0:120, 0:120]), reads=[brc, bC], writes=[self.psb[4]])
            xr = ar.alloc([31, 120], F32)
            bxr = Buf()
            P.op("dve", lambda e, xr=xr, pst=pst: e.tensor_copy(out=xr[:], in_=pst[0:31, 0:120]), reads=[self.psb[4]], writes=[bxr])
            P.op("pe", lambda e, xr=xr, pst=pst: e.matmul(pst[0:120, 128:256], lhsT=xr[:], rhs=sel[:], start=True, stop=True), reads=[bxr, bsel], writes=[self.psb[4]])
            pn = ar.alloc([120, 128], BF16)
            bpn = Buf()
            P.op("dve", lambda e, pn=pn, pst=pst: e.tensor_copy(out=pn[:], in_=pst[0:120, 128:256]), reads=[self.psb[4]], writes=[bpn])
            P.dma("pool", lambda e, l=l, pn=pn: e.dma_start(out=self.pnb[l], in_=pn[:]), reads=[bpn], writes=[self.bpnb])
        P.dma("pool", lambda e: e.dma_start(out=self.cmask[:], in_=self.cin["cmask"][:, :]), writes=[bC])
        P.barrier()
        self.base_mark = mk
        self.bxs = [[Buf() for _ in range(self.NT // T)] for _ in range(1)][0]
        self.bqk = Buf("qk")
        self.bv = Buf("v")
        self.bos = Buf("os")

    def dense_pass(self, lp):
        nc, P, ar = self.nc, self.P, self.ar
        NS, NT = self.NS, self.NT
        ar.reset(self.base_mark)
        doC = lp > 0
        doA = lp < self.depth
        lc = lp - 1
        la = lp
        NSUB = T // 128
        xTs = [ar.alloc([128, 8, T], F32) for _ in range(2)]
        hTs = [ar.alloc([128, 8, T], BF16) for _ in range(2)]
        rstd = ar.alloc([128, T], F32)
        big = ar.alloc([128, 32, T], BF16)
        iost = nc.alloc_sbuf_tensor_at("iost%d" % lp, [128, 4, 1024], F32, offset=_off(big) + 8 * T * 4)
        tmps = [ar.alloc([128, T], F32) for _ in range(3)]
        gT = ar.alloc([128, 16, T], BF16)
        sq = ar.alloc([128, 8, T], BF16)
        oraw = [ar.alloc([128, 1300], F32) for _ in range(2)]
        otoks = [ar.alloc([128, 768], BF16) for _ in range(NSUB)]
        osum = ar.alloc([128, 260], F32)
        rden = ar.alloc([128, 12], F32)
        oT = ar.alloc([128, 6, T], BF16)
        t12 = [ar.alloc([128, T], BF16) for _ in range(4)]
        mixT = ar.alloc([128, 8, T], BF16)
        rl = [ar.alloc([128, T], BF16) for _ in range(2)]
        sq2 = [ar.alloc([128, T], BF16) for _ in range(2)]
        rs2 = [ar.alloc([128, T], F32) for _ in range(2)]
        qst = [ar.alloc([128, T], BF16) for _ in range(3)]
        vst = [ar.alloc([128, 512], BF16) for _ in range(3)]
        NW = 5
        wsl = [ar.alloc([128, 4096], BF16) for _ in range(NW)]
        bxTs = [[Buf() for _ in range(8)] for _ in range(2)]
        bhTs = [[Buf() for _ in range(8)] for _ in range(2)]
        brstd = Buf()
        bbig = [Buf() for _ in range(32)]
        btmps = [Buf() for _ in range(3)]
        bgT = [Buf() for _ in range(16)]
        bsq = [Buf() for _ in range(8)]
        boraw = [Buf(), Buf()]
        botoks = [Buf() for _ in range(NSUB)]
        bosum, brden = Buf(), Buf()
        boT = Buf()
        bt12 = [Buf() for _ in range(4)]
        bmix = [Buf() for _ in range(8)]
        brl = [Buf(), Buf()]
        bsq2 = [Buf(), Buf()]
        brs2 = [Buf(), Buf()]
        bqst = [Buf() for _ in range(3)]
        bvst = [Buf(), Buf(), Buf()]
        bws = [Buf() for _ in range(NW)]
        bC, bmod = self.bC, self.bmod
        tab, modT, G = self.tab, self.modT, self.G
        ntiles = NT // T
        R = {}

        def mkrings():
            R["acc"] = Ring([(self.ps[i], self.psb[i]) for i in range(5)])
            R["stt"] = Ring([(self.ps[i], self.psb[i]) for i in (5, 6)])
            R["t12"] = Ring(list(zip(t12, bt12)))
            R["rl"] = Ring(list(zip(rl, brl)))
            R["sq2"] = Ring(list(zip(sq2, bsq2)))
            R["rs2"] = Ring(list(zip(rs2, brs2)))
            R["qst"] = Ring(list(zip(qst, bqst)))
            R["oraw"] = Ring(list(zip(oraw, boraw)))
            R["tmp"] = Ring(list(zip(tmps, btmps)))
            R["vst"] = Ring(list(zip(vst, bvst)))

        order = []
        wstate = {"emit": 0, "use": 0, "dry": True}

        def wget(n, l, b):
            K_, N_, KC, NC = WDEF[n]
            ncol = min(NC, N_ - b * NC)
            if wstate["dry"]:
                order.append((n, l, b))
                return None, None, ncol
            while wstate["emit"] < len(order) and wstate["emit"] < wstate["use"] + NW - 2:
                i = wstate["emit"]
                n2, l2, b2 = order[i]
                sl, bs = wsl[i % NW], bws[i % NW]
                P.dma("sp", lambda e, sl=sl, n2=n2, l2=l2, b2=b2: e.dma_start(out=sl[:], in_=self.wq[n2][l2, b2]), reads=[self.bwq[(n2, l2, b2)]], writes=[bs])
                wstate["emit"] += 1
            i = wstate["use"]
            wstate["use"] += 1
            assert order[i] == (n, l, b), (order[i], (n, l, b))
            return wsl[i % NW][:, 0:KC * ncol].rearrange("p (k n) -> p k n", k=KC), bws[i % NW], ncol

        def op(*a, **k):
            if not wstate["dry"]:
                P.op(*a, **k)

        def dma(*a, **k):
            if not wstate["dry"]:
                P.dma(*a, **k)

        def mcol(l, j, kc, s):
            o = (l * 48 + j * 8 + kc) * NS + s
            return modT[:, o:o + 1]

        def gcol(l, wh, kc, s):
            o = ((l * 2 + wh) * 8 + kc) * NS + s
            return G[:, o:o + 1]

        def tinfo(ti):
            t0 = ti * T
            s = max(i for i in range(NS) if self.s0[i] <= t0)
            return t0, s, self.s0[s], self.seqs[s]

        def norm_sq(xi, kc):
            xT, bxT = xTs[xi], bxTs[xi]
            op("act", lambda e, kc=kc: e.activation(out=sq[:, kc, :], in_=xT[:, kc, :], func=AF.Square), reads=[bxT[kc]], writes=[bsq[kc]])

        def norm_fin(l, wh, s, xi, hi):
            xT, bxT, hT, bhT = xTs[xi], bxTs[xi], hTs[hi], bhTs[hi]
            pst, bp = R["stt"].next()
            for kc in range(8):
                op("pe", lambda e, kc=kc, pst=pst: e.matmul(pst[:, 0:T], lhsT=self.mones[:], rhs=sq[:, kc, :], start=(kc == 0), stop=(kc == 7)), reads=[bsq[kc], bC], writes=[bp])
            op("act", lambda e, pst=pst: e.activation(out=rstd[:], in_=pst[:, 0:T], func=AF.Ln, bias=EPS, scale=1.0), reads=[bp], writes=[brstd])
            op("act", lambda e: e.activation(out=rstd[:], in_=rstd[:], func=AF.Exp, scale=-0.5), reads=[brstd], writes=[brstd])
            shj = 0 if wh == 0 else 3
            for kc in range(8):
                tm, btm = R["tmp"].next()
                op("dve", lambda e, kc=kc, tm=tm: e.scalar_tensor_tensor(out=tm[:], in0=xT[:, kc, :], scalar=gcol(l, wh, kc, s), in1=rstd[:], op0=ALU.mult, op1=ALU.mult), reads=[bxT[kc], brstd, bmod], writes=[btm])
                op("act", lambda e, kc=kc, tm=tm: e.activation(out=hT[:, kc, :], in_=tm[:], func=AF.Identity, bias=mcol(l, shj, kc, s), scale=1.0), reads=[btm, bmod], writes=[bhT[kc]])

        def norm(l, wh, s, xi, hi):
            for kc in range(8):
                norm_sq(xi, kc)
            norm_fin(l, wh, s, xi, hi)

        def load(ti, only=None):
            t0, s, s0, Ls = tinfo(ti)
            xi = ti % 2
            xT, bxT = xTs[xi], bxTs[xi]
            if lp == 0:
                if only is None or only == "dma":
                    for sub in range(NSUB):
                        dma("sp", lambda e, sub=sub: e.dma_start(out=iost[:, sub, :], in_=self.x[t0 + sub * 128:t0 + (sub + 1) * 128, :]), writes=bbig[16 + 4 * sub:20 + 4 * sub])
                for kc in (range(8) if only is None else ([] if only == "dma" else [only])):
                    pst, bp = R["acc"].next()
                    for sub in range(NSUB):
                        op("pe", lambda e, kc=kc, sub=sub, pst=pst: e.transpose(pst[:, sub * 128:(sub + 1) * 128], iost[:, sub, kc * 128:(kc + 1) * 128], self.ident[:]), reads=bbig[16 + 4 * sub:20 + 4 * sub] + [bC], writes=[bp])
                    op("dve", lambda e, kc=kc, pst=pst: e.tensor_copy(out=xT[:, kc, :], in_=pst[:, 0:T]), reads=[bp], writes=[bxT[kc]])
            else:
                dma("sp", lambda e: e.dma_start(out=xT[:], in_=self.xs[:, :, t0:t0 + T].rearrange("k p t -> p k t")), reads=[self.bxs[ti]], writes=bxT)

        def omerge(ti):
            t0, s, s0, Ls = tinfo(ti)
            for sub in range(NSUB):
                otok, botok = otoks[sub], botoks[sub]
                orw, bor = R["oraw"].next()
                dma("sp", lambda e, orw=orw, sub=sub: e.dma_start(out=orw[:], in_=self.osr[t0 + sub * 128:t0 + (sub + 1) * 128, :]), reads=[self.bos], writes=[bor])
                op("pool", lambda e, orw=orw: e.tensor_tensor(out=osum[:], in0=orw[:, 0:260], in1=orw[:, 260:520], op=ALU.add), reads=[bor], writes=[bosum])
                op("pool", lambda e, orw=orw: e.tensor_tensor(out=osum[:], in0=osum[:], in1=orw[:, 520:780], op=ALU.add), reads=[bor, bosum], writes=[bosum])
                op("dve", lambda e: e.reciprocal(out=rden[:, 0:4], in_=osum[:].rearrange("p (h e) -> p h e", e=65)[:, :, 64]), reads=[bosum], writes=[brden])
                op("dve", lambda e, orw=orw: e.reciprocal(out=rden[:, 4:12], in_=orw[:, 780:1300].rearrange("p (h e) -> p h e", e=65)[:, :, 64]), reads=[bor], writes=[brden])
                op("dve", lambda e, otok=otok: e.tensor_tensor(out=otok[:, 0:256].rearrange("p (h e) -> p h e", e=64), in0=osum[:].rearrange("p (h e) -> p h e", e=65)[:, :, 0:64], in1=rden[:, 0:4].unsqueeze(2).to_broadcast([128, 4, 64]), op=ALU.mult), reads=[bosum, brden], writes=[botok])
                op("dve", lambda e, orw=orw, otok=otok: e.tensor_tensor(out=otok[:, 256:768].rearrange("p (h e) -> p h e", e=64), in0=orw[:, 780:1300].rearrange("p (h e) -> p h e", e=65)[:, :, 0:64], in1=rden[:, 4:12].unsqueeze(2).to_broadcast([128, 8, 64]), op=ALU.mult), reads=[bor, brden], writes=[botok])

        def gate(ti):
            t0, s, s0, Ls = tinfo(ti)
            l = lc
            hT, bhT = hTs[0], bhTs[0]
            for b in range(4):
                wt, bw, _ = wget("w_gate", l, b)
                for c4 in range(4):
                    cc = b * 4 + c4
                    pst, bp = R["acc"].next()
                    for kc in range(8):
                        op("pe", lambda e, wt=wt, kc=kc, c4=c4, pst=pst: e.matmul(pst[:, 0:T], lhsT=wt[:, kc, c4 * 128:(c4 + 1) * 128], rhs=hT[:, kc, :], start=(kc == 0), stop=(kc == 7)), reads=[bw, bhT[kc]], writes=[bp])
                    bcol = self.col["bg"] + l * 16 + cc
                    op("act", lambda e, cc=cc, pst=pst, bcol=bcol: e.activation(out=gT[:, cc, :], in_=pst[:, 0:T], func=AF.Sigmoid, bias=tab[:, bcol:bcol + 1], scale=1.0), reads=[bp, bC], writes=[bgT[cc]])

        def c_rest(ti, after_T=None):
            t0, s, s0, Ls = tinfo(ti)
            l = lc
            xi = ti % 2
            xT, bxT = xTs[xi], bxTs[xi]
            hT, bhT = hTs[0], bhTs[0]
            for sub in range(NSUB):
                otok, botok = otoks[sub], botoks[sub]
                if USE_PS_BITCAST:
                    pstf, bpT = R["acc"].next()
                    pT = pstf[:, 0:384].bitcast(BF16)
                else:
                    pT, bpT = self.ps16[:, 0:768], self.psb[7]
                for kc in range(6):
                    op("pe", lambda e, kc=kc, otok=otok, pT=pT: e.transpose(pT[:, kc * 128:(kc + 1) * 128], otok[:, kc * 128:(kc + 1) * 128], self.identb[:]), reads=[botok, bC], writes=[bpT])
                op("act", lambda e, sub=sub, pT=pT: e.copy(out=oT[:, :, sub * 128:(sub + 1) * 128], in_=pT.rearrange("p (k t) -> p k t", k=6)), reads=[bpT], writes=[boT])
            wa, bwa, _ = wget("w_up_a", l, 0)
            wb, bwb, _ = wget("w_up_b", l, 0)
            for cc in range(8):
                pa, bpa = R["acc"].next()
                for kc in range(2):
                    op("pe", lambda e, kc=kc, cc=cc, pa=pa: e.matmul(pa[:, 0:T], lhsT=wa[:, kc, cc * 128:(cc + 1) * 128], rhs=oT[:, kc, :], start=(kc == 0), stop=(kc == 1)), reads=[bwa, boT], writes=[bpa])
                pb_, bpb = R["acc"].next()
                for kc in range(4):
                    op("pe", lambda e, kc=kc, cc=cc, pb_=pb_: e.matmul(pb_[:, 0:T], lhsT=wb[:, kc, cc * 128:(cc + 1) * 128], rhs=oT[:, 2 + kc, :], start=(kc == 0), stop=(kc == 3)), reads=[bwb, boT], writes=[bpb])
                t1, bt1 = R["t12"].next()
                t2, bt2 = R["t12"].next()
                op("dve", lambda e, cc=cc, pa=pa, t1=t1: e.tensor_tensor(out=t1[:], in0=pa[:, 0:T], in1=gT[:, cc, :], op=ALU.mult), reads=[bpa, bgT[cc]], writes=[bt1])
                op("dve", lambda e, cc=cc, pb_=pb_, t2=t2: e.tensor_tensor(out=t2[:], in0=pb_[:, 0:T], in1=gT[:, 8 + cc, :], op=ALU.mult), reads=[bpb, bgT[8 + cc]], writes=[bt2])
                op("pool", lambda e, cc=cc, t1=t1, t2=t2: e.tensor_tensor(out=mixT[:, cc, :], in0=t1[:], in1=t2[:], op=ALU.add), reads=[bt1, bt2], writes=[bmix[cc]])
            for b in range(2):
                wt, bw, _ = wget("w_o", l, b)
                for c4 in range(4):
                    cc = b * 4 + c4
                    pst, bp = R["acc"].next()
                    for kc in range(8):
                        op("pe", lambda e, wt=wt, kc=kc, c4=c4, pst=pst: e.matmul(pst[:, 0:T], lhsT=wt[:, kc, c4 * 128:(c4 + 1) * 128], rhs=mixT[:, kc, :], start=(kc == 0), stop=(kc == 7)), reads=[bw, bmix[kc]], writes=[bp])
                    op("dve", lambda e, cc=cc, pst=pst: e.scalar_tensor_tensor(out=xT[:, cc, :], in0=pst[:, 0:T], scalar=mcol(l, 2, cc, s), in1=xT[:, cc, :], op0=ALU.mult, op1=ALU.add), reads=[bp, bxT[cc], bmod], writes=[bxT[cc]])
                    norm_sq(xi, cc)
            norm_fin(l, 1, s, xi, 0)
            if after_T is not None:
                after_T()
            for b in range(8):
                wt, bw, _ = wget("w_ff1", l, b)
                for c4 in range(4):
                    cc = b * 4 + c4
                    pst, bp = R["acc"].next()
                    for kc in range(8):
                        op("pe", lambda e, wt=wt, kc=kc, c4=c4, pst=pst: e.matmul(pst[:, 0:T], lhsT=wt[:, kc, c4 * 128:(c4 + 1) * 128], rhs=hT[:, kc, :], start=(kc == 0), stop=(kc == 7)), reads=[bw, bhT[kc]], writes=[bp])
                    r_, br_ = R["rl"].next()
                    op("act", lambda e, pst=pst, r_=r_: e.activation(out=r_[:], in_=pst[:, 0:T], func=AF.Relu), reads=[bp], writes=[br_])
                    op("pool", lambda e, cc=cc, r_=r_: e.tensor_tensor(out=big[:, cc, :], in0=r_[:], in1=r_[:], op=ALU.mult), reads=[br_], writes=[bbig[cc]])
            if ti + 1 < ntiles:
                t0n, sn, _, _ = tinfo(ti + 1)
                norm(lc, 0, sn, (ti + 1) % 2, 0)
            for cc in range(8):
                wt, bw, _ = wget("w_ff2", l, cc)
                pst, bp = R["acc"].next()
                for kc in range(32):
                    op("pe", lambda e, wt=wt, kc=kc, pst=pst: e.matmul(pst[:, 0:T], lhsT=wt[:, kc, :], rhs=big[:, kc, :], start=(kc == 0), stop=(kc == 31)), reads=[bw, bbig[kc]], writes=[bp])
                op("dve", lambda e, cc=cc, pst=pst: e.scalar_tensor_tensor(out=xT[:, cc, :], in0=pst[:, 0:T], scalar=mcol(l, 5, cc, s), in1=xT[:, cc, :], op0=ALU.mult, op1=ALU.add), reads=[bp, bxT[cc], bmod], writes=[bxT[cc]])
                if doA:
                    norm_sq(xi, cc)
            if not doA:
                for sub in range(NSUB):
                    for hf in range(2):
                        pst, bp = R["acc"].next()
                        for k4 in range(4):
                            kc = hf * 4 + k4
                            op("pe", lambda e, kc=kc, k4=k4, sub=sub, pst=pst: e.transpose(pst[:, k4 * 128:(k4 + 1) * 128], xT[:, kc, sub * 128:(sub + 1) * 128], self.ident[:]), reads=[bxT[kc], bC], writes=[bp])
                        wr = [bbig[16 + sub * 4 + hf * 2], bbig[16 + sub * 4 + hf * 2 + 1]]
                        if hf:
                            op("dve", lambda e, hf=hf, sub=sub, pst=pst: e.tensor_copy(out=iost[:, sub, hf * 512:(hf + 1) * 512], in_=pst[:, 0:512]), reads=[bp], writes=wr)
                        else:
                            op("act", lambda e, hf=hf, sub=sub, pst=pst: e.copy(out=iost[:, sub, hf * 512:(hf + 1) * 512], in_=pst[:, 0:512]), reads=[bp], writes=wr)
                    dma("pool", lambda e, sub=sub: e.dma_start(out=self.y[t0 + sub * 128:t0 + (sub + 1) * 128, :], in_=iost[:, sub, :]), reads=bbig[16 + sub * 4:16 + sub * 4 + 4], writes=[])

        def store_x(ti):
            t0, s, s0, Ls = tinfo(ti)
            xi = ti % 2
            dma("pool", lambda e: e.dma_start(out=self.xs[:, :, t0:t0 + T].rearrange("k p t -> p k t"), in_=xTs[xi][:]), reads=bxTs[xi], writes=[self.bxs[ti]])

        def a_qk(ti, hi, j0, j1, held, hook=None):
            t0, s, s0, Ls = tinfo(ti)
            l = la
            hT, bhT = hTs[hi], bhTs[hi]

            def finish(ctx):
                j, pst, bp, s2, bs2 = ctx
                jj = j % 10
                d = DIL[jj // 2] if jj < 6 else 1
                pm, bpm = R["stt"].next()
                op("pe", lambda e, pm=pm, s2=s2: e.matmul(pm[:, 0:T], lhsT=self.bones[:], rhs=s2[:], start=True, stop=True), reads=[bs2, bC], writes=[bpm])
                r2, br2 = R["rs2"].next()
                op("act", lambda e, pm=pm, r2=r2: e.activation(out=r2[:], in_=pm[:, 0:T], func=AF.Ln, bias=EPS, scale=1.0), reads=[bpm], writes=[br2])
                op("act", lambda e, r2=r2: e.activation(out=r2[:], in_=r2[:], func=AF.Exp, scale=-0.5), reads=[br2], writes=[br2])
                qo, bqo = R["qst"].next()
                if j < 10:
                    gc = self.qg8[:, l * 10 + jj:l * 10 + jj + 1]
                else:
                    kcol = self.col["kg"] + l * 10 + jj
                    gc = tab[:, kcol:kcol + 1]
                op("dve", lambda e, pst=pst, r2=r2, qo=qo, gc=gc, d=d: e.scalar_tensor_tensor(out=qo[:].rearrange("p (r m) -> p r m", r=d), in0=pst[:, 0:T].rearrange("p (m r) -> p r m", r=d), scalar=gc, in1=r2[:].rearrange("p (m r) -> p r m", r=d), op0=ALU.mult, op1=ALU.mult), reads=[bp, br2, bC], writes=[bqo])
                dst = (self.qs if j < 10 else self.ks)
                n = Ls // d
                col0 = s0 + (t0 - s0) // d
                dap = bass.AP(dst.tensor, jj * 128 * NT + col0, [[NT, 128], [n, d], [1, T // d]])
                dma("pool", lambda e, dap=dap, qo=qo, d=d: e.dma_start(out=dap, in_=qo[:].rearrange("p (r m) -> p r m", r=d)), reads=[bqo], writes=[self.bqk])

            for j in range(j0, j1):
                if j % 4 == 0:
                    held[0] = wget("w_in", l, j // 4)
                wt, bw, _ = held[0]
                c4 = j % 4
                jj = j % 10
                d = DIL[jj // 2] if jj < 6 else 1
                pst, bp = R["acc"].next()
                for kc in range(8):
                    op("pe", lambda e, wt=wt, kc=kc, c4=c4, pst=pst: e.matmul(pst[:, 0:T], lhsT=wt[:, kc, c4 * 128:(c4 + 1) * 128], rhs=hT[:, kc, :], start=(kc == 0), stop=(kc == 7)), reads=[bw, bhT[kc]], writes=[bp])
                s2, bs2 = R["sq2"].next()
                op("act", lambda e, pst=pst, s2=s2: e.activation(out=s2[:], in_=pst[:, 0:T], func=AF.Square), reads=[bp], writes=[bs2])
                if held[1] is not None:
                    finish(held[1])
                held[1] = (j, pst, bp, s2, bs2)
                while held[2] < -(-(j + 1) * 3 * NSUB // 20):
                    v_group(ti, hi, held)
                if hook is not None:
                    hook(j)
            if j1 == 20:
                finish(held[1])
                held[1] = None

        def v_group(ti, hi, held):
            t0, s, s0, Ls = tinfo(ti)
            l = la
            hT, bhT = hTs[hi], bhTs[hi]
            gi = held[2]
            held[2] += 1
            vi, sub = gi // NSUB, gi % NSUB
            if sub == 0:
                held[3] = wget("w_in", l, 5 + vi)
            wt, bw, ncol = held[3]
            pst, bp = R["acc"].next()
            for kc in range(8):
                op("pe", lambda e, wt=wt, kc=kc, sub=sub, pst=pst, ncol=ncol: e.matmul(pst[:, 0:ncol], lhsT=hT[:, kc, sub * 128:(sub + 1) * 128], rhs=wt[:, kc, :], start=(kc == 0), stop=(kc == 7)), reads=[bw, bhT[kc]], writes=[bp])
            vs_, bvs_ = R["vst"].next()
            op("dve", lambda e, pst=pst, ncol=ncol, vs_=vs_: e.tensor_copy(out=vs_[:, 0:ncol], in_=pst[:, 0:ncol]), reads=[bp], writes=[bvs_])
            dma("pool", lambda e, sub=sub, vi=vi, ncol=ncol, vs_=vs_: e.dma_start(out=self.vs[t0 + sub * 128:t0 + (sub + 1) * 128, vi * 512:vi * 512 + ncol], in_=vs_[:, 0:ncol]), reads=[bvs_], writes=[self.bv])

        def emit_all():
            mkrings()
            if lp == 0:
                load(0)
                _, s_, _, _ = tinfo(0)
                norm(la, 0, s_, 0, 0)
                for ti in range(ntiles):
                    hook = None
                    if ti + 1 < ntiles:
                        load(ti + 1, "dma")
                        hook = (lambda j, ti=ti: load(ti + 1, j) if j < 8 else None)
                    store_x(ti)
                    held = [None, None, 0, None]
                    a_qk(ti, ti % 2, 0, 10, held, hook)
                    if ti + 1 < ntiles:
                        _, sn, _, _ = tinfo(ti + 1)
                        norm(la, 0, sn, (ti + 1) % 2, (ti + 1) % 2)
                    a_qk(ti, ti % 2, 10, 20, held)
            else:
                load(0)
                omerge(0)
                _, s_, _, _ = tinfo(0)
                norm(lc, 0, s_, 0, 0)
                gate(0)
                for ti in range(ntiles):
                    if ti + 1 < ntiles:
                        load(ti + 1)
                    c_rest(ti, (lambda ti=ti: omerge(ti + 1)) if ti + 1 < ntiles else None)
                    if doA:
                        store_x(ti)
                        _, s_, _, _ = tinfo(ti)
                        norm_fin(la, 0, s_, ti % 2, 1)
                    if ti + 1 < ntiles:
                        gate(ti + 1)
                    if doA:
                        held = [None, None, 0, None]
                        a_qk(ti, 1, 0, 20, held)

        emit_all()
        wstate["dry"] = False
        emit_all()
        P.barrier()

    def attention(self, l):
        nc, P, ar = self.nc, self.P, self.ar
        NS, NT = self.NS, self.NT
        ar.reset(self.base_mark)
        bC = self.bC
        Tdil = ar.alloc([128, 12, 256], BF16)
        bTd = Buf()
        for half, off in ((0, 128), (1, 0)):
            src = bass.AP(self.rfull.tensor, off, [[1, 128], [384, 12], [1, 128]])
            P.dma("sp", lambda e, src=src, half=half: e.dma_start(out=Tdil[:, :, half * 128:(half + 1) * 128], in_=src), reads=[self.brf], writes=[bTd])
        Tnb = ar.alloc([128, 2, 8, 448], BF16)
        bTn = Buf()
        for par in range(2):
            for slot in range(7):
                e_ = (-6 if par == 0 else -7) + 2 * slot
                for ci in range(2):
                    src = bass.AP(self.pnb.tensor, l * 120 * 128 + (e_ + 8 - ci) * 128, [[1, 64], [15 * 128, 8], [1, 64]])
                    P.dma("sp", lambda e, src=src, par=par, slot=slot, ci=ci: e.dma_start(out=Tnb[ci * 64:(ci + 1) * 64, par, :, slot * 64:(slot + 1) * 64], in_=src), reads=[self.bpnb], writes=[bTn])
        P.op("dve", lambda e: e.tensor_tensor(out=Tnb[:].rearrange("p a h (s q) -> p (a h s) q", q=64), in0=Tnb[:].rearrange("p a h (s q) -> p (a h s) q", q=64), in1=self.cmask[:].unsqueeze(1).to_broadcast([128, 112, 64]), op=ALU.add), reads=[bTn, bC], writes=[bTn])
        EBd = ar.alloc([128, 12, 256], BF16)
        EBn = ar.alloc([128, 2, 8, 448], BF16)
        bEd, bEn = Buf(), Buf()
        for hh in range(12):
            pb = hh % 3
            P.op("pe", lambda e, hh=hh, pb=pb: e.matmul(self.ps[pb][:, 0:256], lhsT=self.J[:], rhs=Tdil[:, hh, :], start=True, stop=True), reads=[bC, bTd], writes=[self.psb[pb]])
            P.op("act", lambda e, hh=hh, pb=pb: e.activation(out=EBd[:, hh, :], in_=self.ps[pb][:, 0:256], func=AF.Exp), reads=[self.psb[pb]], writes=[bEd])
        for par in range(2):
            for hh in range(8):
                pb = hh % 3
                P.op("pe", lambda e, hh=hh, pb=pb, par=par: e.matmul(self.ps[pb][:, 0:448], lhsT=self.J[:], rhs=Tnb[:, par, hh, :], start=True, stop=True), reads=[bC, bTn], writes=[self.psb[pb]])
                P.op("act", lambda e, hh=hh, pb=pb, par=par: e.activation(out=EBn[:, par, hh, :], in_=self.ps[pb][:, 0:448], func=AF.Exp), reads=[self.psb[pb]], writes=[bEn])
        NCH = SEG // 128
        QT = [ar.alloc([128, 2, SEG], BF16) for _ in range(2)]
        KTW = 4096
        KT = [ar.alloc([128, 2, KTW], BF16) for _ in range(2)]
        VV = [ar.alloc([128, 40, 4, 65], BF16) for _ in range(2)]
        OST = [ar.alloc([128, NCH, 260], F32) for _ in range(2)]
        PT = [ar.alloc([128, 256], BF16) for _ in range(8)]
        bQT, bOST = [Buf(), Buf()], [Buf(), Buf()]
        bKT = [[Buf(), Buf()] for _ in range(2)]
        bVV = [[Buf() for _ in range(40)] for _ in range(2)]
        bPT = [Buf() for _ in range(8)]
        for i in range(2):
            P.op("pool", lambda e, i=i: e.memset(VV[i][:], 1.0), writes=bVV[i])
            P.op("pool", lambda e, i=i: e.memset(KT[i][:], 0.0), writes=bKT[i])
        stR = Ring([(self.ps[i], self.psb[i]) for i in (0, 1, 2, 5, 6)])
        poR = Ring([(self.ps[i], self.psb[i]) for i in (3, 4)])
        ptR = Ring(list(zip(PT, bPT)))
        item = [0]
        pending = []

        def run_item(kind, s, g, rs, P0, S):
            ib = item[0] % 2
            item[0] += 1
            qt, kt, vv, ost = QT[ib], KT[ib], VV[ib], OST[ib]
            bq, bk, bv_, bo = bQT[ib], bKT[ib], bVV[ib], bOST[ib]
            s0, Ls = self.s0[s], self.seqs[s]
            nsub = len(rs)
            nC = S // 128
            if kind == "d":
                d = DIL[g]
                n = Ls // d
                jq = 2 * g
                vcol = 256 * g
                ocol = 260 * g
                cb = s0 + rs[0] * n
                klo, khi = max(P0 - 64, 0), min(P0 + S + 64, n)
                koff = klo - (P0 - 64)
                KW_ = S + 128
                VT_ = nC + 1
                assert nsub == 1 or (P0 == 0 and S == n)
                assert nsub * S <= SEG and nsub * KW_ <= KTW and nsub * VT_ <= 40 and nsub * nC <= SEG // 128
            else:
                d = 1
                n = Ls
                jq = 6 + 2 * g
                vcol = 768 + 256 * g
                ocol = 780 + 260 * g
                cb = s0
                klo, khi = max(P0 - 256, 0), min(P0 + S + 256, n)
                koff = klo - (P0 - 256)
            P.dma("sp", lambda e: e.dma_start(out=qt[:, :, 0:nsub * S], in_=self.qs[jq:jq + 2, :, cb + P0:cb + P0 + nsub * S].rearrange("j p t -> p j t")), reads=[self.bqk], writes=[bq])
            if kind == "d":
                if nsub > 1 or koff > 0 or khi < P0 + S + 64:
                    P.op("pool", lambda e: e.memset(kt[:, :, 0:nsub * KW_], 0.0), writes=bk)
                for pr in range(2):
                    dstv = kt[:, pr, 0:nsub * KW_].rearrange("p (i c) -> p i c", c=KW_)[:, :, koff:koff + khi - klo]
                    srcv = bass.AP(self.ks.tensor, (jq + pr) * 128 * NT + cb + klo, [[NT, 128], [n, nsub], [1, khi - klo]])
                    P.dma("sp", lambda e, dstv=dstv, srcv=srcv: e.dma_start(out=dstv, in_=srcv), reads=[self.bqk], writes=[bk[pr]])
                for i, r in enumerate(rs):
                    for t_ in range(VT_):
                        p_lo = P0 + 128 * t_ - 64
                        a, b_ = 0, 128
                        if p_lo < 0:
                            a = 64
                        if p_lo + 128 > n:
                            b_ = 64
                        if a > 0 or b_ < 128:
                            za, zb = (0, 64) if a > 0 else (64, 128)
                            P.op("pool", lambda e, i=i, t_=t_, za=za, zb=zb: e.memset(vv[za:zb, i * VT_ + t_, :, :], 0.0), writes=[bv_[i * VT_ + t_]])
                        src = bass.AP(self.vs.tensor, (s0 + r + d * (p_lo + a)) * 1280 + vcol, [[d * 1280, b_ - a], [64, 4], [1, 64]])
                        P.dma("sp", lambda e, i=i, t_=t_, a=a, b_=b_, src=src: e.dma_start(out=vv[a:b_, i * VT_ + t_, :, 0:64], in_=src), reads=[self.bv], writes=[bv_[i * VT_ + t_]])
            else:
                P.dma("sp", lambda e: e.dma_start(out=kt[:, :, koff:koff + khi - klo], in_=self.ks[jq:jq + 2, :, cb + klo:cb + khi].rearrange("j p t -> p j t")), reads=[self.bqk], writes=bk)
                R0 = P0 // 64
                rows = Ls // 64
                rbase = max(R0 - 4, 0)
                rhi = min(R0 + S // 64 + 4, rows) - 2
                for rho in range(rbase, rhi + 1):
                    src = bass.AP(self.vs.tensor, (s0 + rho * 64) * 1280 + vcol, [[1280, 128], [64, 4], [1, 64]])
                    P.dma("sp", lambda e, rho=rho, src=src: e.dma_start(out=vv[:, rho - rbase, :, 0:64], in_=src), reads=[self.bv], writes=[bv_[rho - rbase]])
            units = []
            if kind == "d":
                for i in range(nsub):
                    for c in range(nC):
                        for h in range(4):
                            units.append((i, c, h, 0))
            else:
                for c in range(nC):
                    for hpair in range(2):
                        for half in range(2):
                            units.append((0, c, 2 * hpair, half))
                            units.append((0, c, 2 * hpair + 1, half))
            pobox = {}

            def emitS2(ua, ub):
                res = []
                ctx = []
                for u in (ua, ub):
                    i, c, h, half = u
                    pair, hp = h // 2, (h % 2) * 64
                    stp, bst = stR.next()
                    pt_, bpt = ptR.next()
                    ctx.append((i, c, h, half, pair, hp, stp, bst, pt_, bpt))
                if kind == "d":
                    for ab in range(2):
                        for (i, c, h, half, pair, hp, stp, bst, pt_, bpt) in ctx:
                            qc = i * S + 128 * c
                            kc_ = i * KW_ + 128 * c + 128 * ab
                            P.op("pe", lambda e, stp=stp, pair=pair, hp=hp, qc=qc, kc_=kc_, ab=ab: e.matmul(stp[:, 128 * ab:128 * ab + 128], lhsT=kt[hp:hp + 64, pair, kc_:kc_ + 128], rhs=qt[hp:hp + 64, pair, qc:qc + 128], start=(ab == 0), stop=(ab == 1)), reads=[bk[pair], bq], writes=[bst])
                    infos = [None, None]
                    ebs = [(EBd[:, 4 * g + cx[2], :], bEd) for cx in ctx]
                else:
                    infos = []
                    ebs = []
                    geo = []
                    for (i, c, h, half, pair, hp, stp, bst, pt_, bpt) in ctx:
                        i_ = R0 + 2 * c + half
                        rstart = min(max(i_ - 4, 0), rows - 8)
                        dr0 = rstart - i_
                        par = dr0 % 2
                        emin = -6 if par == 0 else -7
                        toff = 64 * ((dr0 - emin) // 2)
                        ebs.append((EBn[:, par, 4 * g + h, toff:toff + 256], bEn))
                        infos.append(rstart)
                        geo.append((rstart, (2 * c + half) * 64))
                    for p4 in range(4):
                        for cx, (rstart, qcol) in zip(ctx, geo):
                            (i, c, h, half, pair, hp, stp, bst, pt_, bpt) = cx
                            rho = rstart + 2 * p4
                            kcol = 64 * rho - (P0 - 256)
                            P.op("pe", lambda e, stp=stp, pair=pair, hp=hp, p4=p4, kcol=kcol, qcol=qcol: e.matmul(stp[:, 64 * p4:64 * p4 + 64], lhsT=kt[hp:hp + 64, pair, kcol:kcol + 128], rhs=qt[hp:hp + 64, pair, qcol:qcol + 64], start=(p4 == 0), stop=(p4 == 3)), reads=[bk[pair], bq], writes=[bst])
                for cx, info, (eb, beb) in zip(ctx, infos, ebs):
                    (i, c, h, half, pair, hp, stp, bst, pt_, bpt) = cx
                    P.op("act", lambda e, stp=stp, pt_=pt_: e.activation(out=pt_[:], in_=stp[:, 0:256], func=AF.Exp), reads=[bst], writes=[bpt])
                    P.op("dve", lambda e, pt_=pt_, eb=eb: e.tensor_tensor(out=pt_[:], in0=pt_[:], in1=eb, op=ALU.mult), reads=[bpt, beb], writes=[bpt])
                    res.append((pt_, bpt, info))
                return res

            def emitPV(u, sres):
                i, c, h, half = u
                pt_, bpt, info = sres
                oc = i * nC + c
                if h == 0 and half == 0:
                    pobox[oc] = poR.next()
                po, bpo = pobox[oc]
                if kind == "d":
                    vt = i * VT_ + c
                    P.op("pe", lambda e: e.matmul(po[:, 65 * h:65 * h + 65], lhsT=pt_[:, 0:128], rhs=vv[:, vt, h, :], start=True, stop=False), reads=[bpt, bv_[vt]], writes=[bpo])
                    P.op("pe", lambda e: e.matmul(po[:, 65 * h:65 * h + 65], lhsT=pt_[:, 128:256], rhs=vv[:, vt + 1, h, :], start=False, stop=True), reads=[bpt, bv_[vt + 1]], writes=[bpo])
                    last = (h == 3)
                else:
                    rstart = info
                    for p4 in range(4):
                        rho = rstart + 2 * p4
                        P.op("pe", lambda e, p4=p4, rho=rho: e.matmul(po[64 * half:64 * half + 64, 65 * h:65 * h + 65], lhsT=pt_[:, 64 * p4:64 * p4 + 64], rhs=vv[:, rho - rbase, h, :], start=(p4 == 0), stop=(p4 == 3)), reads=[bpt, bv_[rho - rbase]], writes=[bpo])
                    last = (h == 3 and half == 1)
                if last:
                    P.op("dve", lambda e: e.tensor_copy(out=ost[:, oc, :], in_=po[:, 0:260]), reads=[bpo], writes=[bo])

            def finish():
                for i, r in enumerate(rs):
                    tok0 = (s0 + r + d * P0) if kind == "d" else (s0 + P0)
                    dap = bass.AP(self.osr.tensor, tok0 * 1300 + ocol, [[d * 1300, 128], [128 * d * 1300, nC], [1, 260]])
                    P.dma("pool", lambda e, dap=dap, i=i: e.dma_start(out=dap, in_=ost[:, i * nC:(i + 1) * nC, :]), reads=[bo], writes=[self.bos])
                    if kind == "d":
                        for t_ in range(VT_):
                            p_lo = P0 + 128 * t_ - 64
                            if p_lo < 0:
                                P.op("pool", lambda e, i=i, t_=t_: e.memset(vv[0:64, i * VT_ + t_, :, 64:65], 1.0), writes=[bv_[i * VT_ + t_]])
                            if p_lo + 128 > n:
                                P.op("pool", lambda e, i=i, t_=t_: e.memset(vv[64:128, i * VT_ + t_, :, 64:65], 1.0), writes=[bv_[i * VT_ + t_]])

            for ui in range(0, len(units), 2):
                ua, ub = units[ui], units[ui + 1]
                ra, rb = emitS2(ua, ub)
                while len(pending) > 2:
                    pending.pop(0)()
                pending.append(lambda u=ua, sres=ra: emitPV(u, sres))
                pending.append(lambda u=ub, sres=rb: emitPV(u, sres))
            pending.append(finish)

        for s in range(NS):
            Ls = self.seqs[s]
            for g in range(3):
                d = DIL[g]
                n = Ls // d
                if n >= SEG:
                    for r in range(d):
                        for P0 in range(0, n, SEG):
                            run_item("d", s, g, [r], P0, SEG)
                else:
                    per = min(d, SEG // n, KTW // (n + 128), 40 // (n // 128 + 1))
                    for r0 in range(0, d, per):
                        run_item("d", s, g, list(range(r0, min(d, r0 + per))), 0, n)
            for g in range(2):
                for P0 in range(0, Ls, SEG):
                    run_item("n", s, g, [0], P0, min(SEG, Ls))
        while pending:
            pending.pop(0)()
        P.barrier()


def _off(t):
    return t.manual_sbuf_range[0]


_CACHE = {}


def _get_nc(seqs):
    key = tuple(seqs)
    if key not in _CACHE:
        _CACHE[key] = Builder(seqs).build()
    return _CACHE[key]


def kernel(x_prompt, x_sample, c_prompt, c_sample, norm1_g, norm2_g, w_mod, b_mod, w_in, q_norm_g, k_norm_g,
           rel_bias, rpb, w_gate, b_gate, w_up_a, w_up_b, w_o, w_ff1, w_ff2):
    f = lambda a: np.ascontiguousarray(np.asarray(a, dtype=np.float32))
    x_prompt, x_sample, c_prompt, c_sample = f(x_prompt), f(x_sample), f(c_prompt), f(c_sample)
    ncores = 8
    Bp, Lp, _ = x_prompt.shape
    Bs, Ls, _ = x_sample.shape
    pp, sp_ = Bp // ncores, Bs // ncores
    seqs = [Lp] * pp + [Ls] * sp_
    nc = _get_nc(seqs)
    shared = {"norm1_g": f(norm1_g), "norm2_g": f(norm2_g), "w_mod": f(w_mod), "b_mod": f(b_mod), "w_in": f(w_in),
              "q_norm_g": f(q_norm_g), "k_norm_g": f(k_norm_g), "rel_bias": f(rel_bias), "rpb": f(rpb),
              "w_gate": f(w_gate), "b_gate": f(b_gate), "w_up_a": f(w_up_a), "w_up_b": f(w_up_b), "w_o": f(w_o),
              "w_ff1": f(w_ff1), "w_ff2": f(w_ff2)}
    for k, v in make_consts().items():
        shared["c_" + k] = v
    in_maps = []
    for c in range(ncores):
        xs = [x_prompt[c * pp + i] for i in range(pp)] + [x_sample[c * sp_ + i] for i in range(sp_)]
        cs = [c_prompt[c * pp + i] for i in range(pp)] + [c_sample[c * sp_ + i] for i in range(sp_)]
        m = dict(shared)
        m["x"] = np.ascontiguousarray(np.concatenate(xs, axis=0))
        m["c"] = np.ascontiguousarray(np.stack(cs, axis=0))
        in_maps.append(m)
    res = run_bass_kernel_spmd(nc, in_maps, core_ids=list(range(ncores)))
    yp = np.empty_like(x_prompt)
    ys = np.empty_like(x_sample)
    for c in range(ncores):
        y = res.results[c]["y"]
        o = 0
        for i in range(pp):
            yp[c * pp + i] = y[o:o + Lp]
            o += Lp
        for i in range(sp_):
            ys[c * sp_ + i] = y[o:o + Ls]
            o += Ls
    return (yp, ys)
```

```python
import numpy as np
from contextlib import ExitStack
import concourse.bass as bass
import concourse.mybir as mybir
from concourse.bass_utils import run_bass_kernel_spmd

F32 = mybir.dt.float32
BF16 = mybir.dt.bfloat16
AF = mybir.ActivationFunctionType
ALU = mybir.AluOpType

D = 1024
DEPTH = 2
NH = 20
HD = 64
DIL = (1, 4, 16)
D_FF = 4096
EPS = 1e-6
NEGV = -30000.0
GRID_W = 64
T = 512
SEG = 2048

SAME_ENG_SYNC = True
DEBUG_SCRATCH = False
USE_PS_BITCAST = True
EPOCH = 30000
DMA_RING = 16


class Buf:
    __slots__ = ("name", "w", "rs", "rd")

    def __init__(self, name=""):
        self.name = name
        self.w = None
        self.rs = {}
        self.rd = []


class Op:
    __slots__ = ("eng", "fn", "waits", "signal", "sem", "val", "is_dma", "idx")

    def __init__(self, eng, fn, is_dma):
        self.eng = eng
        self.fn = fn
        self.waits = []
        self.signal = False
        self.sem = None
        self.val = 0
        self.is_dma = is_dma
        self.idx = 0


class Prog:
    ENGS = ("pe", "act", "dve", "pool", "sp")

    def __init__(self, nc, stack):
        self.nc = nc
        self.stack = stack
        self.ops = {e: [] for e in self.ENGS}
        self.seen = {e: {f: -1 for f in self.ENGS} for e in self.ENGS}
        self.seen_dma = {e: {} for e in self.ENGS}
        self.rings = {}
        self.nsem = 0
        self.pend = {e: [] for e in self.ENGS}

    def new_sem(self, name):
        self.nsem += 1
        return self.stack.enter_context(self.nc.semaphore(f"{name}_{self.nsem}"))

    def _need(self, op, P, raw):
        E = op.eng
        if P is op:
            return
        if P.is_dma:
            sd = self.seen_dma[E]
            key = id(P.sem)
            if sd.get(key, -1) >= P.val:
                return
            sd[key] = P.val
            op.waits.append(P)
            return
        F = P.eng
        if F == E:
            if E == "pe" or E == "sp" or not SAME_ENG_SYNC or not raw:
                return
        if P.idx <= self.seen[E][F]:
            return
        self.seen[E][F] = P.idx
        P.signal = True
        op.waits.append(P)

    def _add(self, op, reads, writes):
        E = op.eng
        lst = self.ops[E]
        op.idx = len(lst)
        if self.pend[E]:
            for Pp in self.pend[E]:
                self._need(op, Pp, True)
            self.pend[E] = []
        for b in reads:
            if b.w is not None:
                self._need(op, b.w, True)
        for b in writes:
            if b.w is not None:
                self._need(op, b.w, True)
            for r in b.rs.values():
                self._need(op, r, False)
            for r in b.rd:
                self._need(op, r, False)
        for b in writes:
            b.w = op
            b.rs = {}
            b.rd = []
        for b in reads:
            if op.is_dma:
                b.rd.append(op)
            else:
                b.rs[E] = op
        lst.append(op)
        return op

    def op(self, eng, fn, reads=(), writes=()):
        return self._add(Op(eng, fn, False), reads, writes)

    def dma(self, q, fn, reads=(), writes=()):
        op = Op(q, fn, True)
        ring = self.rings.get(q)
        if ring is None:
            ring = {"n": 0, "slots": [None] * DMA_RING}
            self.rings[q] = ring
        s = ring["n"] % DMA_RING
        ring["n"] += 1
        slot = ring["slots"][s]
        if slot is None:
            slot = {"sem": self.new_sem(f"d{q}{s}"), "val": 0, "last": None}
            ring["slots"][s] = slot
        if slot["last"] is not None:
            self._need(op, slot["last"], False)
        if slot["val"] + 16 > EPOCH:
            slot["sem"] = self.new_sem(f"d{q}{s}")
            slot["val"] = 0
        slot["val"] += 16
        op.sem = slot["sem"]
        op.val = slot["val"]
        slot["last"] = op
        return self._add(op, reads, writes)

    def barrier(self):
        lasts = []
        for e in self.ENGS:
            for o in reversed(self.ops[e]):
                if not o.is_dma:
                    lasts.append(o)
                    break
        for ring in self.rings.values():
            for slot in ring["slots"]:
                if slot is not None and slot["last"] is not None:
                    lasts.append(slot["last"])
        for e in self.ENGS:
            self.pend[e] = list(lasts)

    def emit(self):
        nc = self.nc
        for e in self.ENGS:
            cnt = 0
            sems = []
            for o in self.ops[e]:
                if o.is_dma or not o.signal:
                    continue
                ep = cnt // EPOCH
                while len(sems) <= ep:
                    sems.append(self.new_sem(f"e{e}"))
                o.sem = sems[ep]
                o.val = cnt % EPOCH + 1
                cnt += 1
        fin = []
        for ring in self.rings.values():
            for slot in ring["slots"]:
                if slot is not None and slot["last"] is not None:
                    fin.append(slot["last"])
        engmap = {"pe": "tensor", "act": "scalar", "dve": "vector", "pool": "gpsimd", "sp": "sync"}
        with nc.Block() as block:
            for e in self.ENGS:
                ops = self.ops[e]

                def body(eng, ops=ops, e=e):
                    for o in ops:
                        for Pw in o.waits:
                            eng.wait_ge(Pw.sem, Pw.val)
                        ins = o.fn(eng)
                        if o.is_dma:
                            ins.then_inc(o.sem, 16)
                        elif o.signal:
                            ins.then_inc(o.sem, 1)
                    if e == "sp":
                        for Pw in fin:
                            eng.wait_ge(Pw.sem, Pw.val)

                getattr(block, engmap[e])(body)


def t5_bucket(rel):
    nb = 16
    max_exact = 8
    ret = (rel > 0).astype(np.int32) * nb
    n = np.abs(rel)
    large = max_exact + (np.log(np.maximum(n, 1) / max_exact) / np.log(1024 / max_exact) * (nb - max_exact)).astype(np.int32)
    large = np.minimum(large, nb - 1)
    return (ret + np.where(n < max_exact, n, large)).astype(np.int32)


def make_consts():
    c = {}
    c["ident"] = np.eye(128, dtype=np.float32)
    c["antiid"] = np.eye(128, dtype=np.float32)[::-1].copy()
    bo = np.zeros((128, 128), np.float32)
    bo[:64, :64] = 1.0 / 64
    bo[64:, 64:] = 1.0 / 64
    c["blockones"] = bo
    c["meanones"] = np.full((128, 128), 1.0 / 1024, np.float32)
    oh = np.zeros((3, 33, 384), np.float32)
    w = np.arange(384)
    j = 191 - w
    for g, d in enumerate(DIL):
        ok = np.abs(j) <= 64
        b = t5_bucket(d * j)
        for ww in range(384):
            if ok[ww]:
                oh[g, b[ww], ww] = 1.0
            else:
                oh[g, 32, ww] = 1.0
    c["ohaug"] = oh
    sel = np.zeros((31, 128), np.float32)
    for dc in range(31):
        sel[dc, 78 - dc] = 1.0
    c["sel"] = sel
    cm = np.zeros((128, 64), np.float32)
    for ci in range(2):
        for cj in range(64):
            kj = 63 - cj
            for qj in range(64):
                cs = min(max(qj - 8, 0), 48)
                if not (cs <= kj < cs + 16):
                    cm[ci * 64 + cj, qj] = NEGV
    c["cmask"] = cm
    return c


WDEF = {
    "w_in": (1024, 3840, 8, 512),
    "w_gate": (1024, 2048, 8, 512),
    "w_up_a": (256, 1024, 2, 1024),
    "w_up_b": (512, 1024, 4, 1024),
    "w_o": (1024, 1024, 8, 512),
    "w_ff1": (1024, 4096, 8, 512),
    "w_ff2": (4096, 1024, 32, 128),
}


def nblk(name):
    K, N, KC, NC = WDEF[name]
    return (N + NC - 1) // NC


class Arena:
    def __init__(self, nc, lo, hi):
        self.nc = nc
        self.lo = lo
        self.hi = hi
        self.p = lo
        self.n = 0

    def alloc(self, shape, dtype):
        esz = 4 if dtype == F32 else 2
        nb = esz
        for s in shape[1:]:
            nb *= s
        off = (self.p + 63) // 64 * 64
        assert off + nb <= self.hi, f"SBUF arena overflow: {off + nb} > {self.hi}"
        self.p = off + nb
        self.n += 1
        return self.nc.alloc_sbuf_tensor_at(f"ar{self.n}", list(shape), dtype, offset=off)

    def mark(self):
        return self.p

    def reset(self, m):
        self.p = m


class Ring:
    def __init__(self, items):
        self.items = items
        self.i = 0

    def next(self):
        it = self.items[self.i % len(self.items)]
        self.i += 1
        return it


class Builder:
    def __init__(self, seqs, depth=DEPTH):
        self.seqs = list(seqs)
        self.NS = len(seqs)
        self.NT = sum(seqs)
        self.depth = depth
        self.s0 = [sum(seqs[:i]) for i in range(self.NS)]

    def build(self):
        nc = bass.Bass("TRN2", target_bir_lowering=False)
        self.nc = nc
        NT, NS, L_ = self.NT, self.NS, self.depth
        di = lambda n, s, dt=F32: nc.dram_tensor(n, list(s), dt, kind="ExternalInput").ap()
        dsc = lambda n, s, dt: nc.dram_tensor(n, list(s), dt, kind=("ExternalOutput" if DEBUG_SCRATCH else "Internal")).ap()
        self.x = di("x", [NT, D])
        self.c = di("c", [NS, D])
        self.norm1_g = di("norm1_g", [DEPTH, D])
        self.norm2_g = di("norm2_g", [DEPTH, D])
        self.w_mod = di("w_mod", [DEPTH, D, 6 * D])
        self.b_mod = di("b_mod", [DEPTH, 6 * D])
        self.q_norm_g = di("q_norm_g", [DEPTH, NH, HD])
        self.k_norm_g = di("k_norm_g", [DEPTH, NH, HD])
        self.rel_bias = di("rel_bias", [32, 12])
        self.rpb = di("rpb", [DEPTH, 8, 15, 31])
        self.b_gate = di("b_gate", [DEPTH, 2 * D])
        self.wsrc = {}
        for n, (K, N, KC, NC) in WDEF.items():
            self.wsrc[n] = di(n, [DEPTH, K, N])
        cs = make_consts()
        self.cin = {k: di("c_" + k, v.shape) for k, v in cs.items()}
        self.y = nc.dram_tensor("y", [NT, D], F32, kind="ExternalOutput").ap()
        self.wq = {n: dsc("wq_" + n, [DEPTH, nblk(n), 128, 4096], BF16) for n in WDEF}
        self.xs = dsc("xs", [8, 128, NT], F32)
        self.qs = dsc("qs", [10, 128, NT], BF16)
        self.ks = dsc("ks", [10, 128, NT], BF16)
        self.vs = dsc("vs", [NT, 1280], BF16)
        self.osr = dsc("osr", [NT, 1300], F32)
        self.rfull = dsc("rfull", [12, 384], BF16)
        self.pnb = dsc("pnb", [DEPTH, 120, 128], BF16)
        with ExitStack() as st:
            self.P = Prog(nc, st)
            self.ar = Arena(nc, 16640, 229376 - 64)
            self.ps = [nc.alloc_psum_tensor(f"psb{i}", [128, 512], F32) for i in range(7)]
            self.psb = [Buf(f"ps{i}") for i in range(8)]
            self.ps16 = nc.alloc_psum_tensor("psb7", [128, 1024], BF16)
            self.prologue()
            for l in range(self.depth):
                self.dense_pass(l)
                self.attention(l)
            self.dense_pass(self.depth)
            self.P.emit()
        return nc

    def prologue(self):
        nc, P, ar = self.nc, self.P, self.ar
        NS = self.NS
        self.ident = ar.alloc([128, 128], F32)
        self.identb = ar.alloc([128, 128], BF16)
        self.J = ar.alloc([128, 128], BF16)
        self.bones = ar.alloc([128, 128], BF16)
        self.mones = ar.alloc([128, 128], BF16)
        self.tab = ar.alloc([128, 128], F32)
        self.modT = ar.alloc([128, DEPTH * 48 * NS], F32)
        self.G = ar.alloc([128, DEPTH * 2 * 8 * NS], F32)
        self.qg8 = ar.alloc([128, DEPTH * 10], F32)
        self.cmask = ar.alloc([128, 64], BF16)
        bC = Buf("consts")
        self.bC = bC
        mk = ar.mark()
        tmpf = ar.alloc([128, 128], F32)
        btmp = Buf()
        P.dma("sp", lambda e: e.dma_start(out=self.ident[:], in_=self.cin["ident"][:, :]), writes=[bC])
        for src, dst in (("ident", self.identb), ("antiid", self.J), ("blockones", self.bones), ("meanones", self.mones)):
            P.dma("sp", lambda e, src=src: e.dma_start(out=tmpf[:], in_=self.cin[src][:, :]), writes=[btmp])
            P.op("dve", lambda e, dst=dst: e.tensor_copy(out=dst[:], in_=tmpf[:]), reads=[btmp], writes=[bC])
        ptab = ar.alloc([128, 128], F32)
        bpt = Buf()
        P.op("dve", lambda e: e.memset(ptab[:], 0.0), writes=[bpt])
        rows = []
        r = 0
        self.col = {}

        def addrows(name, ap2d, n):
            nonlocal r
            self.col[name] = r
            P.dma("sp", lambda e, r=r: e.dma_start(out=ptab[r:r + n, :], in_=ap2d), writes=[bpt])
            r += n

        addrows("c", self.c.rearrange("s (k p) -> (s k) p", p=128), NS * 8)
        addrows("n1", self.norm1_g.rearrange("l (k p) -> (l k) p", p=128), DEPTH * 8)
        addrows("n2", self.norm2_g.rearrange("l (k p) -> (l k) p", p=128), DEPTH * 8)
        addrows("qg", self.q_norm_g.rearrange("l (j a) e -> (l j) (a e)", a=2), DEPTH * 10)
        addrows("kg", self.k_norm_g.rearrange("l (j a) e -> (l j) (a e)", a=2), DEPTH * 10)
        addrows("bg", self.b_gate.rearrange("l (k p) -> (l k) p", p=128), DEPTH * 16)
        assert r <= 128
        pt = self.ps[0]
        P.op("pe", lambda e: e.transpose(pt[:, 0:128], ptab[:], self.ident[:]), reads=[bpt, bC], writes=[self.psb[0]])
        P.op("dve", lambda e: e.tensor_copy(out=self.tab[:], in_=pt[:, 0:128]), reads=[self.psb[0]], writes=[bC])
        cq = self.col["qg"]
        P.op("dve", lambda e: e.tensor_scalar(out=self.qg8[:], in0=self.tab[:, cq:cq + DEPTH * 10], scalar1=0.125, scalar2=None, op0=ALU.mult), reads=[bC], writes=[bC])
        bmT = ar.alloc([128, 96], F32)
        ptab2 = ar.alloc([128, 128], F32)
        bpt2 = Buf()
        P.dma("sp", lambda e: e.dma_start(out=ptab2[0:96, :], in_=self.b_mod.rearrange("l (k p) -> (l k) p", p=128)), writes=[bpt2])
        P.op("pe", lambda e: e.transpose(pt[:, 128:224], ptab2[0:96, :], self.ident[0:96, 0:96]), reads=[bpt2, bC], writes=[self.psb[0]])
        bbm = Buf()
        P.op("dve", lambda e: e.tensor_copy(out=bmT[:], in_=pt[:, 128:224]), reads=[self.psb[0]], writes=[bbm])
        siluT = ar.alloc([128, NS * 8], F32)
        bsl = Buf()
        cc = self.col["c"]
        P.op("act", lambda e: e.activation(out=siluT[:], in_=self.tab[:, cc:cc + NS * 8], func=AF.Silu), reads=[bC], writes=[bsl])
        wm = [ar.alloc([128, 8, 512], F32) for _ in range(2)]
        bwm = [Buf(), Buf()]
        bmod = Buf("mod")
        self.bmod = bmod
        it = 0
        for l in range(self.depth):
            for blk in range(12):
                wt, bw = wm[it % 2], bwm[it % 2]
                it += 1
                P.dma("sp", lambda e, wt=wt, l=l, blk=blk: e.dma_start(out=wt[:], in_=self.w_mod[l].rearrange("(k p) n -> p k n", p=128)[:, :, blk * 512:(blk + 1) * 512]), writes=[bw])
                for c4 in range(4):
                    ccol = blk * 4 + c4
                    pb = 1 + (ccol % 2)
                    pst = self.ps[pb]
                    for kc in range(8):
                        P.op("pe", lambda e, wt=wt, kc=kc, c4=c4, pst=pst: e.matmul(pst[:, 0:NS], lhsT=wt[:, kc, c4 * 128:(c4 + 1) * 128], rhs=siluT[:].rearrange("p (s k) -> p k s", k=8)[:, kc, :], start=(kc == 0), stop=(kc == 7)), reads=[bw, bsl], writes=[self.psb[pb]])
                    o0 = (l * 48 + ccol) * NS
                    P.op("dve", lambda e, pst=pst, o0=o0, l=l, ccol=ccol: e.tensor_scalar(out=self.modT[:, o0:o0 + NS], in0=pst[:, 0:NS], scalar1=bmT[:, l * 48 + ccol:l * 48 + ccol + 1], scalar2=None, op0=ALU.add), reads=[self.psb[pb], bbm], writes=[bmod])
        for l in range(self.depth):
            for wh in range(2):
                for kc in range(8):
                    mi = (l * 48 + (1 if wh == 0 else 4) * 8 + kc) * NS
                    gi = ((l * 2 + wh) * 8 + kc) * NS
                    ncol = self.col["n1" if wh == 0 else "n2"] + l * 8 + kc
                    P.op("dve", lambda e, mi=mi, gi=gi, ncol=ncol: e.tensor_scalar(out=self.G[:, gi:gi + NS], in0=self.modT[:, mi:mi + NS], scalar1=1.0, scalar2=self.tab[:, ncol:ncol + 1], op0=ALU.add, op1=ALU.mult), reads=[bmod, bC], writes=[bmod])
        self.bwq = {}
        for l in range(self.depth):
            for n in ("w_in", "w_gate", "w_up_a", "w_up_b", "w_o", "w_ff1", "w_ff2"):
                K, N, KC, NC = WDEF[n]
                for b in range(nblk(n)):
                    ncol = min(NC, N - b * NC)
                    bb = Buf()
                    self.bwq[(n, l, b)] = bb
                    src = self.wsrc[n][l].rearrange("(k p) n -> p k n", p=128)[:, :, b * NC:b * NC + ncol]
                    dst = self.wq[n][l, b][:, 0:KC * ncol].rearrange("p (k n) -> p k n", k=KC)
                    P.dma("pool", lambda e, src=src, dst=dst: e.dma_start(out=dst, in_=src), writes=[bb])
        taug = ar.alloc([33, 12], F32)
        btg = Buf()
        P.op("dve", lambda e: e.memset(taug[32:33, :], NEGV), writes=[btg])
        P.dma("sp", lambda e: e.dma_start(out=taug[0:32, :], in_=self.rel_bias[:, :]), writes=[btg])
        oh = ar.alloc([33, 3, 384], F32)
        boh = Buf()
        P.dma("sp", lambda e: e.dma_start(out=oh[:], in_=self.cin["ohaug"].rearrange("g b w -> b g w")), writes=[boh])
        rst = ar.alloc([4, 3, 384], BF16)
        brst = Buf()
        for g in range(3):
            pst = self.ps[3]
            P.op("pe", lambda e, g=g, pst=pst: e.matmul(pst[0:4, 0:384], lhsT=taug[:, 4 * g:4 * g + 4], rhs=oh[:, g, :], start=True, stop=True), reads=[btg, boh], writes=[self.psb[3]])
            P.op("dve", lambda e, g=g, pst=pst: e.tensor_copy(out=rst[:, g, :], in_=pst[0:4, 0:384]), reads=[self.psb[3]], writes=[brst])
        self.brf = Buf("rfull")
        P.dma("pool", lambda e: e.dma_start(out=self.rfull.rearrange("(g h) w -> h g w", h=4), in_=rst[:]), reads=[brst], writes=[self.brf])
        self.bpnb = Buf("pnb")
        sel = ar.alloc([31, 128], F32)
        bsel = Buf()
        P.dma("sp", lambda e: e.dma_start(out=sel[:], in_=self.cin["sel"][:, :]), writes=[bsel])
        for l in range(self.depth):
            rc = ar.alloc([120, 31], F32)
            brc = Buf()
            P.dma("sp", lambda e, l=l, rc=rc: e.dma_start(out=rc[:], in_=self.rpb[l].rearrange("h r c -> (h r) c")), writes=[brc])
            pst = self.ps[4]
            P.op("pe", lambda e, rc=rc, pst=pst: e.transpose(pst[0:31, 0:120], rc[:], self.ident[0:120, 0:120]), reads=[brc, bC], writes=[self.psb[4]])
            xr = ar.alloc([31, 120], F32)
            bxr = Buf()
            P.op("dve", lambda e, xr=xr, pst=pst: e.tensor_copy(out=xr[:], in_=pst[0:31, 0:120]), reads=[self.psb[4]], writes=[bxr])
            P.op("pe", lambda e, xr=xr, pst=pst: e.matmul(pst[0:120, 128:256], lhsT=xr[:], rhs=sel[:], start=True, stop=True), reads=[bxr, bsel], writes=[self.psb[4]])
            pn = ar.alloc([120, 128], BF16)
            bpn = Buf()
            P.op("dve", lambda e, pn=pn, pst=pst: e.tensor_copy(out=pn[:], in_=pst[0:120, 128:256]), reads=[self.psb[4]], writes=[bpn])
            P.dma("pool", lambda e, l=l, pn=pn: e.dma_start(out=self.pnb[l], in_=pn[:]), reads=[bpn], writes=[self.bpnb])
        P.dma("pool", lambda e: e.dma_start(out=self.cmask[:], in_=self.cin["cmask"][:, :]), writes=[bC])
        P.barrier()
        self.base_mark = mk
        self.bxs = [[Buf() for _ in range(self.NT // T)] for _ in range(1)][0]
        self.bqk = Buf("qk")
        self.bv = Buf("v")
        self.bos = Buf("os")

    def dense_pass(self, lp):
        nc, P, ar = self.nc, self.P, self.ar
        NS, NT = self.NS, self.NT
        ar.reset(self.base_mark)
        doC = lp > 0
        doA = lp < self.depth
        lc = lp - 1
        la = lp
        NSUB = T // 128
        xTs = [ar.alloc([128, 8, T], F32) for _ in range(2)]
        hTs = [ar.alloc([128, 8, T], BF16) for _ in range(2)]
        rstd = ar.alloc([128, T], F32)
        big = ar.alloc([128, 32, T], BF16)
        iost = nc.alloc_sbuf_tensor_at("iost%d" % lp, [128, 4, 1024], F32, offset=_off(big) + 8 * T * 4)
        tmps = [ar.alloc([128, T], F32) for _ in range(3)]
        gT = ar.alloc([128, 16, T], BF16)
        sq = ar.alloc([128, 8, T], BF16)
        oraw = [ar.alloc([128, 1300], F32) for _ in range(2)]
        otoks = [ar.alloc([128, 768], BF16) for _ in range(NSUB)]
        osum = ar.alloc([128, 260], F32)
        rden = ar.alloc([128, 12], F32)
        oT = ar.alloc([128, 6, T], BF16)
        t12 = [ar.alloc([128, T], BF16) for _ in range(4)]
        mixT = ar.alloc([128, 8, T], BF16)
        rl = [ar.alloc([128, T], BF16) for _ in range(2)]
        sq2 = [ar.alloc([128, T], BF16) for _ in range(2)]
        rs2 = [ar.alloc([128, T], F32) for _ in range(2)]
        qst = [ar.alloc([128, T], BF16) for _ in range(3)]
        vst = [ar.alloc([128, 512], BF16) for _ in range(3)]
        NW = 5
        wsl = [ar.alloc([128, 4096], BF16) for _ in range(NW)]
        bxTs = [[Buf() for _ in range(8)] for _ in range(2)]
        bhTs = [[Buf() for _ in range(8)] for _ in range(2)]
        brstd = Buf()
        bbig = [Buf() for _ in range(32)]
        btmps = [Buf() for _ in range(3)]
        bgT = [Buf() for _ in range(16)]
        bsq = [Buf() for _ in range(8)]
        boraw = [Buf(), Buf()]
        botoks = [Buf() for _ in range(NSUB)]
        bosum, brden = Buf(), Buf()
        boT = Buf()
        bt12 = [Buf() for _ in range(4)]
        bmix = [Buf() for _ in range(8)]
        brl = [Buf(), Buf()]
        bsq2 = [Buf(), Buf()]
        brs2 = [Buf(), Buf()]
        bqst = [Buf() for _ in range(3)]
        bvst = [Buf(), Buf(), Buf()]
        bws = [Buf() for _ in range(NW)]
        bC, bmod = self.bC, self.bmod
        tab, modT, G = self.tab, self.modT, self.G
        ntiles = NT // T
        R = {}

        def mkrings():
            R["acc"] = Ring([(self.ps[i], self.psb[i]) for i in (0, 1, 2, 3, 4, 5)] + [(self.ps16[:, :].bitcast(F32), self.psb[7])])
            R["stt"] = Ring([(self.ps[i], self.psb[i]) for i in (6,)])
            R["t12"] = Ring(list(zip(t12, bt12)))
            R["rl"] = Ring(list(zip(rl, brl)))
            R["sq2"] = Ring(list(zip(sq2, bsq2)))
            R["rs2"] = Ring(list(zip(rs2, brs2)))
            R["qst"] = Ring(list(zip(qst, bqst)))
            R["oraw"] = Ring(list(zip(oraw, boraw)))
            R["tmp"] = Ring(list(zip(tmps, btmps)))
            R["vst"] = Ring(list(zip(vst, bvst)))

        order = []
        wstate = {"emit": 0, "use": 0, "dry": True}

        def wget(n, l, b):
            K_, N_, KC, NC = WDEF[n]
            ncol = min(NC, N_ - b * NC)
            if wstate["dry"]:
                order.append((n, l, b))
                return None, None, ncol
            while wstate["emit"] < len(order) and wstate["emit"] < wstate["use"] + NW - 2:
                i = wstate["emit"]
                n2, l2, b2 = order[i]
                sl, bs = wsl[i % NW], bws[i % NW]
                P.dma("sp", lambda e, sl=sl, n2=n2, l2=l2, b2=b2: e.dma_start(out=sl[:], in_=self.wq[n2][l2, b2]), reads=[self.bwq[(n2, l2, b2)]], writes=[bs])
                wstate["emit"] += 1
            i = wstate["use"]
            wstate["use"] += 1
            assert order[i] == (n, l, b), (order[i], (n, l, b))
            return wsl[i % NW][:, 0:KC * ncol].rearrange("p (k n) -> p k n", k=KC), bws[i % NW], ncol

        def op(*a, **k):
            if not wstate["dry"]:
                P.op(*a, **k)

        def dma(*a, **k):
            if not wstate["dry"]:
                P.dma(*a, **k)

        def mcol(l, j, kc, s):
            o = (l * 48 + j * 8 + kc) * NS + s
            return modT[:, o:o + 1]

        def gcol(l, wh, kc, s):
            o = ((l * 2 + wh) * 8 + kc) * NS + s
            return G[:, o:o + 1]

        def tinfo(ti):
            t0 = ti * T
            s = max(i for i in range(NS) if self.s0[i] <= t0)
            return t0, s, self.s0[s], self.seqs[s]

        def norm_sq(xi, kc):
            xT, bxT = xTs[xi], bxTs[xi]
            op("act", lambda e, kc=kc: e.activation(out=sq[:, kc, :], in_=xT[:, kc, :], func=AF.Square), reads=[bxT[kc]], writes=[bsq[kc]])

        def norm_fin(l, wh, s, xi, hi):
            xT, bxT, hT, bhT = xTs[xi], bxTs[xi], hTs[hi], bhTs[hi]
            pst, bp = R["stt"].next()
            for kc in range(8):
                op("pe", lambda e, kc=kc, pst=pst: e.matmul(pst[:, 0:T], lhsT=self.mones[:], rhs=sq[:, kc, :], start=(kc == 0), stop=(kc == 7)), reads=[bsq[kc], bC], writes=[bp])
            op("act", lambda e, pst=pst: e.activation(out=rstd[:], in_=pst[:, 0:T], func=AF.Ln, bias=EPS, scale=1.0), reads=[bp], writes=[brstd])
            op("act", lambda e: e.activation(out=rstd[:], in_=rstd[:], func=AF.Exp, scale=-0.5), reads=[brstd], writes=[brstd])
            shj = 0 if wh == 0 else 3
            for kc in range(8):
                tm, btm = R["tmp"].next()
                op("dve", lambda e, kc=kc, tm=tm: e.scalar_tensor_tensor(out=tm[:], in0=xT[:, kc, :], scalar=gcol(l, wh, kc, s), in1=rstd[:], op0=ALU.mult, op1=ALU.mult), reads=[bxT[kc], brstd, bmod], writes=[btm])
                op("act", lambda e, kc=kc, tm=tm: e.activation(out=hT[:, kc, :], in_=tm[:], func=AF.Identity, bias=mcol(l, shj, kc, s), scale=1.0), reads=[btm, bmod], writes=[bhT[kc]])

        def norm(l, wh, s, xi, hi):
            for kc in range(8):
                norm_sq(xi, kc)
            norm_fin(l, wh, s, xi, hi)

        def load(ti, only=None):
            t0, s, s0, Ls = tinfo(ti)
            xi = ti % 2
            xT, bxT = xTs[xi], bxTs[xi]
            if lp == 0:
                if only is None or only == "dma":
                    for sub in range(NSUB):
                        dma("sp", lambda e, sub=sub: e.dma_start(out=iost[:, sub, :], in_=self.x[t0 + sub * 128:t0 + (sub + 1) * 128, :]), writes=bbig[16 + 4 * sub:20 + 4 * sub])
                for kc in (range(8) if only is None else ([] if only == "dma" else [only])):
                    pst, bp = R["acc"].next()
                    for sub in range(NSUB):
                        op("pe", lambda e, kc=kc, sub=sub, pst=pst: e.transpose(pst[:, sub * 128:(sub + 1) * 128], iost[:, sub, kc * 128:(kc + 1) * 128], self.ident[:]), reads=bbig[16 + 4 * sub:20 + 4 * sub] + [bC], writes=[bp])
                    op("dve", lambda e, kc=kc, pst=pst: e.tensor_copy(out=xT[:, kc, :], in_=pst[:, 0:T]), reads=[bp], writes=[bxT[kc]])
            else:
                dma("sp", lambda e: e.dma_start(out=xT[:], in_=self.xs[:, :, t0:t0 + T].rearrange("k p t -> p k t")), reads=[self.bxs[ti]], writes=bxT)

        def omerge(ti):
            t0, s, s0, Ls = tinfo(ti)
            for sub in range(NSUB):
                otok, botok = otoks[sub], botoks[sub]
                orw, bor = R["oraw"].next()
                dma("sp", lambda e, orw=orw, sub=sub: e.dma_start(out=orw[:], in_=self.osr[t0 + sub * 128:t0 + (sub + 1) * 128, :]), reads=[self.bos], writes=[bor])
                op("pool", lambda e, orw=orw: e.tensor_tensor(out=osum[:], in0=orw[:, 0:260], in1=orw[:, 260:520], op=ALU.add), reads=[bor], writes=[bosum])
                op("pool", lambda e, orw=orw: e.tensor_tensor(out=osum[:], in0=osum[:], in1=orw[:, 520:780], op=ALU.add), reads=[bor, bosum], writes=[bosum])
                op("dve", lambda e: e.reciprocal(out=rden[:, 0:4], in_=osum[:].rearrange("p (h e) -> p h e", e=65)[:, :, 64]), reads=[bosum], writes=[brden])
                op("dve", lambda e, orw=orw: e.reciprocal(out=rden[:, 4:12], in_=orw[:, 780:1300].rearrange("p (h e) -> p h e", e=65)[:, :, 64]), reads=[bor], writes=[brden])
                op("dve", lambda e, otok=otok: e.tensor_tensor(out=otok[:, 0:256].rearrange("p (h e) -> p h e", e=64), in0=osum[:].rearrange("p (h e) -> p h e", e=65)[:, :, 0:64], in1=rden[:, 0:4].unsqueeze(2).to_broadcast([128, 4, 64]), op=ALU.mult), reads=[bosum, brden], writes=[botok])
                op("dve", lambda e, orw=orw, otok=otok: e.tensor_tensor(out=otok[:, 256:768].rearrange("p (h e) -> p h e", e=64), in0=orw[:, 780:1300].rearrange("p (h e) -> p h e", e=65)[:, :, 0:64], in1=rden[:, 4:12].unsqueeze(2).to_broadcast([128, 8, 64]), op=ALU.mult), reads=[bor, brden], writes=[botok])

        def gate(ti):
            t0, s, s0, Ls = tinfo(ti)
            l = lc
            hT, bhT = hTs[0], bhTs[0]
            for b in range(4):
                wt, bw, _ = wget("w_gate", l, b)
                for c4 in range(4):
                    cc = b * 4 + c4
                    pst, bp = R["acc"].next()
                    for kc in range(8):
                        op("pe", lambda e, wt=wt, kc=kc, c4=c4, pst=pst: e.matmul(pst[:, 0:T], lhsT=wt[:, kc, c4 * 128:(c4 + 1) * 128], rhs=hT[:, kc, :], start=(kc == 0), stop=(kc == 7)), reads=[bw, bhT[kc]], writes=[bp])
                    bcol = self.col["bg"] + l * 16 + cc
                    op("act", lambda e, cc=cc, pst=pst, bcol=bcol: e.activation(out=gT[:, cc, :], in_=pst[:, 0:T], func=AF.Sigmoid, bias=tab[:, bcol:bcol + 1], scale=1.0), reads=[bp, bC], writes=[bgT[cc]])

        def c_rest(ti, after_T=None):
            t0, s, s0, Ls = tinfo(ti)
            l = lc
            xi = ti % 2
            xT, bxT = xTs[xi], bxTs[xi]
            hT, bhT = hTs[0], bhTs[0]
            for sub in range(NSUB):
                otok, botok = otoks[sub], botoks[sub]
                if USE_PS_BITCAST:
                    pstf, bpT = R["acc"].next()
                    pT = pstf[:, 0:384].bitcast(BF16)
                else:
                    pT, bpT = self.ps16[:, 0:768], self.psb[7]
                for kc in range(6):
                    op("pe", lambda e, kc=kc, otok=otok, pT=pT: e.transpose(pT[:, kc * 128:(kc + 1) * 128], otok[:, kc * 128:(kc + 1) * 128], self.identb[:]), reads=[botok, bC], writes=[bpT])
                op("act", lambda e, sub=sub, pT=pT: e.copy(out=oT[:, :, sub * 128:(sub + 1) * 128], in_=pT.rearrange("p (k t) -> p k t", k=6)), reads=[bpT], writes=[boT])
            wa, bwa, _ = wget("w_up_a", l, 0)
            wb, bwb, _ = wget("w_up_b", l, 0)
            for cc in range(8):
                pa, bpa = R["acc"].next()
                for kc in range(2):
                    op("pe", lambda e, kc=kc, cc=cc, pa=pa: e.matmul(pa[:, 0:T], lhsT=wa[:, kc, cc * 128:(cc + 1) * 128], rhs=oT[:, kc, :], start=(kc == 0), stop=(kc == 1)), reads=[bwa, boT], writes=[bpa])
                pb_, bpb = R["acc"].next()
                for kc in range(4):
                    op("pe", lambda e, kc=kc, cc=cc, pb_=pb_: e.matmul(pb_[:, 0:T], lhsT=wb[:, kc, cc * 128:(cc + 1) * 128], rhs=oT[:, 2 + kc, :], start=(kc == 0), stop=(kc == 3)), reads=[bwb, boT], writes=[bpb])
                t1, bt1 = R["t12"].next()
                t2, bt2 = R["t12"].next()
                op("dve", lambda e, cc=cc, pa=pa, t1=t1: e.tensor_tensor(out=t1[:], in0=pa[:, 0:T], in1=gT[:, cc, :], op=ALU.mult), reads=[bpa, bgT[cc]], writes=[bt1])
                op("dve", lambda e, cc=cc, pb_=pb_, t2=t2: e.tensor_tensor(out=t2[:], in0=pb_[:, 0:T], in1=gT[:, 8 + cc, :], op=ALU.mult), reads=[bpb, bgT[8 + cc]], writes=[bt2])
                op("pool", lambda e, cc=cc, t1=t1, t2=t2: e.tensor_tensor(out=mixT[:, cc, :], in0=t1[:], in1=t2[:], op=ALU.add), reads=[bt1, bt2], writes=[bmix[cc]])
            for b in range(2):
                wt, bw, _ = wget("w_o", l, b)
                for c4 in range(4):
                    cc = b * 4 + c4
                    pst, bp = R["acc"].next()
                    for kc in range(8):
                        op("pe", lambda e, wt=wt, kc=kc, c4=c4, pst=pst: e.matmul(pst[:, 0:T], lhsT=wt[:, kc, c4 * 128:(c4 + 1) * 128], rhs=mixT[:, kc, :], start=(kc == 0), stop=(kc == 7)), reads=[bw, bmix[kc]], writes=[bp])
                    op("dve", lambda e, cc=cc, pst=pst: e.scalar_tensor_tensor(out=xT[:, cc, :], in0=pst[:, 0:T], scalar=mcol(l, 2, cc, s), in1=xT[:, cc, :], op0=ALU.mult, op1=ALU.add), reads=[bp, bxT[cc], bmod], writes=[bxT[cc]])
                    norm_sq(xi, cc)
            norm_fin(l, 1, s, xi, 0)
            if after_T is not None:
                after_T()
            for b in range(8):
                wt, bw, _ = wget("w_ff1", l, b)
                for c4 in range(4):
                    cc = b * 4 + c4
                    pst, bp = R["acc"].next()
                    for kc in range(8):
                        op("pe", lambda e, wt=wt, kc=kc, c4=c4, pst=pst: e.matmul(pst[:, 0:T], lhsT=wt[:, kc, c4 * 128:(c4 + 1) * 128], rhs=hT[:, kc, :], start=(kc == 0), stop=(kc == 7)), reads=[bw, bhT[kc]], writes=[bp])
                    r_, br_ = R["rl"].next()
                    op("act", lambda e, pst=pst, r_=r_: e.activation(out=r_[:], in_=pst[:, 0:T], func=AF.Relu), reads=[bp], writes=[br_])
                    op("pool", lambda e, cc=cc, r_=r_: e.tensor_tensor(out=big[:, cc, :], in0=r_[:], in1=r_[:], op=ALU.mult), reads=[br_], writes=[bbig[cc]])
            if ti + 1 < ntiles:
                t0n, sn, _, _ = tinfo(ti + 1)
                norm(lc, 0, sn, (ti + 1) % 2, 0)
            for cc in range(8):
                wt, bw, _ = wget("w_ff2", l, cc)
                pst, bp = R["acc"].next()
                for kc in range(32):
                    op("pe", lambda e, wt=wt, kc=kc, pst=pst: e.matmul(pst[:, 0:T], lhsT=wt[:, kc, :], rhs=big[:, kc, :], start=(kc == 0), stop=(kc == 31)), reads=[bw, bbig[kc]], writes=[bp])
                op("dve", lambda e, cc=cc, pst=pst: e.scalar_tensor_tensor(out=xT[:, cc, :], in0=pst[:, 0:T], scalar=mcol(l, 5, cc, s), in1=xT[:, cc, :], op0=ALU.mult, op1=ALU.add), reads=[bp, bxT[cc], bmod], writes=[bxT[cc]])
                if doA:
                    norm_sq(xi, cc)
            if not doA:
                for sub in range(NSUB):
                    for hf in range(2):
                        pst, bp = R["acc"].next()
                        for k4 in range(4):
                            kc = hf * 4 + k4
                            op("pe", lambda e, kc=kc, k4=k4, sub=sub, pst=pst: e.transpose(pst[:, k4 * 128:(k4 + 1) * 128], xT[:, kc, sub * 128:(sub + 1) * 128], self.ident[:]), reads=[bxT[kc], bC], writes=[bp])
                        wr = [bbig[16 + sub * 4 + hf * 2], bbig[16 + sub * 4 + hf * 2 + 1]]
                        if hf:
                            op("dve", lambda e, hf=hf, sub=sub, pst=pst: e.tensor_copy(out=iost[:, sub, hf * 512:(hf + 1) * 512], in_=pst[:, 0:512]), reads=[bp], writes=wr)
                        else:
                            op("act", lambda e, hf=hf, sub=sub, pst=pst: e.copy(out=iost[:, sub, hf * 512:(hf + 1) * 512], in_=pst[:, 0:512]), reads=[bp], writes=wr)
                    dma("pool", lambda e, sub=sub: e.dma_start(out=self.y[t0 + sub * 128:t0 + (sub + 1) * 128, :], in_=iost[:, sub, :]), reads=bbig[16 + sub * 4:16 + sub * 4 + 4], writes=[])

        def store_x(ti):
            t0, s, s0, Ls = tinfo(ti)
            xi = ti % 2
            dma("pool", lambda e: e.dma_start(out=self.xs[:, :, t0:t0 + T].rearrange("k p t -> p k t"), in_=xTs[xi][:]), reads=bxTs[xi], writes=[self.bxs[ti]])

        def a_qk(ti, hi, j0, j1, held, hook=None):
            t0, s, s0, Ls = tinfo(ti)
            l = la
            hT, bhT = hTs[hi], bhTs[hi]

            def finish(ctx):
                j, pst, bp, s2, bs2 = ctx
                jj = j % 10
                d = DIL[jj // 2] if jj < 6 else 1
                pm, bpm = R["stt"].next()
                op("pe", lambda e, pm=pm, s2=s2: e.matmul(pm[:, 0:T], lhsT=self.bones[:], rhs=s2[:], start=True, stop=True), reads=[bs2, bC], writes=[bpm])
                r2, br2 = R["rs2"].next()
                op("act", lambda e, pm=pm, r2=r2: e.activation(out=r2[:], in_=pm[:, 0:T], func=AF.Ln, bias=EPS, scale=1.0), reads=[bpm], writes=[br2])
                op("act", lambda e, r2=r2: e.activation(out=r2[:], in_=r2[:], func=AF.Exp, scale=-0.5), reads=[br2], writes=[br2])
                qo, bqo = R["qst"].next()
                if j < 10:
                    gc = self.qg8[:, l * 10 + jj:l * 10 + jj + 1]
                else:
                    kcol = self.col["kg"] + l * 10 + jj
                    gc = tab[:, kcol:kcol + 1]
                op("dve", lambda e, pst=pst, r2=r2, qo=qo, gc=gc, d=d: e.scalar_tensor_tensor(out=qo[:].rearrange("p (r m) -> p r m", r=d), in0=pst[:, 0:T].rearrange("p (m r) -> p r m", r=d), scalar=gc, in1=r2[:].rearrange("p (m r) -> p r m", r=d), op0=ALU.mult, op1=ALU.mult), reads=[bp, br2, bC], writes=[bqo])
                dst = (self.qs if j < 10 else self.ks)
                n = Ls // d
                col0 = s0 + (t0 - s0) // d
                dap = bass.AP(dst.tensor, jj * 128 * NT + col0, [[NT, 128], [n, d], [1, T // d]])
                dma("pool", lambda e, dap=dap, qo=qo, d=d: e.dma_start(out=dap, in_=qo[:].rearrange("p (r m) -> p r m", r=d)), reads=[bqo], writes=[self.bqk])

            for j in range(j0, j1):
                if j % 4 == 0:
                    held[0] = wget("w_in", l, j // 4)
                wt, bw, _ = held[0]
                c4 = j % 4
                jj = j % 10
                d = DIL[jj // 2] if jj < 6 else 1
                pst, bp = R["acc"].next()
                for kc in range(8):
                    op("pe", lambda e, wt=wt, kc=kc, c4=c4, pst=pst: e.matmul(pst[:, 0:T], lhsT=wt[:, kc, c4 * 128:(c4 + 1) * 128], rhs=hT[:, kc, :], start=(kc == 0), stop=(kc == 7)), reads=[bw, bhT[kc]], writes=[bp])
                s2, bs2 = R["sq2"].next()
                op("act", lambda e, pst=pst, s2=s2: e.activation(out=s2[:], in_=pst[:, 0:T], func=AF.Square), reads=[bp], writes=[bs2])
                if held[1] is not None:
                    finish(held[1])
                held[1] = (j, pst, bp, s2, bs2)
                while held[2] < -(-(j + 1) * 3 * NSUB // 20):
                    v_group(ti, hi, held)
                if hook is not None:
                    hook(j)
            if j1 == 20:
                finish(held[1])
                held[1] = None

        def v_group(ti, hi, held):
            t0, s, s0, Ls = tinfo(ti)
            l = la
            hT, bhT = hTs[hi], bhTs[hi]
            gi = held[2]
            held[2] += 1
            vi, sub = gi // NSUB, gi % NSUB
            if sub == 0:
                held[3] = wget("w_in", l, 5 + vi)
            wt, bw, ncol = held[3]
            pst, bp = R["acc"].next()
            for kc in range(8):
                op("pe", lambda e, wt=wt, kc=kc, sub=sub, pst=pst, ncol=ncol: e.matmul(pst[:, 0:ncol], lhsT=hT[:, kc, sub * 128:(sub + 1) * 128], rhs=wt[:, kc, :], start=(kc == 0), stop=(kc == 7)), reads=[bw, bhT[kc]], writes=[bp])
            vs_, bvs_ = R["vst"].next()
            op("dve", lambda e, pst=pst, ncol=ncol, vs_=vs_: e.tensor_copy(out=vs_[:, 0:ncol], in_=pst[:, 0:ncol]), reads=[bp], writes=[bvs_])
            dma("pool", lambda e, sub=sub, vi=vi, ncol=ncol, vs_=vs_: e.dma_start(out=self.vs[t0 + sub * 128:t0 + (sub + 1) * 128, vi * 512:vi * 512 + ncol], in_=vs_[:, 0:ncol]), reads=[bvs_], writes=[self.bv])

        def emit_all():
            mkrings()
            if lp == 0:
                load(0)
                _, s_, _, _ = tinfo(0)
                norm(la, 0, s_, 0, 0)
                for ti in range(ntiles):
                    hook = None
                    if ti + 1 < ntiles:
                        load(ti + 1, "dma")
                        hook = (lambda j, ti=ti: load(ti + 1, j) if j < 8 else None)
                    store_x(ti)
                    held = [None, None, 0, None]
                    a_qk(ti, ti % 2, 0, 10, held, hook)
                    if ti + 1 < ntiles:
                        _, sn, _, _ = tinfo(ti + 1)
                        norm(la, 0, sn, (ti + 1) % 2, (ti + 1) % 2)
                    a_qk(ti, ti % 2, 10, 20, held)
            else:
                load(0)
                omerge(0)
                _, s_, _, _ = tinfo(0)
                norm(lc, 0, s_, 0, 0)
                gate(0)
                for ti in range(ntiles):
                    if ti + 1 < ntiles:
                        load(ti + 1)
                    c_rest(ti, (lambda ti=ti: omerge(ti + 1)) if ti + 1 < ntiles else None)
                    if doA:
                        store_x(ti)
                        _, s_, _, _ = tinfo(ti)
                        norm_fin(la, 0, s_, ti % 2, 1)
                    if ti + 1 < ntiles:
                        gate(ti + 1)
                    if doA:
                        held = [None, None, 0, None]
                        a_qk(ti, 1, 0, 20, held)

        emit_all()
        wstate["dry"] = False
        emit_all()
        P.barrier()

    def attention(self, l):
        nc, P, ar = self.nc, self.P, self.ar
        NS, NT = self.NS, self.NT
        ar.reset(self.base_mark)
        bC = self.bC
        Tdil = ar.alloc([128, 12, 256], BF16)
        bTd = Buf()
        for half, off in ((0, 128), (1, 0)):
            src = bass.AP(self.rfull.tensor, off, [[1, 128], [384, 12], [1, 128]])
            P.dma("sp", lambda e, src=src, half=half: e.dma_start(out=Tdil[:, :, half * 128:(half + 1) * 128], in_=src), reads=[self.brf], writes=[bTd])
        Tnb = ar.alloc([128, 2, 8, 448], BF16)
        bTn = Buf()
        for par in range(2):
            for slot in range(7):
                e_ = (-6 if par == 0 else -7) + 2 * slot
                for ci in range(2):
                    src = bass.AP(self.pnb.tensor, l * 120 * 128 + (e_ + 8 - ci) * 128, [[1, 64], [15 * 128, 8], [1, 64]])
                    P.dma("sp", lambda e, src=src, par=par, slot=slot, ci=ci: e.dma_start(out=Tnb[ci * 64:(ci + 1) * 64, par, :, slot * 64:(slot + 1) * 64], in_=src), reads=[self.bpnb], writes=[bTn])
        P.op("dve", lambda e: e.tensor_tensor(out=Tnb[:].rearrange("p a h (s q) -> p (a h s) q", q=64), in0=Tnb[:].rearrange("p a h (s q) -> p (a h s) q", q=64), in1=self.cmask[:].unsqueeze(1).to_broadcast([128, 112, 64]), op=ALU.add), reads=[bTn, bC], writes=[bTn])
        EBd = ar.alloc([128, 12, 256], BF16)
        EBn = ar.alloc([128, 2, 8, 448], BF16)
        bEd, bEn = Buf(), Buf()
        for hh in range(12):
            pb = hh % 3
            P.op("pe", lambda e, hh=hh, pb=pb: e.matmul(self.ps[pb][:, 0:256], lhsT=self.J[:], rhs=Tdil[:, hh, :], start=True, stop=True), reads=[bC, bTd], writes=[self.psb[pb]])
            P.op("act", lambda e, hh=hh, pb=pb: e.activation(out=EBd[:, hh, :], in_=self.ps[pb][:, 0:256], func=AF.Exp), reads=[self.psb[pb]], writes=[bEd])
        for par in range(2):
            for hh in range(8):
                pb = hh % 3
                P.op("pe", lambda e, hh=hh, pb=pb, par=par: e.matmul(self.ps[pb][:, 0:448], lhsT=self.J[:], rhs=Tnb[:, par, hh, :], start=True, stop=True), reads=[bC, bTn], writes=[self.psb[pb]])
                P.op("act", lambda e, hh=hh, pb=pb, par=par: e.activation(out=EBn[:, par, hh, :], in_=self.ps[pb][:, 0:448], func=AF.Exp), reads=[self.psb[pb]], writes=[bEn])
        NCH = SEG // 128
        QT = [ar.alloc([128, 2, SEG], BF16) for _ in range(2)]
        KTW = 4096
        KT = [ar.alloc([128, 2, KTW], BF16) for _ in range(2)]
        VV = [ar.alloc([128, 40, 4, 65], BF16) for _ in range(2)]
        OST = [ar.alloc([128, NCH, 260], F32) for _ in range(2)]
        PT = [ar.alloc([128, 256], BF16) for _ in range(8)]
        bQT, bOST = [Buf(), Buf()], [Buf(), Buf()]
        bKT = [[Buf(), Buf()] for _ in range(2)]
        bVV = [[Buf() for _ in range(40)] for _ in range(2)]
        bPT = [Buf() for _ in range(8)]
        for i in range(2):
            P.op("pool", lambda e, i=i: e.memset(VV[i][:], 1.0), writes=bVV[i])
            P.op("pool", lambda e, i=i: e.memset(KT[i][:], 0.0), writes=bKT[i])
        stR = Ring([(self.ps[i], self.psb[i]) for i in (0, 1, 2, 5, 6)])
        poR = Ring([(self.ps[i], self.psb[i]) for i in (3, 4)])
        ptR = Ring(list(zip(PT, bPT)))
        item = [0]
        pending = []

        def run_item(kind, s, g, rs, P0, S):
            ib = item[0] % 2
            item[0] += 1
            qt, kt, vv, ost = QT[ib], KT[ib], VV[ib], OST[ib]
            bq, bk, bv_, bo = bQT[ib], bKT[ib], bVV[ib], bOST[ib]
            s0, Ls = self.s0[s], self.seqs[s]
            nsub = len(rs)
            nC = S // 128
            if kind == "d":
                d = DIL[g]
                n = Ls // d
                jq = 2 * g
                vcol = 256 * g
                ocol = 260 * g
                cb = s0 + rs[0] * n
                klo, khi = max(P0 - 64, 0), min(P0 + S + 64, n)
                koff = klo - (P0 - 64)
                KW_ = S + 128
                VT_ = nC + 1
                assert nsub == 1 or (P0 == 0 and S == n)
                assert nsub * S <= SEG and nsub * KW_ <= KTW and nsub * VT_ <= 40 and nsub * nC <= SEG // 128
            else:
                d = 1
                n = Ls
                jq = 6 + 2 * g
                vcol = 768 + 256 * g
                ocol = 780 + 260 * g
                cb = s0
                klo, khi = max(P0 - 256, 0), min(P0 + S + 256, n)
                koff = klo - (P0 - 256)
            P.dma("sp", lambda e: e.dma_start(out=qt[:, :, 0:nsub * S], in_=self.qs[jq:jq + 2, :, cb + P0:cb + P0 + nsub * S].rearrange("j p t -> p j t")), reads=[self.bqk], writes=[bq])
            if kind == "d":
                if nsub > 1 or koff > 0 or khi < P0 + S + 64:
                    P.op("pool", lambda e: e.memset(kt[:, :, 0:nsub * KW_], 0.0), writes=bk)
                for pr in range(2):
                    dstv = kt[:, pr, 0:nsub * KW_].rearrange("p (i c) -> p i c", c=KW_)[:, :, koff:koff + khi - klo]
                    srcv = bass.AP(self.ks.tensor, (jq + pr) * 128 * NT + cb + klo, [[NT, 128], [n, nsub], [1, khi - klo]])
                    P.dma("sp", lambda e, dstv=dstv, srcv=srcv: e.dma_start(out=dstv, in_=srcv), reads=[self.bqk], writes=[bk[pr]])
                for i, r in enumerate(rs):
                    for t_ in range(VT_):
                        p_lo = P0 + 128 * t_ - 64
                        a, b_ = 0, 128
                        if p_lo < 0:
                            a = 64
                        if p_lo + 128 > n:
                            b_ = 64
                        if a > 0 or b_ < 128:
                            za, zb = (0, 64) if a > 0 else (64, 128)
                            P.op("pool", lambda e, i=i, t_=t_, za=za, zb=zb: e.memset(vv[za:zb, i * VT_ + t_, :, :], 0.0), writes=[bv_[i * VT_ + t_]])
                        src = bass.AP(self.vs.tensor, (s0 + r + d * (p_lo + a)) * 1280 + vcol, [[d * 1280, b_ - a], [64, 4], [1, 64]])
                        P.dma("sp", lambda e, i=i, t_=t_, a=a, b_=b_, src=src: e.dma_start(out=vv[a:b_, i * VT_ + t_, :, 0:64], in_=src), reads=[self.bv], writes=[bv_[i * VT_ + t_]])
            else:
                P.dma("sp", lambda e: e.dma_start(out=kt[:, :, koff:koff + khi - klo], in_=self.ks[jq:jq + 2, :, cb + klo:cb + khi].rearrange("j p t -> p j t")), reads=[self.bqk], writes=bk)
                R0 = P0 // 64
                rows = Ls // 64
                rbase = max(R0 - 4, 0)
                rhi = min(R0 + S // 64 + 4, rows) - 2
                for rho in range(rbase, rhi + 1):
                    src = bass.AP(self.vs.tensor, (s0 + rho * 64) * 1280 + vcol, [[1280, 128], [64, 4], [1, 64]])
                    P.dma("sp", lambda e, rho=rho, src=src: e.dma_start(out=vv[:, rho - rbase, :, 0:64], in_=src), reads=[self.bv], writes=[bv_[rho - rbase]])
            units = []
            if kind == "d":
                for i in range(nsub):
                    for c in range(nC):
                        for h in range(4):
                            units.append((i, c, h, 0))
            else:
                for c in range(nC):
                    for hpair in range(2):
                        for half in range(2):
                            units.append((0, c, 2 * hpair, half))
                            units.append((0, c, 2 * hpair + 1, half))
            pobox = {}

            def emitS2(ua, ub):
                res = []
                ctx = []
                for u in (ua, ub):
                    i, c, h, half = u
                    pair, hp = h // 2, (h % 2) * 64
                    stp, bst = stR.next()
                    pt_, bpt = ptR.next()
                    ctx.append((i, c, h, half, pair, hp, stp, bst, pt_, bpt))
                if kind == "d":
                    for ab in range(2):
                        for (i, c, h, half, pair, hp, stp, bst, pt_, bpt) in ctx:
                            qc = i * S + 128 * c
                            kc_ = i * KW_ + 128 * c + 128 * ab
                            P.op("pe", lambda e, stp=stp, pair=pair, hp=hp, qc=qc, kc_=kc_, ab=ab: e.matmul(stp[:, 128 * ab:128 * ab + 128], lhsT=kt[hp:hp + 64, pair, kc_:kc_ + 128], rhs=qt[hp:hp + 64, pair, qc:qc + 128], start=(ab == 0), stop=(ab == 1)), reads=[bk[pair], bq], writes=[bst])
                    infos = [None, None]
                    ebs = [(EBd[:, 4 * g + cx[2], :], bEd) for cx in ctx]
                else:
                    infos = []
                    ebs = []
                    geo = []
                    for (i, c, h, half, pair, hp, stp, bst, pt_, bpt) in ctx:
                        i_ = R0 + 2 * c + half
                        rstart = min(max(i_ - 4, 0), rows - 8)
                        dr0 = rstart - i_
                        par = dr0 % 2
                        emin = -6 if par == 0 else -7
                        toff = 64 * ((dr0 - emin) // 2)
                        ebs.append((EBn[:, par, 4 * g + h, toff:toff + 256], bEn))
                        infos.append(rstart)
                        geo.append((rstart, (2 * c + half) * 64))
                    for p4 in range(4):
                        for cx, (rstart, qcol) in zip(ctx, geo):
                            (i, c, h, half, pair, hp, stp, bst, pt_, bpt) = cx
                            rho = rstart + 2 * p4
                            kcol = 64 * rho - (P0 - 256)
                            P.op("pe", lambda e, stp=stp, pair=pair, hp=hp, p4=p4, kcol=kcol, qcol=qcol: e.matmul(stp[:, 64 * p4:64 * p4 + 64], lhsT=kt[hp:hp + 64, pair, kcol:kcol + 128], rhs=qt[hp:hp + 64, pair, qcol:qcol + 64], start=(p4 == 0), stop=(p4 == 3)), reads=[bk[pair], bq], writes=[bst])
                for cx, info, (eb, beb) in zip(ctx, infos, ebs):
                    (i, c, h, half, pair, hp, stp, bst, pt_, bpt) = cx
                    P.op("act", lambda e, stp=stp, pt_=pt_: e.activation(out=pt_[:], in_=stp[:, 0:256], func=AF.Exp), reads=[bst], writes=[bpt])
                    P.op("dve", lambda e, pt_=pt_, eb=eb: e.tensor_tensor(out=pt_[:], in0=pt_[:], in1=eb, op=ALU.mult), reads=[bpt, beb], writes=[bpt])
                    res.append((pt_, bpt, info))
                return res

            def emitPV(u, sres):
                i, c, h, half = u
                pt_, bpt, info = sres
                oc = i * nC + c
                if h == 0 and half == 0:
                    pobox[oc] = poR.next()
                po, bpo = pobox[oc]
                if kind == "d":
                    vt = i * VT_ + c
                    P.op("pe", lambda e: e.matmul(po[:, 65 * h:65 * h + 65], lhsT=pt_[:, 0:128], rhs=vv[:, vt, h, :], start=True, stop=False), reads=[bpt, bv_[vt]], writes=[bpo])
                    P.op("pe", lambda e: e.matmul(po[:, 65 * h:65 * h + 65], lhsT=pt_[:, 128:256], rhs=vv[:, vt + 1, h, :], start=False, stop=True), reads=[bpt, bv_[vt + 1]], writes=[bpo])
                    last = (h == 3)
                else:
                    rstart = info
                    for p4 in range(4):
                        rho = rstart + 2 * p4
                        P.op("pe", lambda e, p4=p4, rho=rho: e.matmul(po[64 * half:64 * half + 64, 65 * h:65 * h + 65], lhsT=pt_[:, 64 * p4:64 * p4 + 64], rhs=vv[:, rho - rbase, h, :], start=(p4 == 0), stop=(p4 == 3)), reads=[bpt, bv_[rho - rbase]], writes=[bpo])
                    last = (h == 3 and half == 1)
                if last:
                    P.op("dve", lambda e: e.tensor_copy(out=ost[:, oc, :], in_=po[:, 0:260]), reads=[bpo], writes=[bo])

            def finish():
                for i, r in enumerate(rs):
                    tok0 = (s0 + r + d * P0) if kind == "d" else (s0 + P0)
                    dap = bass.AP(self.osr.tensor, tok0 * 1300 + ocol, [[d * 1300, 128], [128 * d * 1300, nC], [1, 260]])
                    P.dma("pool", lambda e, dap=dap, i=i: e.dma_start(out=dap, in_=ost[:, i * nC:(i + 1) * nC, :]), reads=[bo], writes=[self.bos])
                    if kind == "d":
                        for t_ in range(VT_):
                            p_lo = P0 + 128 * t_ - 64
                            if p_lo < 0:
                                P.op("pool", lambda e, i=i, t_=t_: e.memset(vv[0:64, i * VT_ + t_, :, 64:65], 1.0), writes=[bv_[i * VT_ + t_]])
                            if p_lo + 128 > n:
                                P.op("pool", lambda e, i=i, t_=t_: e.memset(vv[64:128, i * VT_ + t_, :, 64:65], 1.0), writes=[bv_[i * VT_ + t_]])

            for ui in range(0, len(units), 2):
                ua, ub = units[ui], units[ui + 1]
                ra, rb = emitS2(ua, ub)
                while len(pending) > 2:
                    pending.pop(0)()
                pending.append(lambda u=ua, sres=ra: emitPV(u, sres))
                pending.append(lambda u=ub, sres=rb: emitPV(u, sres))
            pending.append(finish)

        for s in range(NS):
            Ls = self.seqs[s]
            for g in range(3):
                d = DIL[g]
                n = Ls // d
                if n >= SEG:
                    for r in range(d):
                        for P0 in range(0, n, SEG):
                            run_item("d", s, g, [r], P0, SEG)
                else:
                    per = min(d, SEG // n, KTW // (n + 128), 40 // (n // 128 + 1))
                    for r0 in range(0, d, per):
                        run_item("d", s, g, list(range(r0, min(d, r0 + per))), 0, n)
            for g in range(2):
                for P0 in range(0, Ls, SEG):
                    run_item("n", s, g, [0], P0, min(SEG, Ls))
        while pending:
            pending.pop(0)()
        P.barrier()


def _off(t):
    return t.manual_sbuf_range[0]


_CACHE = {}


def _get_nc(seqs):
    key = tuple(seqs)
    if key not in _CACHE:
        _CACHE[key] = Builder(seqs).build()
    return _CACHE[key]


def kernel(x_prompt, x_sample, c_prompt, c_sample, norm1_g, norm2_g, w_mod, b_mod, w_in, q_norm_g, k_norm_g,
           rel_bias, rpb, w_gate, b_gate, w_up_a, w_up_b, w_o, w_ff1, w_ff2):
    f = lambda a: np.ascontiguousarray(np.asarray(a, dtype=np.float32))
    x_prompt, x_sample, c_prompt, c_sample = f(x_prompt), f(x_sample), f(c_prompt), f(c_sample)
    ncores = 8
    Bp, Lp, _ = x_prompt.shape
    Bs, Ls, _ = x_sample.shape
    pp, sp_ = Bp // ncores, Bs // ncores
    seqs = [Lp] * pp + [Ls] * sp_
    nc = _get_nc(seqs)
    shared = {"norm1_g": f(norm1_g), "norm2_g": f(norm2_g), "w_mod": f(w_mod), "b_mod": f(b_mod), "w_in": f(w_in),
              "q_norm_g": f(q_norm_g), "k_norm_g": f(k_norm_g), "rel_bias": f(rel_bias), "rpb": f(rpb),
              "w_gate": f(w_gate), "b_gate": f(b_gate), "w_up_a": f(w_up_a), "w_up_b": f(w_up_b), "w_o": f(w_o),
              "w_ff1": f(w_ff1), "w_ff2": f(w_ff2)}
    for k, v in make_consts().items():
        shared["c_" + k] = v
    in_maps = []
    for c in range(ncores):
        xs = [x_prompt[c * pp + i] for i in range(pp)] + [x_sample[c * sp_ + i] for i in range(sp_)]
        cs = [c_prompt[c * pp + i] for i in range(pp)] + [c_sample[c * sp_ + i] for i in range(sp_)]
        m = dict(shared)
        m["x"] = np.ascontiguousarray(np.concatenate(xs, axis=0))
        m["c"] = np.ascontiguousarray(np.stack(cs, axis=0))
        in_maps.append(m)
    res = run_bass_kernel_spmd(nc, in_maps, core_ids=list(range(ncores)))
    yp = np.empty_like(x_prompt)
    ys = np.empty_like(x_sample)
    for c in range(ncores):
        y = res.results[c]["y"]
        o = 0
        for i in range(pp):
            yp[c * pp + i] = y[o:o + Lp]
            o += Lp
        for i in range(sp_):
            ys[c * sp_ + i] = y[o:o + Ls]
            o += Ls
    return (yp, ys)
```
